# Optimizing a Trainium2 kernel written in Bass

```python
import math
import jax, jax.numpy as jnp
from jax import lax
import numpy as np

D_MODEL = 1024
BATCH = 16
SEQ = 2048
DEPTH = 1

N_META = 16
D_MIX = D_MODEL
N_HEADS = 8
QK_NOPE = 64
QK_ROPE = 32
QK_HEAD = QK_NOPE + QK_ROPE
V_HEAD = 64
D_ATTN = N_HEADS * V_HEAD
Q_LORA = 384
KV_LORA = 256
D_RNN = D_MIX - D_ATTN
RNN_BLOCKS = 8
RNN_BW = D_RNN // RNN_BLOCKS
CONV_W = 4
CONV_PAD = (2, 1)
LRU_C = 8.0
ROPE_THETA = 10000.0
Q_BLOCK = 128
OFF_CQ = Q_LORA
OFF_CKV = OFF_CQ + KV_LORA
OFF_KR = OFF_CKV + QK_ROPE
OFF_XR = OFF_KR + D_RNN
IN_COLS = OFF_XR + D_RNN
D_FF = int(math.ceil(8 * D_MODEL / 3 / 256) * 256)
EPS = 1e-6

kernel_name = "hymba_mla_rglru_hybrid_encoder"


def rms_norm(x, g):
    xf = x.astype(jnp.float32)
    y = xf * lax.rsqrt(jnp.mean(xf * xf, axis=-1, keepdims=True) + EPS)
    return (y * g.astype(jnp.float32)).astype(x.dtype)


def rope(x, pos):
    half = x.shape[-1] // 2
    freqs = 1.0 / (ROPE_THETA ** (jnp.arange(half, dtype=jnp.float32) / half))
    ang = pos[:, None] * freqs[None, :]
    cos = jnp.cos(ang)[None, :, None, :]
    sin = jnp.sin(ang)[None, :, None, :]
    xf = x.astype(jnp.float32)
    x1, x2 = xf[..., :half], xf[..., half:]
    out = jnp.concatenate([x1 * cos - x2 * sin, x1 * sin + x2 * cos], axis=-1)
    return out.astype(x.dtype)


def attend_block(q_blk, k, v):
    s = jnp.einsum('bhqd,bhkd->bhqk', q_blk, k).astype(jnp.float32) * (QK_HEAD ** -0.5)
    p = jax.nn.softmax(s, axis=-1)
    return jnp.einsum('bhqk,bhkd->bhqd', p.astype(v.dtype), v)


def mla_group(c_q, c_kv, k_r, q_a_g, w_uq, kv_a_g, w_ukv, q_g, k_g, pos):
    B, T, _ = c_q.shape
    q = (rms_norm(c_q, q_a_g) @ w_uq).reshape(B, T, N_HEADS, QK_HEAD)
    kv = (rms_norm(c_kv, kv_a_g) @ w_ukv).reshape(B, T, N_HEADS, QK_NOPE + V_HEAD)
    k_nope, v = kv[..., :QK_NOPE], kv[..., QK_NOPE:]
    k = jnp.concatenate([k_nope, jnp.broadcast_to(k_r[:, :, None, :], (B, T, N_HEADS, QK_ROPE))], axis=-1)
    q = rms_norm(q, q_g)
    k = rms_norm(k, k_g)
    q = jnp.concatenate([q[..., :QK_NOPE], rope(q[..., QK_NOPE:], pos)], axis=-1)
    k = jnp.concatenate([k[..., :QK_NOPE], rope(k[..., QK_NOPE:], pos)], axis=-1)
    q = q.transpose(0, 2, 1, 3)
    k = k.transpose(0, 2, 1, 3)
    v = v.transpose(0, 2, 1, 3)
    o_meta = attend_block(q[:, :, :N_META], k, v)
    q_real = q[:, :, N_META:]
    n_blk = q_real.shape[2] // Q_BLOCK
    q_blocks = q_real.reshape(B, N_HEADS, n_blk, Q_BLOCK, QK_HEAD).transpose(2, 0, 1, 3, 4)
    o_blocks = lax.map(lambda qb: attend_block(qb, k, v), q_blocks)
    o_real = o_blocks.transpose(1, 2, 0, 3, 4).reshape(B, N_HEADS, n_blk * Q_BLOCK, V_HEAD)
    o = jnp.concatenate([o_meta, o_real], axis=2)
    return o.transpose(0, 2, 1, 3).reshape(B, T, D_ATTN)


def _linear_combine(c1, c2):
    a1, b1 = c1
    a2, b2 = c2
    return a1 * a2, a2 * b1 + b2


def rg_lru(xc, wa, ba, wi, bi, lam, reverse):
    B, T, _ = xc.shape
    xg = xc.reshape(B, T, RNN_BLOCKS, RNN_BW)
    r = jax.nn.sigmoid((jnp.einsum('btgi,gij->btgj', xg, wa).reshape(B, T, D_RNN) + ba).astype(jnp.float32))
    i = jax.nn.sigmoid((jnp.einsum('btgi,gij->btgj', xg, wi).reshape(B, T, D_RNN) + bi).astype(jnp.float32))
    log_a = -LRU_C * r * jax.nn.softplus(-lam.astype(jnp.float32))
    a = jnp.exp(log_a)
    b = jnp.sqrt(jnp.maximum(-jnp.expm1(2.0 * log_a), 0.0)) * (i * xc.astype(jnp.float32))
    _, h = lax.associative_scan(_linear_combine, (a, b), axis=1, reverse=reverse)
    return h.astype(xc.dtype)


def rglru_group(x_r, x_gate, conv_w, conv_b, wa, ba, wi, bi, lam):
    xc = lax.conv_general_dilated(
        x_r, conv_w[:, None, :], window_strides=(1,), padding=[CONV_PAD],
        dimension_numbers=('NWC', 'WIO', 'NWC'), feature_group_count=D_RNN) + conv_b
    y = rg_lru(xc, wa[0], ba[0], wi[0], bi[0], lam[0], reverse=False) \
        + rg_lru(xc, wa[1], ba[1], wi[1], bi[1], lam[1], reverse=True)
    return y * jax.nn.gelu(x_gate)


def setup_inputs(seed: int = 0) -> dict:
    key = jax.random.key(seed)
    ks = iter(jax.random.split(key, 40))
    L = DEPTH
    f32 = jnp.float32

    def nrm(shape, fan_in):
        return jax.random.normal(next(ks), shape, f32) * (fan_in ** -0.5)

    def gain(shape):
        return 1.0 + 0.02 * jax.random.normal(next(ks), shape, f32)

    def bias(shape):
        return 0.01 * jax.random.normal(next(ks), shape, f32)

    x = jax.random.normal(next(ks), (BATCH, SEQ, D_MODEL), f32)
    meta_tokens = jax.random.normal(next(ks), (N_META, D_MODEL), f32)
    u = jax.random.uniform(next(ks), (L, 2, D_RNN), f32, 0.9, 0.999)
    s = u ** (1.0 / LRU_C)
    lru_lambda = jnp.log(s) - jnp.log1p(-s)
    return {
        "x": x,
        "meta_tokens": meta_tokens,
        "ln1_g": gain((L, D_MODEL)),
        "w_in": nrm((L, D_MODEL, IN_COLS), D_MODEL),
        "q_a_norm_g": gain((L, Q_LORA)),
        "w_uq": nrm((L, Q_LORA, N_HEADS * QK_HEAD), Q_LORA),
        "kv_a_norm_g": gain((L, KV_LORA)),
        "w_ukv": nrm((L, KV_LORA, N_HEADS * (QK_NOPE + V_HEAD)), KV_LORA),
        "q_norm_g": gain((L, QK_HEAD)),
        "k_norm_g": gain((L, QK_HEAD)),
        "conv_w": nrm((L, CONV_W, D_RNN), CONV_W),
        "conv_b": bias((L, D_RNN)),
        "lru_wa": nrm((L, 2, RNN_BLOCKS, RNN_BW, RNN_BW), RNN_BW),
        "lru_ba": bias((L, 2, D_RNN)),
        "lru_wi": nrm((L, 2, RNN_BLOCKS, RNN_BW, RNN_BW), RNN_BW),
        "lru_bi": bias((L, 2, D_RNN)),
        "lru_lambda": lru_lambda,
        "attn_out_g": gain((L, D_ATTN)),
        "rnn_out_g": gain((L, D_RNN)),
        "w_out": nrm((L, D_MIX, D_MODEL), D_MIX),
        "ln2_g": gain((L, D_MODEL)),
        "w_gate": nrm((L, D_MODEL, D_FF), D_MODEL),
        "w_up": nrm((L, D_MODEL, D_FF), D_MODEL),
        "w_down": nrm((L, D_FF, D_MODEL), D_FF),
    }


def reference(x, meta_tokens, ln1_g, w_in, q_a_norm_g, w_uq, kv_a_norm_g, w_ukv,
              q_norm_g, k_norm_g, conv_w, conv_b, lru_wa, lru_ba, lru_wi, lru_bi,
              lru_lambda, attn_out_g, rnn_out_g, w_out, ln2_g, w_gate, w_up, w_down):
    B = x.shape[0]
    meta = jnp.broadcast_to(meta_tokens[None].astype(x.dtype), (B, N_META, x.shape[-1]))
    h = jnp.concatenate([meta, x], axis=1)
    T = h.shape[1]
    pos = jnp.arange(T, dtype=jnp.float32)
    for l in range(DEPTH):
        hn = rms_norm(h, ln1_g[l])
        p = hn @ w_in[l]
        c_q = p[..., :OFF_CQ]
        c_kv = p[..., OFF_CQ:OFF_CKV]
        k_r = p[..., OFF_CKV:OFF_KR]
        x_r = p[..., OFF_KR:OFF_XR]
        x_gate = p[..., OFF_XR:]
        o_attn = mla_group(c_q, c_kv, k_r, q_a_norm_g[l], w_uq[l], kv_a_norm_g[l],
                           w_ukv[l], q_norm_g[l], k_norm_g[l], pos)
        o_rnn = rglru_group(x_r, x_gate, conv_w[l], conv_b[l], lru_wa[l], lru_ba[l],
                            lru_wi[l], lru_bi[l], lru_lambda[l])
        mix = jnp.concatenate([rms_norm(o_attn, attn_out_g[l]), rms_norm(o_rnn, rnn_out_g[l])], axis=-1)
        h = h + mix @ w_out[l]
        hn = rms_norm(h, ln2_g[l])
        h = h + (jax.nn.silu(hn @ w_gate[l]) * (hn @ w_up[l])) @ w_down[l]
    return h[:, N_META:]
```

```python
import os
from contextlib import ExitStack
import numpy as np
import concourse.bass as bass
import concourse.mybir as mybir
from concourse.bass_utils import run_bass_kernel_spmd

F32 = mybir.dt.float32
BF16 = mybir.dt.bfloat16
AF = mybir.ActivationFunctionType
ALU = mybir.AluOpType
AX = mybir.AxisListType

NCORES = 8
NSEQ = 2
D = 1024
SEQ = 2048
NM = 16
T = SEQ + NM
NH = 8
DFF = 2816
NJF = DFF // 128
EPS = 1e-6
RING = 7
KDEBUG = os.environ.get("KDEBUG", "") != ""
KSTOP = os.environ.get("KSTOP", "")
SAME_ENG_SYNC = os.environ.get("KNOSAME", "") == ""


class _Stop(Exception):
    pass


def phase_end(name):
    if KSTOP == name:
        raise _Stop()

V_G1, V_GQA, V_GKVA, V_QG, V_KG, V_CW, V_CB, V_BA, V_BI, V_LAM, V_GA, V_GR, V_G2, V_EPS, V_ONE = (
    0, 8, 11, 13, 14, 15, 31, 35, 43, 51, 59, 63, 67, 75, 76)
NV = 80


class Buf:
    def __init__(self, name, space, lo, hi):
        self.name, self.space, self.lo, self.hi = name, space, lo, hi
        self.w = None
        self.r = []
        self.ov = [self]


class Op:
    __slots__ = ("eng", "calls", "dma", "ndma", "deps", "need", "cnt", "sem", "id")


class _Rec:
    def __init__(self):
        self.calls = []

    def __getattr__(self, name):
        def m(*a, **k):
            self.calls.append((name, a, k))
            return self
        return m


class Sched:
    ENGS = ("pe", "act", "dve", "pool", "sp")

    def __init__(self, nc):
        self.nc = nc
        self.bufs = []
        self.ops = []
        self.dma_keys = {}
        self.store_ops = []

    def buf(self, name, space, lo, hi):
        b = Buf(name, space, lo, hi)
        for y in self.bufs:
            if y.space == space and y.lo < hi and lo < y.hi:
                y.ov.append(b)
                b.ov.append(y)
        self.bufs.append(b)
        return b

    def _rec(self, op, reads, writes):
        deps = set()
        for b in reads:
            for y in b.ov:
                if y.w is not None:
                    deps.add(y.w)
                if b.space == "ps":
                    deps.update(o for o in y.r if o.eng != op.eng)
        for b in writes:
            for y in b.ov:
                if y.w is not None:
                    deps.add(y.w)
                deps.update(y.r)
        deps.discard(op)
        op.deps = deps
        for b in reads:
            if not op.dma:
                b.r = [o for o in b.r if o.dma or o.eng != op.eng]
            b.r.append(op)
        for b in writes:
            b.w = op
            b.r = []
            for y in b.ov:
                if y is not b and y.lo >= b.lo and y.hi <= b.hi:
                    y.w = op
                    y.r = []
        op.id = len(self.ops)
        self.ops.append(op)

    def op(self, eng, fn, reads=(), writes=()):
        o = Op()
        r = _Rec()
        fn(r)
        assert r.calls
        o.eng, o.calls, o.dma, o.ndma, o.need, o.cnt, o.sem = eng, r.calls, False, 0, False, 0, None
        self._rec(o, list(reads), list(writes))
        return o

    def dma(self, queue, fn, key, n=1, reads=(), writes=(), store=False):
        o = Op()
        r = _Rec()
        fn(r, lambda ins: ins)
        n = len(r.calls)
        assert n >= 1
        o.eng, o.calls, o.dma, o.ndma, o.need = queue, r.calls, True, n, True
        c = self.dma_keys.setdefault(key, [0])
        c[0] += n
        o.cnt, o.sem = c[0], key
        self._rec(o, list(reads), list(writes))
        if store:
            self.store_ops.append(o)
        return o

    def emit(self):
        nc = self.nc
        for o in self.ops:
            for d in o.deps:
                if d.dma:
                    continue
                if o.dma or d.eng != o.eng or (SAME_ENG_SYNC and o.eng != "pe"):
                    d.need = True
        per = {e: [] for e in self.ENGS}
        for o in self.ops:
            per[o.eng].append(o)
        for e in self.ENGS:
            c = 0
            for o in per[e]:
                if not o.dma and o.need:
                    c += 1
                    o.cnt = c
        with ExitStack() as st:
            esem = {e: st.enter_context(nc.semaphore("s_" + e)) for e in self.ENGS}
            dsem = {k: st.enter_context(nc.semaphore("d_%d" % i)) for i, k in enumerate(self.dma_keys)}
            block = st.enter_context(nc.Block())
            engobj = {"pe": block.tensor, "act": block.scalar, "dve": block.vector,
                      "pool": block.gpsimd, "sp": block.sync}

            def run(ename):
                def body(eng):
                    waited = {}

                    def wait(sem, key, val):
                        if waited.get(key, 0) < val:
                            eng.wait_ge(sem, val)
                            waited[key] = val

                    for o in per[ename]:
                        need = {}
                        for d in o.deps:
                            if d.dma:
                                k = ("d", d.sem)
                                need[k] = max(need.get(k, 0), 16 * d.cnt)
                            elif d.eng != ename or o.dma or (SAME_ENG_SYNC and ename != "pe"):
                                k = ("e", d.eng)
                                need[k] = max(need.get(k, 0), d.cnt)
                        for k, v in need.items():
                            wait(dsem[k[1]] if k[0] == "d" else esem[k[1]], k, v)
                        ins = None
                        for (mname, a, k) in o.calls:
                            ins = getattr(eng, mname)(*a, **k)
                            if o.dma:
                                ins.then_inc(dsem[o.sem], 16)
                        if not o.dma and o.need:
                            ins.then_inc(esem[ename], 1)
                    if ename == "sp":
                        for key, c in self.dma_keys.items():
                            if any(s.sem == key for s in self.store_ops):
                                wait(dsem[key], ("d", key), 16 * c[0])
                return body

            for e in self.ENGS:
                engobj[e](run(e))


class Region:
    def __init__(self, S, arena, lo, hi):
        self.S, self.arena, self.lo, self.hi, self.cur = S, arena, lo, hi, lo

    def alloc(self, name, shape, dtype, nbuf=None):
        esz = 4 if dtype == F32 else 2
        n = 1
        for s in shape[1:]:
            n *= s
        nbytes = (n * esz + 3) // 4 * 4
        lo = self.cur
        self.cur += nbytes
        assert self.cur <= self.hi, (name, self.cur, self.hi)
        ap = self.arena[:, lo // 4:(lo + nbytes) // 4]
        if dtype == BF16:
            ap = ap.bitcast(BF16)
        ap = ap[:, 0:n]
        if len(shape) == 3:
            ap = ap.rearrange("p (a b) -> p a b", a=shape[1])
        elif len(shape) == 4:
            ap = ap.rearrange("p (a b c) -> p a b c", a=shape[1], b=shape[2])
        b = self.S.buf(name, "sb", lo, lo + nbytes)
        return b, ap


def build():
    nc = bass.Bass("TRN2", target_bir_lowering=False)
    dram = lambda n, s, dt=F32, k="ExternalInput": nc.dram_tensor(n, s, dt, kind=k).ap()
    x_d = dram("x", [NSEQ, SEQ, D])
    meta_d = dram("meta", [NM, D])
    vecs_d = dram("vecs", [128, NV])
    cst_d = dram("cst", [128, 288])
    rope_d = dram("rope", [32, 2, T])
    w_in_d = dram("w_in", [D, 1696])
    w_uq_d = dram("w_uq", [384, 768])
    w_kn_d = dram("w_kn", [256, 512])
    w_v_d = dram("w_v", [256, 512])
    lru_d = dram("lru_w", [2, 2, 8, 64, 64])
    w_out_d = dram("w_out", [D, D])
    wgu_d = dram("wgu", [NJF, 128, 2048])
    wd_d = dram("w_down", [DFF, D])
    out_d = dram("out", [NSEQ, SEQ, D], F32, "ExternalOutput")
    dbg_d = {}

    S = Sched(nc)
    K = 1024
    with ExitStack() as st:
        arena = st.enter_context(nc.sbuf_tensor("arena", [128, 212800 // 4], F32))
        banks = [st.enter_context(nc.psum_tensor("bank%d" % i, [128, 512], F32)) for i in range(8)]
        PB = [S.buf("bank%d" % i, "ps", i, i + 1) for i in range(8)]

        R0 = Region(S, arena, 0, 19 * K)
        b_vecs, vecs = R0.alloc("vecs", [128, NV], F32)
        b_ident, ident = R0.alloc("ident", [128, 128], BF16)
        b_ones, ones = R0.alloc("ones", [128, 128], BF16)
        b_pmat, pmat = R0.alloc("pmat", [128, 32], F32)
        b_rope, rope = R0.alloc("rope", [128, 2, T], F32)
        b_lamc, lamc = R0.alloc("lamc", [128, 16], F32)
        b_lamt, lamt = R0.alloc("lamt", [128, 8], F32)

        def V(c, p0=0, p1=128):
            return vecs[p0:p1, c:c + 1]

        RA = Region(S, arena, 19 * K, 51 * K)
        b_ornT, ornT = RA.alloc("ornT", [128, 4, SEQ], BF16)
        b_oatT, oatT = RA.alloc("oatT", [128, 4, SEQ], BF16)
        RL3 = Region(S, arena, 51 * K, 80 * K)
        b_cqnT, cqnT = RL3.alloc("cqnT", [128, 3, T], BF16)
        b_ckvnT, ckvnT = RL3.alloc("ckvnT", [128, 2, T], BF16)
        b_krT, krT = RL3.alloc("krT", [128, T], F32)

        S.dma("sp", lambda e, f: f(e.dma_start(out=vecs, in_=vecs_d[:, :])), "c_vecs", writes=[b_vecs])
        S.dma("pool", lambda e, f: f(e.dma_start(out=ident, in_=cst_d[:, 0:128])), "c_id", writes=[b_ident])
        S.dma("pool", lambda e, f: f(e.dma_start(out=ones, in_=cst_d[:, 128:256])), "c_on", writes=[b_ones])
        S.dma("sp", lambda e, f: f(e.dma_start(out=pmat, in_=cst_d[:, 256:288])), "c_pm", writes=[b_pmat])
        S.dma("sp", lambda e, f: f(e.dma_start(out=rope[64:96, :, :], in_=rope_d[:, :, :])), "c_rope", writes=[b_rope])
        S.op("act", lambda e: e.activation(out=lamt, in_=vecs[:, V_LAM:V_LAM + 8], func=AF.Exp, scale=-1.0),
             reads=[b_vecs], writes=[b_lamt])
        S.op("act", lambda e: e.activation(out=lamt, in_=lamt, func=AF.Ln, bias=V(V_ONE), scale=1.0),
             reads=[b_vecs, b_lamt], writes=[b_lamt])
        S.op("dve", lambda e: e.tensor_scalar(out=lamc[:, 0:8], in0=lamt, scalar1=-8.0, scalar2=None, op0=ALU.mult),
             reads=[b_lamt], writes=[b_lamc])
        S.op("dve", lambda e: e.tensor_scalar(out=lamc[:, 8:16], in0=lamt, scalar1=-16.0, scalar2=None, op0=ALU.mult),
             reads=[b_lamt], writes=[b_lamc])

        def dump(name, b, ap, shape, dt=F32):
            if not KDEBUG:
                return
            dd = dram("dbg_" + name, shape, dt, "ExternalOutput")
            dbg_d[name] = dd
            S.dma("sp", lambda e, f: f(e.dma_start(out=dd, in_=ap)), "dbg_" + name, reads=[b], store=True)

        def rstd_fm(ps_ap, rs_ap, np_, n, inv_n, b_ps, b_rs):
            S.op("act", lambda e: e.activation(out=rs_ap, in_=ps_ap, func=AF.Ln, bias=V(V_EPS, 0, np_), scale=inv_n),
                 reads=[b_ps, b_vecs], writes=[b_rs])
            S.op("act", lambda e: e.activation(out=rs_ap, in_=rs_ap, func=AF.Exp, scale=-0.5),
                 reads=[b_rs], writes=[b_rs])

        try:
            for s in range(NSEQ):
                R1a = Region(S, arena, 19 * K, 51 * K)
                b_xt, xt = [], []
                for i in range(2):
                    b, a = R1a.alloc("xt%d" % i, [128, 4, D], F32)
                    b_xt.append(b); xt.append(a)
                R1 = Region(S, arena, 80 * K, 158 * K)
                b_win, win = R1.alloc("w_in", [128, 8, 1696], BF16)
                b_xs, xs = R1.alloc("xs", [128, 4, D], BF16)
                b_xnT, xnT = [], []
                for i in range(2):
                    b, a = R1.alloc("xnT%d" % i, [128, 8, 512], BF16)
                    b_xnT.append(b); xnT.append(a)
                b_latf, latf = R1.alloc("latf", [128, 3, 512], F32)
                b_sqb, sqb = R1.alloc("sqb", [128, 3, 512], BF16)
                b_rs, rs = R1.alloc("rs", [128, 512], F32)
                b_gt1, gt1 = R1.alloc("gt1", [128, 512], F32)
                b_gt2, gt2 = R1.alloc("gt2", [128, 512], F32)
                b_ss, ss = R1.alloc("ss", [128, 4], F32)
                b_rstd, rstd = R1.alloc("rstd", [128, 4], F32)
                R2 = Region(S, arena, 158 * K, 207 * K + 800)
                b_xr, xr = [], []
                for cc in range(4):
                    b, a = R2.alloc("xr%d" % cc, [128, T + 4], F32)
                    b_xr.append(b); xr.append(a)
                b_gg, gg = R2.alloc("gg", [128, 4, SEQ], BF16)

                def ld_win(e, f):
                    for kc in range(8):
                        f(e.dma_start(out=win[:, kc, :], in_=w_in_d[kc * 128:(kc + 1) * 128, :]))
                S.dma("pool", ld_win, "w_in", n=8, writes=[b_win])
                for cc in range(4):
                    S.op("dve", lambda e, cc=cc: e.memset(xr[cc][:, 0:2], 0.0), writes=[b_xr[cc]])
                    S.op("dve", lambda e, cc=cc: e.memset(xr[cc][:, T + 2:T + 4], 0.0), writes=[b_xr[cc]])

                chunks = [(0, NM, None)] + [(NM + 512 * c, 512, c) for c in range(4)]

                def ld_x(ci):
                    pos0, n, c = chunks[ci]
                    sl = ci % 2
                    if c is None:
                        S.dma("sp", lambda e, f: f(e.dma_start(out=xt[sl][0:NM, 0, :], in_=meta_d[:, :])),
                              "xt%d" % sl, writes=[b_xt[sl]])
                    else:
                        src = x_d[s, 512 * c:512 * c + 512, :].rearrange("(j p) f -> p j f", p=128)
                        S.dma("sp", lambda e, f: f(e.dma_start(out=xt[sl], in_=src)), "xt%d" % sl, writes=[b_xt[sl]])

                def norm_tm(xin, b_xin, np_, nt, gcol, dstT, b_dstT, tpb):
                    n = 128 * nt if np_ == 128 else np_
                    for j in range(nt):
                        S.op("act", lambda e, j=j: e.activation(out=xs[0:np_, j, :], in_=xin[0:np_, j, :], func=AF.Square),
                             reads=[b_xin], writes=[b_xs])
                    S.op("dve", lambda e: e.tensor_reduce(out=ss[0:np_, 0:nt], in_=xs[0:np_, 0:nt, :], axis=AX.X, op=ALU.add),
                         reads=[b_xs], writes=[b_ss])
                    S.op("act", lambda e: e.activation(out=rstd[0:np_, 0:nt], in_=ss[0:np_, 0:nt], func=AF.Ln,
                                                       bias=V(V_EPS, 0, np_), scale=1.0 / D),
                         reads=[b_ss, b_vecs], writes=[b_rstd])
                    S.op("act", lambda e: e.activation(out=rstd[0:np_, 0:nt], in_=rstd[0:np_, 0:nt], func=AF.Exp, scale=-0.5),
                         reads=[b_rstd], writes=[b_rstd])
                    for j in range(nt):
                        S.op("dve", lambda e, j=j: e.tensor_scalar(out=xs[0:np_, j, :], in0=xin[0:np_, j, :],
                                                                   scalar1=rstd[0:np_, j:j + 1], scalar2=None, op0=ALU.mult),
                             reads=[b_xin, b_rstd], writes=[b_xs])
                    for kc in range(8):
                        bk = tpb[kc % 2]
                        tp = banks[bk][:, :].bitcast(BF16)

                        def tr(e, kc=kc, tp=tp):
                            ins = None
                            for j in range(nt):
                                w = np_
                                ins = e.transpose(out=tp[:, j * 128:j * 128 + w], in_=xs[0:np_, j, kc * 128:(kc + 1) * 128],
                                                  identity=ident[0:np_, 0:np_])
                            return ins
                        S.op("pe", tr, reads=[b_xs, b_ident], writes=[PB[bk]])
                        S.op("dve", lambda e, kc=kc, tp=tp: e.tensor_scalar(out=dstT[:, kc, 0:n], in0=tp[:, 0:n],
                                                                           scalar1=V(gcol + kc), scalar2=None, op0=ALU.mult),
                             reads=[PB[bk], b_vecs], writes=[b_dstT])

                def mm_acc(e, out_ap, lhs_list, rhs_list):
                    ins = None
                    n = len(lhs_list)
                    for i in range(n):
                        ins = e.matmul(out_ap, lhs_list[i], rhs_list[i], start=(i == 0), stop=(i == n - 1))
                    return ins

                mmb = [2, 3, 4, 5]
                mmi = [0]

                def next_bank():
                    b = mmb[mmi[0] % len(mmb)]
                    mmi[0] += 1
                    return b

                ld_x(0)
                for ci in range(5):
                    pos0, n, c = chunks[ci]
                    sl = ci % 2
                    if ci + 1 < 5:
                        ld_x(ci + 1)
                    np_ = 128 if c is not None else NM
                    nt = 4 if c is not None else 1
                    xT = xnT[sl]
                    bxT = b_xnT[sl]
                    if ci == 0:
                        phase_end("c0")
                    norm_tm(xt[sl], b_xt[sl], np_, nt, V_G1, xT, bxT, (0, 1))
                    phase_end("p1n%d" % ci)

                    def win_mm(col0, m, bk, p0=0):
                        S.op("pe", lambda e: mm_acc(e, banks[bk][p0:p0 + m, 0:n],
                                                    [win[:, kc, col0:col0 + m] for kc in range(8)],
                                                    [xT[:, kc, 0:n] for kc in range(8)]),
                             reads=[b_win, bxT], writes=[PB[bk]])

                    for (col0, ng, gcol, dst, b_dst, inv) in ((0, 3, V_GQA, cqnT, b_cqnT, 1.0 / 384),
                                                               (384, 2, V_GKVA, ckvnT, b_ckvnT, 1.0 / 256)):
                        for g in range(ng):
                            bk = next_bank()
                            win_mm(col0 + 128 * g, 128, bk)
                            S.op("act", lambda e, g=g, bk=bk: e.activation(out=sqb[:, g, 0:n], in_=banks[bk][:, 0:n], func=AF.Square),
                                 reads=[PB[bk]], writes=[b_sqb])
                            S.op("dve", lambda e, g=g, bk=bk: e.tensor_copy(out=latf[:, g, 0:n], in_=banks[bk][:, 0:n]),
                                 reads=[PB[bk]], writes=[b_latf])
                        S.op("pe", lambda e, ng=ng: mm_acc(e, banks[6][:, 0:n], [ones[:, :]] * ng,
                                                           [sqb[:, g, 0:n] for g in range(ng)]),
                             reads=[b_ones, b_sqb], writes=[PB[6]])
                        rstd_fm(banks[6][:, 0:n], rs[:, 0:n], 128, n, inv, PB[6], b_rs)
                        for g in range(ng):
                            S.op("dve", lambda e, g=g, dst=dst, gcol=gcol: e.scalar_tensor_tensor(
                                out=dst[:, g, pos0:pos0 + n], in0=latf[:, g, 0:n], scalar=V(gcol + g), in1=rs[:, 0:n],
                                op0=ALU.mult, op1=ALU.mult), reads=[b_latf, b_rs, b_vecs], writes=[b_dst])
                    phase_end("p1l%d" % ci)
                    bk = next_bank()
                    win_mm(640, 32, bk, p0=64)
                    S.op("act", lambda e, bk=bk: e.activation(out=krT[64:96, pos0:pos0 + n], in_=banks[bk][64:96, 0:n], func=AF.Copy),
                         reads=[PB[bk]], writes=[b_krT])
                    phase_end("p1k%d" % ci)
                    for cc in range(4):
                        bk = next_bank()
                        win_mm(672 + 128 * cc, 128, bk)
                        S.op("act", lambda e, cc=cc, bk=bk: e.activation(out=xr[cc][:, 2 + pos0:2 + pos0 + n], in_=banks[bk][:, 0:n], func=AF.Copy),
                             reads=[PB[bk]], writes=[b_xr[cc]])
                    if c is not None:
                        for cc in range(4):
                            bk = next_bank()
                            win_mm(1184 + 128 * cc, 128, bk)
                            S.op("act", lambda e, bk=bk: e.activation(out=gt1, in_=banks[bk][:, :], func=AF.Square),
                                 reads=[PB[bk]], writes=[b_gt1])
                            S.op("dve", lambda e: e.tensor_scalar(out=gt1, in0=gt1, scalar1=0.044715, scalar2=1.0,
                                                                  op0=ALU.mult, op1=ALU.add), reads=[b_gt1], writes=[b_gt1])
                            S.op("dve", lambda e, bk=bk: e.tensor_tensor(out=gt1, in0=banks[bk][:, :], in1=gt1, op=ALU.mult),
                                 reads=[PB[bk], b_gt1], writes=[b_gt1])
                            S.op("act", lambda e: e.activation(out=gt2, in_=gt1, func=AF.Sigmoid, scale=1.5957691216057308),
                                 reads=[b_gt1], writes=[b_gt2])
                            S.op("dve", lambda e, cc=cc, bk=bk, c=c: e.tensor_tensor(out=gg[:, cc, 512 * c:512 * c + 512],
                                                                                 in0=banks[bk][:, :], in1=gt2, op=ALU.mult),
                                 reads=[PB[bk], b_gt2], writes=[b_gg])
                if s == 0:
                    dump("cqnT", b_cqnT, cqnT, [128, 3, T], BF16)
                    dump("ckvnT", b_ckvnT, ckvnT, [128, 2, T], BF16)
                    dump("krT", b_krT, krT[64:96, :], [32, T])
                    dump("xr0", b_xr[0], xr[0], [128, T + 4])
                    dump("gg", b_gg, gg, [128, 4, SEQ], BF16)
                phase_end("p1")

                R4 = Region(S, arena, 80 * K, 158 * K)
                b_lw, lw = R4.alloc("lru_w", [128, 16, 128], BF16)
                b_xc, xc = R4.alloc("xc", [128, T], F32)
                b_xcb, xcb = R4.alloc("xcb", [128, T], BF16)
                lb = {}
                for nm in ("r", "i", "a", "b", "hf", "hb"):
                    lb[nm] = R4.alloc("l_" + nm, [128, T], F32)
                b_ctmp, ctmp = R4.alloc("ctmp", [128, T], F32)
                R4s = Region(S, arena, R4.cur - T * 4, R4.cur)
                b_sq4, sq4 = R4s.alloc("sq4", [128, 4, 512], BF16)
                b_rsr, rsr = R4s.alloc("rsr", [128, 512], F32)
                S.op("dve", lambda e: e.memset(lw, 0.0), writes=[b_lw])

                def ld_lru(e, f):
                    for g in range(2):
                        for d in range(2):
                            src = lru_d[g, d].rearrange("(c b) i j -> b i c j", b=2)
                            k0 = (g * 2 + d) * 4
                            for bh in range(2):
                                f(e.dma_start(out=lw[bh * 64:(bh + 1) * 64, k0:k0 + 4, bh * 64:(bh + 1) * 64], in_=src[bh]))
                S.dma("pool", ld_lru, "lru_w", n=8, writes=[b_lw])

                pieces = [(512 * p, 512) for p in range(4)] + [(2048, 16)]
                for cc in range(4):
                    S.op("pool", lambda e, cc=cc: e.tensor_scalar(out=xc, in0=xr[cc][:, 0:T], scalar1=V(V_CW + cc * 4),
                                                                  scalar2=V(V_CB + cc), op0=ALU.mult, op1=ALU.add),
                         reads=[b_xr[cc], b_vecs], writes=[b_xc])
                    for j in range(1, 4):
                        S.op("pool", lambda e, cc=cc, j=j: e.tensor_scalar(out=ctmp, in0=xr[cc][:, j:j + T], scalar1=V(V_CW + cc * 4 + j),
                                                                           scalar2=0.0, op0=ALU.mult, op1=ALU.add),
                             reads=[b_xr[cc], b_vecs], writes=[b_ctmp])
                        S.op("pool", lambda e: e.tensor_tensor(out=xc, in0=xc, in1=ctmp, op=ALU.add),
                             reads=[b_xc, b_ctmp], writes=[b_xc])
                    S.op("act", lambda e: e.activation(out=xcb, in_=xc, func=AF.Copy), reads=[b_xc], writes=[b_xcb])
                    for d in range(2):
                        b_r, r_ = lb["r"]; b_i, i_ = lb["i"]; b_a, a_ = lb["a"]; b_b, bb_ = lb["b"]
                        b_h, h_ = lb["hf"] if d == 0 else lb["hb"]
                        for (p0, pn) in pieces:
                            for g, (bdst, dst, bcol) in enumerate(((b_r, r_, V_BA), (b_i, i_, V_BI))):
                                bk = next_bank()
                                S.op("pe", lambda e, g=g, bk=bk, p0=p0, pn=pn, d=d, cc=cc: e.matmul(
                                    banks[bk][:, 0:pn], lw[:, (g * 2 + d) * 4 + cc, :], xcb[:, p0:p0 + pn], start=True, stop=True),
                                    reads=[b_lw, b_xcb], writes=[PB[bk]])
                                S.op("act", lambda e, bk=bk, p0=p0, pn=pn, dst=dst, bcol=bcol, d=d, cc=cc: e.activation(
                                    out=dst[:, p0:p0 + pn], in_=banks[bk][:, 0:pn], func=AF.Sigmoid,
                                    bias=V(bcol + d * 4 + cc), scale=1.0), reads=[PB[bk], b_vecs], writes=[bdst])
                        ci_ = d * 4 + cc
                        S.op("act", lambda e, ci_=ci_: e.activation(out=a_, in_=r_, func=AF.Exp, scale=lamc[:, ci_:ci_ + 1]),
                             reads=[b_r, b_lamc], writes=[b_a])
                        S.op("act", lambda e, ci_=ci_: e.activation(out=r_, in_=r_, func=AF.Exp, scale=lamc[:, 8 + ci_:9 + ci_]),
                             reads=[b_r, b_lamc], writes=[b_r])
                        S.op("act", lambda e: e.activation(out=r_, in_=r_, func=AF.Sqrt, bias=V(V_ONE), scale=-1.0),
                             reads=[b_r, b_vecs], writes=[b_r])
                        S.op("pool", lambda e: e.tensor_tensor(out=bb_, in0=i_, in1=xc, op=ALU.mult),
                             reads=[b_i, b_xc], writes=[b_b])
                        S.op("dve", lambda e: e.tensor_tensor(out=bb_, in0=bb_, in1=r_, op=ALU.mult),
                             reads=[b_b, b_r], writes=[b_b])
                        if d == 0:
                            S.op("dve", lambda e, h_=h_: e.tensor_tensor_scan(out=h_, data0=a_, data1=bb_, initial=0.0,
                                                                             op0=ALU.mult, op1=ALU.add),
                                 reads=[b_a, b_b], writes=[b_h])
                        else:
                            S.op("dve", lambda e, h_=h_: e.tensor_tensor_scan(out=h_[:, ::-1], data0=a_[:, ::-1], data1=bb_[:, ::-1],
                                                                             initial=0.0, op0=ALU.mult, op1=ALU.add),
                                 reads=[b_a, b_b], writes=[b_h])
                    b_hf, hf = lb["hf"]; b_hb, hb = lb["hb"]
                    S.op("pool", lambda e: e.tensor_tensor(out=hf[:, NM:T], in0=hf[:, NM:T], in1=hb[:, NM:T], op=ALU.add),
                         reads=[b_hf, b_hb], writes=[b_hf])
                    S.op("dve", lambda e, cc=cc: e.tensor_tensor(out=xr[cc][:, 2 + NM:2 + T], in0=hf[:, NM:T], in1=gg[:, cc, :], op=ALU.mult),
                         reads=[b_hf, b_gg], writes=[b_xr[cc]])
                for c in range(4):
                    c0 = 2 + NM + 512 * c
                    for cc in range(4):
                        S.op("act", lambda e, cc=cc, c0=c0: e.activation(out=sq4[:, cc, :], in_=xr[cc][:, c0:c0 + 512], func=AF.Square),
                             reads=[b_xr[cc]], writes=[b_sq4])
                    S.op("pe", lambda e: mm_acc(e, banks[6][:, :], [ones[:, :]] * 4, [sq4[:, cc, :] for cc in range(4)]),
                         reads=[b_ones, b_sq4], writes=[PB[6]])
                    rstd_fm(banks[6][:, :], rsr, 128, 512, 1.0 / 512, PB[6], b_rsr)
                    for cc in range(4):
                        S.op("dve", lambda e, cc=cc, c0=c0, c=c: e.scalar_tensor_tensor(
                            out=ornT[:, cc, 512 * c:512 * c + 512], in0=xr[cc][:, c0:c0 + 512], scalar=V(V_GR + cc), in1=rsr,
                            op0=ALU.mult, op1=ALU.mult), reads=[b_xr[cc], b_rsr, b_vecs], writes=[b_ornT])
                if s == 0:
                    dump("ornT", b_ornT, ornT, [128, 4, SEQ], BF16)
                phase_end("p2")

                R6 = Region(S, arena, 80 * K, 207 * K + 800)
                b_wkn, wkn = R6.alloc("w_kn", [128, 2, 512], BF16)
                b_wv, wv = R6.alloc("w_v", [128, 2, 512], BF16)
                b_wuq, wuq = R6.alloc("w_uq", [128, 3, 768], BF16)
                b_KT, KT = [], []
                for h in range(NH):
                    b, a = R6.alloc("KT%d" % h, [128, T], BF16)
                    b_KT.append(b); KT.append(a)
                b_va, va = R6.alloc("vaug", [128, 17, NH, 128], BF16)
                b_QT, QT = [], []
                for i in range(2):
                    b, a = R6.alloc("QT%d" % i, [128, NH, 512], BF16)
                    b_QT.append(b); QT.append(a)
                b_PT, PT = [], []
                for i in range(4):
                    b, a = R6.alloc("PT%d" % i, [128, 512], BF16)
                    b_PT.append(b); PT.append(a)
                b_oraw, oraw = [], []
                for i in range(4):
                    b, a = R6.alloc("oraw%d" % i, [128, 512], F32)
                    b_oraw.append(b); oraw.append(a)
                b_sqk, sqk = [], []
                for i in range(2):
                    b, a = R6.alloc("sqk%d" % i, [128, 512], BF16)
                    b_sqk.append(b); sqk.append(a)
                b_rden, rden = [], []
                for i in range(2):
                    b, a = R6.alloc("rden%d" % i, [128, 512], F32)
                    b_rden.append(b); rden.append(a)
                b_xk, xk = R6.alloc("xk", [128, 512], F32)
                b_t1, t1 = R6.alloc("t1", [128, 512], F32)
                b_t2, t2 = R6.alloc("t2", [128, 512], F32)
                b_krp, krp = R6.alloc("krp", [128, 512], F32)
                b_rec, rec = R6.alloc("rec", [128, 512], F32)
                b_sqa, sqa = R6.alloc("sqa", [128, 4, 512], BF16)
                b_rsa, rsa = R6.alloc("rsa", [128, 512], F32)

                def ld_kvw(e, f):
                    for kc in range(2):
                        f(e.dma_start(out=wkn[:, kc, :], in_=w_kn_d[kc * 128:(kc + 1) * 128, :]))
                        f(e.dma_start(out=wv[:, kc, :], in_=w_v_d[kc * 128:(kc + 1) * 128, :]))
                S.dma("pool", ld_kvw, "w_kv", n=4, writes=[b_wkn, b_wv])

                def ld_uq(e, f):
                    for kc in range(3):
                        f(e.dma_start(out=wuq[:, kc, :], in_=w_uq_d[kc * 128:(kc + 1) * 128, :]))
                S.dma("pool", ld_uq, "w_uq", n=3, writes=[b_wuq])
                S.op("pool", lambda e: e.memset(va, 1.0), writes=[b_va])

                def rope_rows(src_ps_or_sb, b_src, gcol, cols0, n, dst, b_dst, rd, b_rd, bk_px):
                    S.op("dve", lambda e: e.tensor_scalar(out=xk[64:96, 0:n], in0=src_ps_or_sb, scalar1=V(gcol, 64, 96),
                                                          scalar2=None, op0=ALU.mult), reads=[b_src, b_vecs], writes=[b_xk])
                    S.op("pe", lambda e: e.matmul(banks[bk_px][64:96, 0:n], pmat[64:96, 0:32], xk[64:96, 0:n], start=True, stop=True),
                         reads=[b_pmat, b_xk], writes=[PB[bk_px]])
                    S.op("dve", lambda e: e.tensor_tensor(out=t1[64:96, 0:n], in0=xk[64:96, 0:n], in1=rope[64:96, 0, cols0:cols0 + n], op=ALU.mult),
                         reads=[b_xk, b_rope], writes=[b_t1])
                    S.op("dve", lambda e: e.tensor_tensor(out=t2[64:96, 0:n], in0=banks[bk_px][64:96, 0:n], in1=rope[64:96, 1, cols0:cols0 + n], op=ALU.mult),
                         reads=[PB[bk_px], b_rope], writes=[b_t2])
                    S.op("dve", lambda e: e.tensor_tensor(out=t1[64:96, 0:n], in0=t1[64:96, 0:n], in1=t2[64:96, 0:n], op=ALU.add),
                         reads=[b_t1, b_t2], writes=[b_t1])
                    if rd is None:
                        S.op("dve", lambda e: e.tensor_copy(out=dst, in_=t1[64:96, 0:n]), reads=[b_t1], writes=[b_dst])
                    else:
                        S.op("dve", lambda e: e.tensor_tensor(out=dst, in0=t1[64:96, 0:n], in1=rd, op=ALU.mult),
                             reads=[b_t1, b_rd], writes=[b_dst])

                for ci in range(5):
                    pos0, n, c = chunks[ci]
                    rope_rows(krT[64:96, pos0:pos0 + n], b_krT, V_KG, pos0, n, krp[64:96, 0:n], b_krp, None, None, 7)
                    for i in range(2):
                        S.op("act", lambda e, i=i: e.activation(out=sqk[i][64:96, 0:n], in_=krT[64:96, pos0:pos0 + n], func=AF.Square),
                             reads=[b_krT], writes=[b_sqk[i]])
                    for h in range(NH):
                        bk = next_bank()
                        sl = h % 2
                        S.op("pe", lambda e, h=h, bk=bk: mm_acc(e, banks[bk][0:64, 0:n],
                                                               [wkn[:, kc, h * 64:(h + 1) * 64] for kc in range(2)],
                                                               [ckvnT[:, kc, pos0:pos0 + n] for kc in range(2)]),
                             reads=[b_wkn, b_ckvnT], writes=[PB[bk]])
                        S.op("act", lambda e, bk=bk, sl=sl: e.activation(out=sqk[sl][0:64, 0:n], in_=banks[bk][0:64, 0:n], func=AF.Square),
                             reads=[PB[bk]], writes=[b_sqk[sl]])
                        S.op("pe", lambda e, sl=sl: e.matmul(banks[6][0:96, 0:n], ones[0:96, 0:96], sqk[sl][0:96, 0:n], start=True, stop=True),
                             reads=[b_ones, b_sqk[sl]], writes=[PB[6]])
                        rstd_fm(banks[6][0:96, 0:n], rden[sl][0:96, 0:n], 96, n, 1.0 / 96, PB[6], b_rden[sl])
                        S.op("dve", lambda e, h=h, bk=bk, sl=sl: e.scalar_tensor_tensor(
                            out=KT[h][0:64, pos0:pos0 + n], in0=banks[bk][0:64, 0:n], scalar=V(V_KG, 0, 64), in1=rden[sl][0:64, 0:n],
                            op0=ALU.mult, op1=ALU.mult), reads=[PB[bk], b_rden[sl], b_vecs], writes=[b_KT[h]])
                        S.op("dve", lambda e, h=h, sl=sl: e.tensor_tensor(out=KT[h][64:96, pos0:pos0 + n], in0=krp[64:96, 0:n],
                                                                         in1=rden[sl][64:96, 0:n], op=ALU.mult),
                             reads=[b_krp, b_rden[sl]], writes=[b_KT[h]])
                    ntile = 1 if c is None else 4
                    for j in range(ntile):
                        kt = 0 if c is None else 1 + 4 * c + j
                        npk = NM if c is None else 128
                        bk = next_bank()
                        S.op("pe", lambda e, j=j, bk=bk, npk=npk: mm_acc(e, banks[bk][0:npk, :],
                                                                        [ckvnT[:, kc, pos0 + 128 * j:pos0 + 128 * j + npk] for kc in range(2)],
                                                                        [wv[:, kc, :] for kc in range(2)]),
                             reads=[b_ckvnT, b_wv], writes=[PB[bk]])
                        for par in range(2):
                            src = banks[bk][0:npk, :].rearrange("p (a b d) -> p a b d", a=4, b=2)[:, :, par, :]
                            S.op("act", lambda e, kt=kt, par=par, src=src, npk=npk: e.activation(
                                out=va[0:npk, kt, par::2, par * 64:par * 64 + 64], in_=src, func=AF.Copy),
                                reads=[PB[bk]], writes=[b_va])
                if s == 0:
                    dump("KT0", b_KT[0], KT[0][0:96, :], [96, T], BF16)
                    dump("KT3", b_KT[3], KT[3][0:96, :], [96, T], BF16)
                    dump("vaug", b_va, va, [128, 17, NH, 128], BF16)
                phase_end("kv")

                def qprep_stages(c, h):
                    sl = c % 2
                    pos0 = NM + 512 * c
                    bk = 4
                    sk = h % 2
                    n = 512

                    def st0():
                        S.op("pe", lambda e: mm_acc(e, banks[bk][0:96, :],
                                                    [wuq[:, kc, h * 96:(h + 1) * 96] for kc in range(3)],
                                                    [cqnT[:, kc, pos0:pos0 + 512] for kc in range(3)]),
                             reads=[b_wuq, b_cqnT], writes=[PB[bk]])

                    def st1():
                        S.op("act", lambda e: e.activation(out=sqk[sk][0:96, :], in_=banks[bk][0:96, :], func=AF.Square),
                             reads=[PB[bk]], writes=[b_sqk[sk]])

                    def st2():
                        S.op("pe", lambda e: e.matmul(banks[6][0:96, :], ones[0:96, 0:96], sqk[sk][0:96, :], start=True, stop=True),
                             reads=[b_ones, b_sqk[sk]], writes=[PB[6]])

                    def st3():
                        rstd_fm(banks[6][0:96, :], rden[sk][0:96, :], 96, 512, 1.0 / 96, PB[6], b_rden[sk])

                    def st4():
                        S.op("dve", lambda e: e.scalar_tensor_tensor(
                            out=QT[sl][0:64, h, :], in0=banks[bk][0:64, :], scalar=V(V_QG, 0, 64), in1=rden[sk][0:64, :],
                            op0=ALU.mult, op1=ALU.mult), reads=[PB[bk], b_rden[sk], b_vecs], writes=[b_QT[sl]])
                        S.op("dve", lambda e: e.tensor_scalar(out=xk[64:96, 0:n], in0=banks[bk][64:96, :], scalar1=V(V_QG, 64, 96),
                                                              scalar2=None, op0=ALU.mult), reads=[PB[bk], b_vecs], writes=[b_xk])

                    def st5():
                        S.op("pe", lambda e: e.matmul(banks[7][64:96, 0:n], pmat[64:96, 0:32], xk[64:96, 0:n], start=True, stop=True),
                             reads=[b_pmat, b_xk], writes=[PB[7]])

                    def st6():
                        S.op("dve", lambda e: e.tensor_tensor(out=t1[64:96, 0:n], in0=xk[64:96, 0:n], in1=rope[64:96, 0, pos0:pos0 + n], op=ALU.mult),
                             reads=[b_xk, b_rope], writes=[b_t1])
                        S.op("dve", lambda e: e.tensor_tensor(out=t2[64:96, 0:n], in0=banks[7][64:96, 0:n], in1=rope[64:96, 1, pos0:pos0 + n], op=ALU.mult),
                             reads=[PB[7], b_rope], writes=[b_t2])
                        S.op("dve", lambda e: e.tensor_tensor(out=t1[64:96, 0:n], in0=t1[64:96, 0:n], in1=t2[64:96, 0:n], op=ALU.add),
                             reads=[b_t1, b_t2], writes=[b_t1])
                        S.op("dve", lambda e: e.tensor_tensor(out=QT[sl][64:96, h, :], in0=t1[64:96, 0:n], in1=rden[sk][64:96, :], op=ALU.mult),
                             reads=[b_t1, b_rden[sk]], writes=[b_QT[sl]])
                    return [st0, st1, st2, st3, st4, st5, st6]

                def qprep_head(c, h):
                    for st in qprep_stages(c, h):
                        st()

                ktiles = [(0, NM)] + [(NM + 128 * i, 128) for i in range(16)]
                xcb_junk = va[:, 1, 0:4, :].rearrange('p a b -> p (a b)')
                sbk = [0, 1]
                JUNK = int(os.environ.get('KJUNK', '384'))
                obk = [2, 3]
                scale = 96.0 ** -0.5
                LAG = int(os.environ.get('KLAG', '2'))
                INTER = os.environ.get('KINTER', '1') == '1'

                def attention(c, inter=None):
                    sl = c % 2
                    steps = [(h, kt) for h in range(NH) for kt in range(17)]
                    nst = len(steps)

                    def emit_S(i):
                        h, kt = steps[i]
                        if inter is not None:
                            inter(h, kt)
                        k0, nk = ktiles[kt]
                        sb_ = sbk[i % 2]
                        pt = i % 4
                        S.op("pe", lambda e: e.matmul(banks[sb_][0:nk, :], KT[h][0:96, k0:k0 + nk], QT[sl][0:96, h, :],
                                                      start=True, stop=True),
                             reads=[b_KT[h], b_QT[sl]], writes=[PB[sb_]])
                        S.op("act", lambda e: e.activation(out=PT[pt][0:nk, :], in_=banks[sb_][0:nk, :], func=AF.Exp, scale=scale),
                             reads=[PB[sb_]], writes=[b_PT[pt]])

                    def emit_PV(i):
                        h, kt = steps[i]
                        k0, nk = ktiles[kt]
                        pt = i % 4
                        ob = obk[h % 2]
                        par = h % 2
                        S.op("pe", lambda e: e.matmul(banks[ob][:, :], va[0:nk, kt, h, :], PT[pt][0:nk, :],
                                                      start=(kt == 0), stop=(kt == 16)),
                             reads=[b_va, b_PT[pt]], writes=[PB[ob]])
                        if kt == 16:
                            own = slice(0, 64) if par == 0 else slice(64, 128)
                            oth = slice(64, 128) if par == 0 else slice(0, 64)
                            pr = h // 2
                            S.op("dve", lambda e: e.reciprocal(out=rec[own, :], in_=banks[ob][oth, :]),
                                 reads=[PB[ob]], writes=[b_rec])
                            S.op("dve", lambda e: e.tensor_tensor(out=oraw[pr][own, :], in0=banks[ob][own, :],
                                                                  in1=rec[own, :], op=ALU.mult),
                                 reads=[PB[ob], b_rec], writes=[b_oraw[pr]])

                    for i in range(nst + LAG):
                        if i < nst:
                            emit_S(i)
                        if JUNK:
                            S.op("pe", lambda e: e.matmul(banks[5][:, 0:JUNK], ones[:, :], xcb_junk[:, 0:JUNK], start=True, stop=True),
                                 reads=[b_ones], writes=[PB[5]])
                        if i >= LAG:
                            emit_PV(i - LAG)
                    for pr in range(4):
                        S.op("act", lambda e, pr=pr: e.activation(out=sqa[:, pr, :], in_=oraw[pr], func=AF.Square),
                             reads=[b_oraw[pr]], writes=[b_sqa])
                    S.op("pe", lambda e: mm_acc(e, banks[6][:, :], [ones[:, :]] * 4, [sqa[:, pr, :] for pr in range(4)]),
                         reads=[b_ones, b_sqa], writes=[PB[6]])
                    rstd_fm(banks[6][:, :], rsa, 128, 512, 1.0 / 512, PB[6], b_rsa)
                    for pr in range(4):
                        S.op("dve", lambda e, pr=pr: e.scalar_tensor_tensor(
                            out=oatT[:, pr, 512 * c:512 * c + 512], in0=oraw[pr], scalar=V(V_GA + pr), in1=rsa,
                            op0=ALU.mult, op1=ALU.mult), reads=[b_oraw[pr], b_rsa, b_vecs], writes=[b_oatT])

                mmb[:] = [4]
                for h in range(NH):
                    qprep_head(0, h)
                for c in range(4):
                    if INTER:
                        if c + 1 < 4:
                            stg = {h: qprep_stages(c + 1, h) for h in range(NH)}
                            attention(c, lambda h, kt, stg=stg: stg[h][(kt - 1) // 2]() if (kt % 2 == 1 and kt < 15) else None)
                        else:
                            attention(c, None)
                    else:
                        if c + 1 < 4:
                            for h in range(NH):
                                qprep_head(c + 1, h)
                        attention(c, None)
                mmb[:] = [2, 3, 4, 5]
                if s == 0:
                    dump("oatT", b_oatT, oatT, [128, 4, SEQ], BF16)
                phase_end("att")

                R8 = Region(S, arena, 51 * K, 207 * K + 800)
                b_wo, wo = R8.alloc("w_out", [128, 8, D], BF16)
                b_wd, wd = R8.alloc("w_down", [128, NJF, D], BF16)
                b_ring, ring = [], []
                for i in range(RING):
                    b, a = R8.alloc("ring%d" % i, [128, 2, 8, 128], BF16)
                    b_ring.append(b); ring.append(a)
                b_aT, aT = R8.alloc("aT", [128, NJF, 512], BF16)
                b_hnT, hnT = R8.alloc("hnT", [128, 8, 512], BF16)
                b_xh, xh = R8.alloc("xh", [128, 4, D], F32)
                b_xs5, xs5 = R8.alloc("xs5", [128, 4, D], BF16)
                b_sg, sg = [], []
                for i in range(2):
                    b, a = R8.alloc("sg%d" % i, [128, 512], F32)
                    b_sg.append(b); sg.append(a)
                b_ss5, ss5 = R8.alloc("ss5", [128, 4], F32)
                b_rstd5, rstd5 = R8.alloc("rstd5", [128, 4], F32)
                b_xs, xs, b_ss, ss, b_rstd, rstd = b_xs5, xs5, b_ss5, ss5, b_rstd5, rstd5

                def ld_wo(e, f):
                    for kc in range(8):
                        f(e.dma_start(out=wo[:, kc, :], in_=w_out_d[kc * 128:(kc + 1) * 128, :]))
                S.dma("pool", ld_wo, "w_out", n=8, writes=[b_wo])

                def ld_wd(e, f):
                    for jf in range(NJF):
                        f(e.dma_start(out=wd[:, jf, :], in_=wd_d[jf * 128:(jf + 1) * 128, :]))
                S.dma("pool", ld_wd, "w_down", n=NJF, writes=[b_wd])

                ring_i = [0]

                def ld_ring(jf):
                    sl = ring_i[0] % RING
                    ring_i[0] += 1
                    dst = ring[sl].rearrange("p a b c -> p (a b c)")
                    S.dma("pool", lambda e, f: f(e.dma_start(out=dst, in_=wgu_d[jf])), "ring%d" % sl, writes=[b_ring[sl]])
                    return sl

                PRE = RING - 1
                for c in range(4):
                    src = x_d[s, 512 * c:512 * c + 512, :].rearrange("(j p) f -> p j f", p=128)
                    S.dma("sp", lambda e, f, src=src: f(e.dma_start(out=xh, in_=src)), "xh", writes=[b_xh])
                    slots = {}
                    for jf in range(PRE):
                        slots[jf] = ld_ring(jf)
                    for j in range(4):
                        for half in range(2):
                            bk = half
                            lhs = [oatT[:, kc, 512 * c + 128 * j:512 * c + 128 * j + 128] for kc in range(4)] + \
                                  [ornT[:, kc, 512 * c + 128 * j:512 * c + 128 * j + 128] for kc in range(4)]
                            rhs = [wo[:, kc, half * 512:(half + 1) * 512] for kc in range(8)]
                            S.op("pe", lambda e, bk=bk, lhs=lhs, rhs=rhs: mm_acc(e, banks[bk][:, :], lhs, rhs),
                                 reads=[b_oatT, b_ornT, b_wo], writes=[PB[bk]])
                            S.op("dve", lambda e, bk=bk, j=j, half=half: e.tensor_tensor(
                                out=xh[:, j, half * 512:(half + 1) * 512], in0=banks[bk][:, :], in1=xh[:, j, half * 512:(half + 1) * 512],
                                op=ALU.add), reads=[PB[bk], b_xh], writes=[b_xh])
                    if s == 0 and c == 0:
                        dump("h0", b_xh, xh, [128, 4, D])
                    norm_tm(xh, b_xh, 128, 4, V_G2, hnT, b_hnT, (2, 3))
                    for jf in range(NJF):
                        if jf + PRE < NJF:
                            slots[jf + PRE] = ld_ring(jf + PRE)
                        sl = slots[jf]
                        gb = 4 + jf % 2
                        ub = 6 + jf % 2
                        S.op("pe", lambda e, sl=sl, gb=gb: mm_acc(e, banks[gb][:, :], [ring[sl][:, 0, kc, :] for kc in range(8)],
                                                                 [hnT[:, kc, :] for kc in range(8)]),
                             reads=[b_ring[sl], b_hnT], writes=[PB[gb]])
                        S.op("pe", lambda e, sl=sl, ub=ub: mm_acc(e, banks[ub][:, :], [ring[sl][:, 1, kc, :] for kc in range(8)],
                                                                 [hnT[:, kc, :] for kc in range(8)]),
                             reads=[b_ring[sl], b_hnT], writes=[PB[ub]])
                        S.op("act", lambda e, gb=gb, jf=jf: e.activation(out=sg[jf % 2], in_=banks[gb][:, :], func=AF.Silu),
                             reads=[PB[gb]], writes=[b_sg[jf % 2]])
                        S.op("dve", lambda e, ub=ub, jf=jf: e.tensor_tensor(out=aT[:, jf, :], in0=banks[ub][:, :], in1=sg[jf % 2], op=ALU.mult),
                             reads=[PB[ub], b_sg[jf % 2]], writes=[b_aT])
                    for j in range(4):
                        for half in range(2):
                            bk = half
                            S.op("pe", lambda e, bk=bk, j=j, half=half: mm_acc(
                                e, banks[bk][:, :], [aT[:, jf, 128 * j:128 * j + 128] for jf in range(NJF)],
                                [wd[:, jf, half * 512:(half + 1) * 512] for jf in range(NJF)]),
                                reads=[b_aT, b_wd], writes=[PB[bk]])
                            S.op("dve", lambda e, bk=bk, j=j, half=half: e.tensor_tensor(
                                out=xh[:, j, half * 512:(half + 1) * 512], in0=banks[bk][:, :], in1=xh[:, j, half * 512:(half + 1) * 512],
                                op=ALU.add), reads=[PB[bk], b_xh], writes=[b_xh])
                    dst = out_d[s, 512 * c:512 * c + 512, :].rearrange("(j p) f -> p j f", p=128)
                    S.dma("sp", lambda e, f, dst=dst: f(e.dma_start(out=dst, in_=xh)), "xh_st", reads=[b_xh], store=True)
        except _Stop:
            pass

        S.emit()
    return nc, dbg_d


def _host_layout(inp):
    f = lambda a: np.ascontiguousarray(np.asarray(a, dtype=np.float32))
    vecs = np.zeros((128, NV), np.float32)
    col = lambda v, n: f(v).reshape(n, 128).T
    vecs[:, V_G1:V_G1 + 8] = col(inp["ln1_g"][0], 8)
    vecs[:, V_GQA:V_GQA + 3] = col(inp["q_a_norm_g"][0], 3)
    vecs[:, V_GKVA:V_GKVA + 2] = col(inp["kv_a_norm_g"][0], 2)
    vecs[0:96, V_QG] = f(inp["q_norm_g"][0])
    vecs[0:96, V_KG] = f(inp["k_norm_g"][0])
    cw = f(inp["conv_w"][0])
    for cc in range(4):
        for j in range(4):
            vecs[:, V_CW + cc * 4 + j] = cw[j, cc * 128:(cc + 1) * 128]
    vecs[:, V_CB:V_CB + 4] = col(inp["conv_b"][0], 4)
    for d in range(2):
        vecs[:, V_BA + d * 4:V_BA + d * 4 + 4] = col(inp["lru_ba"][0, d], 4)
        vecs[:, V_BI + d * 4:V_BI + d * 4 + 4] = col(inp["lru_bi"][0, d], 4)
        vecs[:, V_LAM + d * 4:V_LAM + d * 4 + 4] = col(inp["lru_lambda"][0, d], 4)
    vecs[:, V_GA:V_GA + 4] = col(inp["attn_out_g"][0], 4)
    vecs[:, V_GR:V_GR + 4] = col(inp["rnn_out_g"][0], 4)
    vecs[:, V_G2:V_G2 + 8] = col(inp["ln2_g"][0], 8)
    vecs[:, V_EPS] = EPS
    vecs[:, V_ONE] = 1.0
    cst = np.zeros((128, 288), np.float32)
    cst[:, 0:128] = np.eye(128, dtype=np.float32)
    cst[:, 128:256] = 1.0
    pm = np.zeros((32, 32), np.float32)
    for m in range(16):
        pm[m + 16, m] = -1.0
        pm[m, m + 16] = 1.0
    cst[64:96, 256:288] = pm
    half = 16
    freqs = (1.0 / (np.float32(10000.0) ** (np.arange(half, dtype=np.float32) / np.float32(half)))).astype(np.float32)
    ang = (np.arange(T, dtype=np.float32)[:, None] * freqs[None, :]).astype(np.float32)
    cos = np.cos(ang).astype(np.float32).T
    sin = np.sin(ang).astype(np.float32).T
    rope = np.zeros((32, 2, T), np.float32)
    rope[0:16, 0] = cos; rope[16:32, 0] = cos
    rope[0:16, 1] = sin; rope[16:32, 1] = sin
    w_ukv = f(inp["w_ukv"][0]).reshape(256, NH, 128)
    w_kn = np.ascontiguousarray(w_ukv[:, :, 0:64].reshape(256, 512))
    w_v = np.ascontiguousarray(w_ukv[:, :, 64:128].reshape(256, 512))
    lru_w = np.ascontiguousarray(np.stack([f(inp["lru_wa"][0]), f(inp["lru_wi"][0])], axis=0))
    wg = f(inp["w_gate"][0]).reshape(8, 128, NJF, 128)
    wu = f(inp["w_up"][0]).reshape(8, 128, NJF, 128)
    wgu = np.ascontiguousarray(np.stack([wg, wu], axis=0).transpose(3, 2, 0, 1, 4).reshape(NJF, 128, 2048))
    shared = {
        "meta": f(inp["meta_tokens"]), "vecs": vecs, "cst": cst, "rope": rope,
        "w_in": f(inp["w_in"][0]), "w_uq": f(inp["w_uq"][0]), "w_kn": w_kn, "w_v": w_v, "lru_w": lru_w,
        "w_out": f(inp["w_out"][0]), "wgu": wgu, "w_down": f(inp["w_down"][0]),
    }
    return shared


_CACHE = {}


def kernel(**inputs):
    x = np.asarray(inputs["x"], dtype=np.float32)
    shared = _host_layout(inputs)
    if "nc" not in _CACHE:
        _CACHE["nc"] = build()
    nc, dbg = _CACHE["nc"]
    in_maps = []
    for i in range(NCORES):
        m = dict(shared)
        m["x"] = np.ascontiguousarray(x[NSEQ * i:NSEQ * (i + 1)])
        in_maps.append(m)
    res = run_bass_kernel_spmd(nc, in_maps, core_ids=list(range(NCORES)))
    if KDEBUG:
        _CACHE["dbg"] = {k: np.asarray(res.results[0]["dbg_" + k]) for k in dbg}
    out = np.concatenate([np.asarray(res.results[i]["out"]) for i in range(NCORES)], axis=0)
    return out.astype(np.float32)
```

```python
import os
from contextlib import ExitStack
import numpy as np
import concourse.bass as bass
import concourse.mybir as mybir
from concourse.bass_utils import run_bass_kernel_spmd

F32 = mybir.dt.float32
BF16 = mybir.dt.bfloat16
AF = mybir.ActivationFunctionType
ALU = mybir.AluOpType
AX = mybir.AxisListType

NCORES = 8
NSEQ = 2
D = 1024
SEQ = 2048
NM = 16
T = SEQ + NM
NH = 8
DFF = 2816
NJF = DFF // 128
EPS = 1e-6
RING = 7
KDEBUG = os.environ.get("KDEBUG", "") != ""
KSTOP = os.environ.get("KSTOP", "")
SAME_ENG_SYNC = os.environ.get("KNOSAME", "") == ""


class _Stop(Exception):
    pass


def phase_end(name):
    if KSTOP == name:
        raise _Stop()

V_G1, V_GQA, V_GKVA, V_QG, V_KG, V_CW, V_CB, V_BA, V_BI, V_LAM, V_GA, V_GR, V_G2, V_EPS, V_ONE = (
    0, 8, 11, 13, 14, 15, 31, 35, 43, 51, 59, 63, 67, 75, 76)
NV = 80


class Buf:
    def __init__(self, name, space, lo, hi):
        self.name, self.space, self.lo, self.hi = name, space, lo, hi
        self.w = None
        self.r = []
        self.ov = [self]


class Op:
    __slots__ = ("eng", "calls", "dma", "ndma", "deps", "need", "cnt", "sem", "id")


class _Rec:
    def __init__(self):
        self.calls = []

    def __getattr__(self, name):
        def m(*a, **k):
            self.calls.append((name, a, k))
            return self
        return m


class Sched:
    ENGS = ("pe", "act", "dve", "pool", "sp")

    def __init__(self, nc):
        self.nc = nc
        self.bufs = []
        self.ops = []
        self.dma_keys = {}
        self.store_ops = []

    def buf(self, name, space, lo, hi):
        b = Buf(name, space, lo, hi)
        for y in self.bufs:
            if y.space == space and y.lo < hi and lo < y.hi:
                y.ov.append(b)
                b.ov.append(y)
        self.bufs.append(b)
        return b

    def _rec(self, op, reads, writes):
        deps = set()
        for b in reads:
            for y in b.ov:
                if y.w is not None:
                    deps.add(y.w)
                if b.space == "ps":
                    deps.update(o for o in y.r if o.eng != op.eng)
        for b in writes:
            for y in b.ov:
                if y.w is not None:
                    deps.add(y.w)
                deps.update(y.r)
        deps.discard(op)
        op.deps = deps
        for b in reads:
            if not op.dma:
                b.r = [o for o in b.r if o.dma or o.eng != op.eng]
            b.r.append(op)
        for b in writes:
            b.w = op
            b.r = []
            for y in b.ov:
                if y is not b and y.lo >= b.lo and y.hi <= b.hi:
                    y.w = op
                    y.r = []
        op.id = len(self.ops)
        self.ops.append(op)

    def op(self, eng, fn, reads=(), writes=()):
        o = Op()
        r = _Rec()
        fn(r)
        assert r.calls
        o.eng, o.calls, o.dma, o.ndma, o.need, o.cnt, o.sem = eng, r.calls, False, 0, False, 0, None
        self._rec(o, list(reads), list(writes))
        return o

    def dma(self, queue, fn, key, n=1, reads=(), writes=(), store=False):
        o = Op()
        r = _Rec()
        fn(r, lambda ins: ins)
        n = len(r.calls)
        assert n >= 1
        o.eng, o.calls, o.dma, o.ndma, o.need = queue, r.calls, True, n, True
        c = self.dma_keys.setdefault(key, [0])
        c[0] += n
        o.cnt, o.sem = c[0], key
        self._rec(o, list(reads), list(writes))
        if store:
            self.store_ops.append(o)
        return o

    def emit(self):
        nc = self.nc
        for o in self.ops:
            for d in o.deps:
                if d.dma:
                    continue
                if o.dma or d.eng != o.eng or (SAME_ENG_SYNC and o.eng != "pe"):
                    d.need = True
        per = {e: [] for e in self.ENGS}
        for o in self.ops:
            per[o.eng].append(o)
        for e in self.ENGS:
            c = 0
            for o in per[e]:
                if not o.dma and o.need:
                    c += 1
                    o.cnt = c
        with ExitStack() as st:
            esem = {e: st.enter_context(nc.semaphore("s_" + e)) for e in self.ENGS}
            dsem = {k: st.enter_context(nc.semaphore("d_%d" % i)) for i, k in enumerate(self.dma_keys)}
            block = st.enter_context(nc.Block())
            engobj = {"pe": block.tensor, "act": block.scalar, "dve": block.vector,
                      "pool": block.gpsimd, "sp": block.sync}

            def run(ename):
                def body(eng):
                    waited = {}

                    def wait(sem, key, val):
                        if waited.get(key, 0) < val:
                            eng.wait_ge(sem, val)
                            waited[key] = val

                    for o in per[ename]:
                        need = {}
                        for d in o.deps:
                            if d.dma:
                                k = ("d", d.sem)
                                need[k] = max(need.get(k, 0), 16 * d.cnt)
                            elif d.eng != ename or o.dma or (SAME_ENG_SYNC and ename != "pe"):
                                k = ("e", d.eng)
                                need[k] = max(need.get(k, 0), d.cnt)
                        for k, v in need.items():
                            wait(dsem[k[1]] if k[0] == "d" else esem[k[1]], k, v)
                        ins = None
                        for (mname, a, k) in o.calls:
                            ins = getattr(eng, mname)(*a, **k)
                            if o.dma:
                                ins.then_inc(dsem[o.sem], 16)
                        if not o.dma and o.need:
                            ins.then_inc(esem[ename], 1)
                    if ename == "sp":
                        for key, c in self.dma_keys.items():
                            if any(s.sem == key for s in self.store_ops):
                                wait(dsem[key], ("d", key), 16 * c[0])
                return body

            for e in self.ENGS:
                engobj[e](run(e))


class Region:
    def __init__(self, S, arena, lo, hi):
        self.S, self.arena, self.lo, self.hi, self.cur = S, arena, lo, hi, lo

    def alloc(self, name, shape, dtype, nbuf=None):
        esz = 4 if dtype == F32 else 2
        n = 1
        for s in shape[1:]:
            n *= s
        nbytes = (n * esz + 3) // 4 * 4
        lo = self.cur
        self.cur += nbytes
        assert self.cur <= self.hi, (name, self.cur, self.hi)
        ap = self.arena[:, lo // 4:(lo + nbytes) // 4]
        if dtype == BF16:
            ap = ap.bitcast(BF16)
        ap = ap[:, 0:n]
        if len(shape) == 3:
            ap = ap.rearrange("p (a b) -> p a b", a=shape[1])
        elif len(shape) == 4:
            ap = ap.rearrange("p (a b c) -> p a b c", a=shape[1], b=shape[2])
        b = self.S.buf(name, "sb", lo, lo + nbytes)
        return b, ap


def build():
    nc = bass.Bass("TRN2", target_bir_lowering=False)
    dram = lambda n, s, dt=F32, k="ExternalInput": nc.dram_tensor(n, s, dt, kind=k).ap()
    x_d = dram("x", [NSEQ, SEQ, D])
    meta_d = dram("meta", [NM, D])
    vecs_d = dram("vecs", [128, NV])
    cst_d = dram("cst", [128, 288])
    rope_d = dram("rope", [32, 2, T])
    w_in_d = dram("w_in", [D, 1696])
    w_uq_d = dram("w_uq", [384, 768])
    w_kn_d = dram("w_kn", [256, 512])
    w_v_d = dram("w_v", [256, 512])
    lru_d = dram("lru_w", [2, 2, 8, 64, 64])
    w_out_d = dram("w_out", [D, D])
    wgu_d = dram("wgu", [NJF, 128, 2048])
    wd_d = dram("w_down", [DFF, D])
    out_d = dram("out", [NSEQ, SEQ, D], F32, "ExternalOutput")
    dbg_d = {}

    S = Sched(nc)
    K = 1024
    with ExitStack() as st:
        arena = st.enter_context(nc.sbuf_tensor("arena", [128, 212800 // 4], F32))
        banks = [st.enter_context(nc.psum_tensor("bank%d" % i, [128, 512], F32)) for i in range(8)]
        PB = [S.buf("bank%d" % i, "ps", i, i + 1) for i in range(8)]

        R0 = Region(S, arena, 0, 19 * K)
        b_vecs, vecs = R0.alloc("vecs", [128, NV], F32)
        b_ident, ident = R0.alloc("ident", [128, 128], BF16)
        b_ones, ones = R0.alloc("ones", [128, 128], BF16)
        b_pmat, pmat = R0.alloc("pmat", [128, 32], F32)
        b_rope, rope = R0.alloc("rope", [128, 2, T], F32)
        b_lamc, lamc = R0.alloc("lamc", [128, 16], F32)
        b_lamt, lamt = R0.alloc("lamt", [128, 8], F32)

        def V(c, p0=0, p1=128):
            return vecs[p0:p1, c:c + 1]

        RA = Region(S, arena, 19 * K, 51 * K)
        b_ornT, ornT = RA.alloc("ornT", [128, 4, SEQ], BF16)
        b_oatT, oatT = RA.alloc("oatT", [128, 4, SEQ], BF16)
        RL3 = Region(S, arena, 51 * K, 80 * K)
        b_cqnT, cqnT = RL3.alloc("cqnT", [128, 3, T], BF16)
        b_ckvnT, ckvnT = RL3.alloc("ckvnT", [128, 2, T], BF16)
        b_krT, krT = RL3.alloc("krT", [128, T], F32)

        S.dma("sp", lambda e, f: f(e.dma_start(out=vecs, in_=vecs_d[:, :])), "c_vecs", writes=[b_vecs])
        S.dma("pool", lambda e, f: f(e.dma_start(out=ident, in_=cst_d[:, 0:128])), "c_id", writes=[b_ident])
        S.dma("pool", lambda e, f: f(e.dma_start(out=ones, in_=cst_d[:, 128:256])), "c_on", writes=[b_ones])
        S.dma("sp", lambda e, f: f(e.dma_start(out=pmat, in_=cst_d[:, 256:288])), "c_pm", writes=[b_pmat])
        S.dma("sp", lambda e, f: f(e.dma_start(out=rope[64:96, :, :], in_=rope_d[:, :, :])), "c_rope", writes=[b_rope])
        S.op("act", lambda e: e.activation(out=lamt, in_=vecs[:, V_LAM:V_LAM + 8], func=AF.Exp, scale=-1.0),
             reads=[b_vecs], writes=[b_lamt])
        S.op("act", lambda e: e.activation(out=lamt, in_=lamt, func=AF.Ln, bias=V(V_ONE), scale=1.0),
             reads=[b_vecs, b_lamt], writes=[b_lamt])
        S.op("dve", lambda e: e.tensor_scalar(out=lamc[:, 0:8], in0=lamt, scalar1=-8.0, scalar2=None, op0=ALU.mult),
             reads=[b_lamt], writes=[b_lamc])
        S.op("dve", lambda e: e.tensor_scalar(out=lamc[:, 8:16], in0=lamt, scalar1=-16.0, scalar2=None, op0=ALU.mult),
             reads=[b_lamt], writes=[b_lamc])

        def dump(name, b, ap, shape, dt=F32):
            if not KDEBUG:
                return
            dd = dram("dbg_" + name, shape, dt, "ExternalOutput")
            dbg_d[name] = dd
            S.dma("sp", lambda e, f: f(e.dma_start(out=dd, in_=ap)), "dbg_" + name, reads=[b], store=True)

        def rstd_fm(ps_ap, rs_ap, np_, n, inv_n, b_ps, b_rs):
            S.op("act", lambda e: e.activation(out=rs_ap, in_=ps_ap, func=AF.Ln, bias=V(V_EPS, 0, np_), scale=inv_n),
                 reads=[b_ps, b_vecs], writes=[b_rs])
            S.op("act", lambda e: e.activation(out=rs_ap, in_=rs_ap, func=AF.Exp, scale=-0.5),
                 reads=[b_rs], writes=[b_rs])

        try:
            for s in range(NSEQ):
                R1a = Region(S, arena, 19 * K, 51 * K)
                b_xt, xt = [], []
                for i in range(2):
                    b, a = R1a.alloc("xt%d" % i, [128, 4, D], F32)
                    b_xt.append(b); xt.append(a)
                R1 = Region(S, arena, 80 * K, 158 * K)
                b_win, win = R1.alloc("w_in", [128, 8, 1696], BF16)
                b_xs, xs = R1.alloc("xs", [128, 4, D], BF16)
                b_xnT, xnT = [], []
                for i in range(2):
                    b, a = R1.alloc("xnT%d" % i, [128, 8, 512], BF16)
                    b_xnT.append(b); xnT.append(a)
                b_latf, latf = R1.alloc("latf", [128, 3, 512], F32)
                b_sqb, sqb = R1.alloc("sqb", [128, 3, 512], BF16)
                b_rs, rs = R1.alloc("rs", [128, 512], F32)
                b_gt1, gt1 = R1.alloc("gt1", [128, 512], F32)
                b_gt2, gt2 = R1.alloc("gt2", [128, 512], F32)
                b_ss, ss = R1.alloc("ss", [128, 4], F32)
                b_rstd, rstd = R1.alloc("rstd", [128, 4], F32)
                R2 = Region(S, arena, 158 * K, 207 * K + 800)
                b_xr, xr = [], []
                for cc in range(4):
                    b, a = R2.alloc("xr%d" % cc, [128, T + 4], F32)
                    b_xr.append(b); xr.append(a)
                b_gg, gg = R2.alloc("gg", [128, 4, SEQ], BF16)

                def ld_win(e, f):
                    for kc in range(8):
                        f(e.dma_start(out=win[:, kc, :], in_=w_in_d[kc * 128:(kc + 1) * 128, :]))
                S.dma("pool", ld_win, "w_in", n=8, writes=[b_win])
                for cc in range(4):
                    S.op("dve", lambda e, cc=cc: e.memset(xr[cc][:, 0:2], 0.0), writes=[b_xr[cc]])
                    S.op("dve", lambda e, cc=cc: e.memset(xr[cc][:, T + 2:T + 4], 0.0), writes=[b_xr[cc]])

                chunks = [(0, NM, None)] + [(NM + 512 * c, 512, c) for c in range(4)]

                def ld_x(ci):
                    pos0, n, c = chunks[ci]
                    sl = ci % 2
                    if c is None:
                        S.dma("sp", lambda e, f: f(e.dma_start(out=xt[sl][0:NM, 0, :], in_=meta_d[:, :])),
                              "xt%d" % sl, writes=[b_xt[sl]])
                    else:
                        src = x_d[s, 512 * c:512 * c + 512, :].rearrange("(j p) f -> p j f", p=128)
                        S.dma("sp", lambda e, f: f(e.dma_start(out=xt[sl], in_=src)), "xt%d" % sl, writes=[b_xt[sl]])

                def norm_tm(xin, b_xin, np_, nt, gcol, dstT, b_dstT, tpb):
                    n = 128 * nt if np_ == 128 else np_
                    for j in range(nt):
                        S.op("act", lambda e, j=j: e.activation(out=xs[0:np_, j, :], in_=xin[0:np_, j, :], func=AF.Square),
                             reads=[b_xin], writes=[b_xs])
                    S.op("dve", lambda e: e.tensor_reduce(out=ss[0:np_, 0:nt], in_=xs[0:np_, 0:nt, :], axis=AX.X, op=ALU.add),
                         reads=[b_xs], writes=[b_ss])
                    S.op("act", lambda e: e.activation(out=rstd[0:np_, 0:nt], in_=ss[0:np_, 0:nt], func=AF.Ln,
                                                       bias=V(V_EPS, 0, np_), scale=1.0 / D),
                         reads=[b_ss, b_vecs], writes=[b_rstd])
                    S.op("act", lambda e: e.activation(out=rstd[0:np_, 0:nt], in_=rstd[0:np_, 0:nt], func=AF.Exp, scale=-0.5),
                         reads=[b_rstd], writes=[b_rstd])
                    for j in range(nt):
                        S.op("dve", lambda e, j=j: e.tensor_scalar(out=xs[0:np_, j, :], in0=xin[0:np_, j, :],
                                                                   scalar1=rstd[0:np_, j:j + 1], scalar2=None, op0=ALU.mult),
                             reads=[b_xin, b_rstd], writes=[b_xs])
                    for kc in range(8):
                        bk = tpb[kc % 2]
                        tp = banks[bk][:, :].bitcast(BF16)

                        def tr(e, kc=kc, tp=tp):
                            ins = None
                            for j in range(nt):
                                w = np_
                                ins = e.transpose(out=tp[:, j * 128:j * 128 + w], in_=xs[0:np_, j, kc * 128:(kc + 1) * 128],
                                                  identity=ident[0:np_, 0:np_])
                            return ins
                        S.op("pe", tr, reads=[b_xs, b_ident], writes=[PB[bk]])
                        S.op("dve", lambda e, kc=kc, tp=tp: e.tensor_scalar(out=dstT[:, kc, 0:n], in0=tp[:, 0:n],
                                                                           scalar1=V(gcol + kc), scalar2=None, op0=ALU.mult),
                             reads=[PB[bk], b_vecs], writes=[b_dstT])

                def mm_acc(e, out_ap, lhs_list, rhs_list):
                    ins = None
                    n = len(lhs_list)
                    for i in range(n):
                        ins = e.matmul(out_ap, lhs_list[i], rhs_list[i], start=(i == 0), stop=(i == n - 1))
                    return ins

                mmb = [2, 3, 4, 5]
                mmi = [0]

                def next_bank():
                    b = mmb[mmi[0] % len(mmb)]
                    mmi[0] += 1
                    return b

                ld_x(0)
                for ci in range(5):
                    pos0, n, c = chunks[ci]
                    sl = ci % 2
                    if ci + 1 < 5:
                        ld_x(ci + 1)
                    np_ = 128 if c is not None else NM
                    nt = 4 if c is not None else 1
                    xT = xnT[sl]
                    bxT = b_xnT[sl]
                    if ci == 0:
                        phase_end("c0")
                    norm_tm(xt[sl], b_xt[sl], np_, nt, V_G1, xT, bxT, (0, 1))
                    phase_end("p1n%d" % ci)

                    def win_mm(col0, m, bk, p0=0):
                        S.op("pe", lambda e: mm_acc(e, banks[bk][p0:p0 + m, 0:n],
                                                    [win[:, kc, col0:col0 + m] for kc in range(8)],
                                                    [xT[:, kc, 0:n] for kc in range(8)]),
                             reads=[b_win, bxT], writes=[PB[bk]])

                    for (col0, ng, gcol, dst, b_dst, inv) in ((0, 3, V_GQA, cqnT, b_cqnT, 1.0 / 384),
                                                               (384, 2, V_GKVA, ckvnT, b_ckvnT, 1.0 / 256)):
                        for g in range(ng):
                            bk = next_bank()
                            win_mm(col0 + 128 * g, 128, bk)
                            S.op("act", lambda e, g=g, bk=bk: e.activation(out=sqb[:, g, 0:n], in_=banks[bk][:, 0:n], func=AF.Square),
                                 reads=[PB[bk]], writes=[b_sqb])
                            S.op("dve", lambda e, g=g, bk=bk: e.tensor_copy(out=latf[:, g, 0:n], in_=banks[bk][:, 0:n]),
                                 reads=[PB[bk]], writes=[b_latf])
                        S.op("pe", lambda e, ng=ng: mm_acc(e, banks[6][:, 0:n], [ones[:, :]] * ng,
                                                           [sqb[:, g, 0:n] for g in range(ng)]),
                             reads=[b_ones, b_sqb], writes=[PB[6]])
                        rstd_fm(banks[6][:, 0:n], rs[:, 0:n], 128, n, inv, PB[6], b_rs)
                        for g in range(ng):
                            S.op("dve", lambda e, g=g, dst=dst, gcol=gcol: e.scalar_tensor_tensor(
                                out=dst[:, g, pos0:pos0 + n], in0=latf[:, g, 0:n], scalar=V(gcol + g), in1=rs[:, 0:n],
                                op0=ALU.mult, op1=ALU.mult), reads=[b_latf, b_rs, b_vecs], writes=[b_dst])
                    phase_end("p1l%d" % ci)
                    bk = next_bank()
                    win_mm(640, 32, bk, p0=64)
                    S.op("act", lambda e, bk=bk: e.activation(out=krT[64:96, pos0:pos0 + n], in_=banks[bk][64:96, 0:n], func=AF.Copy),
                         reads=[PB[bk]], writes=[b_krT])
                    phase_end("p1k%d" % ci)
                    for cc in range(4):
                        bk = next_bank()
                        win_mm(672 + 128 * cc, 128, bk)
                        S.op("act", lambda e, cc=cc, bk=bk: e.activation(out=xr[cc][:, 2 + pos0:2 + pos0 + n], in_=banks[bk][:, 0:n], func=AF.Copy),
                             reads=[PB[bk]], writes=[b_xr[cc]])
                    if c is not None:
                        for cc in range(4):
                            bk = next_bank()
                            win_mm(1184 + 128 * cc, 128, bk)
                            S.op("act", lambda e, bk=bk: e.activation(out=gt1, in_=banks[bk][:, :], func=AF.Square),
                                 reads=[PB[bk]], writes=[b_gt1])
                            S.op("dve", lambda e: e.tensor_scalar(out=gt1, in0=gt1, scalar1=0.044715, scalar2=1.0,
                                                                  op0=ALU.mult, op1=ALU.add), reads=[b_gt1], writes=[b_gt1])
                            S.op("dve", lambda e, bk=bk: e.tensor_tensor(out=gt1, in0=banks[bk][:, :], in1=gt1, op=ALU.mult),
                                 reads=[PB[bk], b_gt1], writes=[b_gt1])
                            S.op("act", lambda e: e.activation(out=gt2, in_=gt1, func=AF.Sigmoid, scale=1.5957691216057308),
                                 reads=[b_gt1], writes=[b_gt2])
                            S.op("dve", lambda e, cc=cc, bk=bk, c=c: e.tensor_tensor(out=gg[:, cc, 512 * c:512 * c + 512],
                                                                                 in0=banks[bk][:, :], in1=gt2, op=ALU.mult),
                                 reads=[PB[bk], b_gt2], writes=[b_gg])
                if s == 0:
                    dump("cqnT", b_cqnT, cqnT, [128, 3, T], BF16)
                    dump("ckvnT", b_ckvnT, ckvnT, [128, 2, T], BF16)
                    dump("krT", b_krT, krT[64:96, :], [32, T])
                    dump("xr0", b_xr[0], xr[0], [128, T + 4])
                    dump("gg", b_gg, gg, [128, 4, SEQ], BF16)
                phase_end("p1")

                R4 = Region(S, arena, 80 * K, 158 * K)
                b_lw, lw = R4.alloc("lru_w", [128, 16, 128], BF16)
                b_xc, xc = R4.alloc("xc", [128, T], F32)
                b_xcb, xcb = R4.alloc("xcb", [128, T], BF16)
                lb = {}
                for nm in ("r", "i", "a", "b", "hf", "hb"):
                    lb[nm] = R4.alloc("l_" + nm, [128, T], F32)
                b_ctmp, ctmp = R4.alloc("ctmp", [128, T], F32)
                R4s = Region(S, arena, R4.cur - T * 4, R4.cur)
                b_sq4, sq4 = R4s.alloc("sq4", [128, 4, 512], BF16)
                b_rsr, rsr = R4s.alloc("rsr", [128, 512], F32)
                S.op("dve", lambda e: e.memset(lw, 0.0), writes=[b_lw])

                def ld_lru(e, f):
                    for g in range(2):
                        for d in range(2):
                            src = lru_d[g, d].rearrange("(c b) i j -> b i c j", b=2)
                            k0 = (g * 2 + d) * 4
                            for bh in range(2):
                                f(e.dma_start(out=lw[bh * 64:(bh + 1) * 64, k0:k0 + 4, bh * 64:(bh + 1) * 64], in_=src[bh]))
                S.dma("pool", ld_lru, "lru_w", n=8, writes=[b_lw])

                pieces = [(512 * p, 512) for p in range(4)] + [(2048, 16)]
                for cc in range(4):
                    S.op("dve", lambda e, cc=cc: e.tensor_scalar(out=xc, in0=xr[cc][:, 0:T], scalar1=V(V_CW + cc * 4),
                                                                 scalar2=V(V_CB + cc), op0=ALU.mult, op1=ALU.add),
                         reads=[b_xr[cc], b_vecs], writes=[b_xc])
                    for j in range(1, 4):
                        S.op("dve", lambda e, cc=cc, j=j: e.scalar_tensor_tensor(out=xc, in0=xr[cc][:, j:j + T], scalar=V(V_CW + cc * 4 + j),
                                                                                in1=xc, op0=ALU.mult, op1=ALU.add),
                             reads=[b_xr[cc], b_vecs, b_xc], writes=[b_xc])
                    S.op("act", lambda e: e.activation(out=xcb, in_=xc, func=AF.Copy), reads=[b_xc], writes=[b_xcb])
                    for d in range(2):
                        b_r, r_ = lb["r"]; b_i, i_ = lb["i"]; b_a, a_ = lb["a"]; b_b, bb_ = lb["b"]
                        b_h, h_ = lb["hf"] if d == 0 else lb["hb"]
                        for (p0, pn) in pieces:
                            for g, (bdst, dst, bcol) in enumerate(((b_r, r_, V_BA), (b_i, i_, V_BI))):
                                bk = next_bank()
                                S.op("pe", lambda e, g=g, bk=bk, p0=p0, pn=pn, d=d, cc=cc: e.matmul(
                                    banks[bk][:, 0:pn], lw[:, (g * 2 + d) * 4 + cc, :], xcb[:, p0:p0 + pn], start=True, stop=True),
                                    reads=[b_lw, b_xcb], writes=[PB[bk]])
                                S.op("act", lambda e, bk=bk, p0=p0, pn=pn, dst=dst, bcol=bcol, d=d, cc=cc: e.activation(
                                    out=dst[:, p0:p0 + pn], in_=banks[bk][:, 0:pn], func=AF.Sigmoid,
                                    bias=V(bcol + d * 4 + cc), scale=1.0), reads=[PB[bk], b_vecs], writes=[bdst])
                        ci_ = d * 4 + cc
                        S.op("act", lambda e, ci_=ci_: e.activation(out=a_, in_=r_, func=AF.Exp, scale=lamc[:, ci_:ci_ + 1]),
                             reads=[b_r, b_lamc], writes=[b_a])
                        S.op("act", lambda e, ci_=ci_: e.activation(out=r_, in_=r_, func=AF.Exp, scale=lamc[:, 8 + ci_:9 + ci_]),
                             reads=[b_r, b_lamc], writes=[b_r])
                        S.op("act", lambda e: e.activation(out=r_, in_=r_, func=AF.Sqrt, bias=V(V_ONE), scale=-1.0),
                             reads=[b_r, b_vecs], writes=[b_r])
                        S.op("pool", lambda e: e.tensor_tensor(out=bb_, in0=i_, in1=xc, op=ALU.mult),
                             reads=[b_i, b_xc], writes=[b_b])
                        S.op("dve", lambda e: e.tensor_tensor(out=bb_, in0=bb_, in1=r_, op=ALU.mult),
                             reads=[b_b, b_r], writes=[b_b])
                        if d == 0:
                            S.op("dve", lambda e, h_=h_: e.tensor_tensor_scan(out=h_, data0=a_, data1=bb_, initial=0.0,
                                                                             op0=ALU.mult, op1=ALU.add),
                                 reads=[b_a, b_b], writes=[b_h])
                        else:
                            S.op("dve", lambda e, h_=h_: e.tensor_tensor_scan(out=h_[:, ::-1], data0=a_[:, ::-1], data1=bb_[:, ::-1],
                                                                             initial=0.0, op0=ALU.mult, op1=ALU.add),
                                 reads=[b_a, b_b], writes=[b_h])
                    b_hf, hf = lb["hf"]; b_hb, hb = lb["hb"]
                    S.op("pool", lambda e: e.tensor_tensor(out=hf[:, NM:T], in0=hf[:, NM:T], in1=hb[:, NM:T], op=ALU.add),
                         reads=[b_hf, b_hb], writes=[b_hf])
                    S.op("dve", lambda e, cc=cc: e.tensor_tensor(out=xr[cc][:, 2 + NM:2 + T], in0=hf[:, NM:T], in1=gg[:, cc, :], op=ALU.mult),
                         reads=[b_hf, b_gg], writes=[b_xr[cc]])
                for c in range(4):
                    c0 = 2 + NM + 512 * c
                    for cc in range(4):
                        S.op("act", lambda e, cc=cc, c0=c0: e.activation(out=sq4[:, cc, :], in_=xr[cc][:, c0:c0 + 512], func=AF.Square),
                             reads=[b_xr[cc]], writes=[b_sq4])
                    S.op("pe", lambda e: mm_acc(e, banks[6][:, :], [ones[:, :]] * 4, [sq4[:, cc, :] for cc in range(4)]),
                         reads=[b_ones, b_sq4], writes=[PB[6]])
                    rstd_fm(banks[6][:, :], rsr, 128, 512, 1.0 / 512, PB[6], b_rsr)
                    for cc in range(4):
                        S.op("dve", lambda e, cc=cc, c0=c0, c=c: e.scalar_tensor_tensor(
                            out=ornT[:, cc, 512 * c:512 * c + 512], in0=xr[cc][:, c0:c0 + 512], scalar=V(V_GR + cc), in1=rsr,
                            op0=ALU.mult, op1=ALU.mult), reads=[b_xr[cc], b_rsr, b_vecs], writes=[b_ornT])
                if s == 0:
                    dump("ornT", b_ornT, ornT, [128, 4, SEQ], BF16)
                phase_end("p2")

                R6 = Region(S, arena, 80 * K, 207 * K + 800)
                b_wkn, wkn = R6.alloc("w_kn", [128, 2, 512], BF16)
                b_wv, wv = R6.alloc("w_v", [128, 2, 512], BF16)
                b_wuq, wuq = R6.alloc("w_uq", [128, 3, 768], BF16)
                b_KT, KT = [], []
                for h in range(NH):
                    b, a = R6.alloc("KT%d" % h, [128, T], BF16)
                    b_KT.append(b); KT.append(a)
                b_va, va = R6.alloc("vaug", [128, 17, NH, 128], BF16)
                b_QT, QT = [], []
                for i in range(2):
                    b, a = R6.alloc("QT%d" % i, [128, NH, 512], BF16)
                    b_QT.append(b); QT.append(a)
                b_PT, PT = [], []
                for i in range(4):
                    b, a = R6.alloc("PT%d" % i, [128, 512], BF16)
                    b_PT.append(b); PT.append(a)
                b_oraw, oraw = [], []
                for i in range(4):
                    b, a = R6.alloc("oraw%d" % i, [128, 512], F32)
                    b_oraw.append(b); oraw.append(a)
                b_sqk, sqk = [], []
                for i in range(2):
                    b, a = R6.alloc("sqk%d" % i, [128, 512], BF16)
                    b_sqk.append(b); sqk.append(a)
                b_rden, rden = [], []
                for i in range(2):
                    b, a = R6.alloc("rden%d" % i, [128, 512], F32)
                    b_rden.append(b); rden.append(a)
                b_xk, xk = R6.alloc("xk", [128, 512], F32)
                b_t1, t1 = R6.alloc("t1", [128, 512], F32)
                b_t2, t2 = R6.alloc("t2", [128, 512], F32)
                b_krp, krp = R6.alloc("krp", [128, 512], F32)
                b_rec, rec = R6.alloc("rec", [128, 512], F32)
                b_sqa, sqa = R6.alloc("sqa", [128, 4, 512], BF16)
                b_rsa, rsa = R6.alloc("rsa", [128, 512], F32)

                def ld_kvw(e, f):
                    for kc in range(2):
                        f(e.dma_start(out=wkn[:, kc, :], in_=w_kn_d[kc * 128:(kc + 1) * 128, :]))
                        f(e.dma_start(out=wv[:, kc, :], in_=w_v_d[kc * 128:(kc + 1) * 128, :]))
                S.dma("pool", ld_kvw, "w_kv", n=4, writes=[b_wkn, b_wv])

                def ld_uq(e, f):
                    for kc in range(3):
                        f(e.dma_start(out=wuq[:, kc, :], in_=w_uq_d[kc * 128:(kc + 1) * 128, :]))
                S.dma("pool", ld_uq, "w_uq", n=3, writes=[b_wuq])
                S.op("pool", lambda e: e.memset(va, 1.0), writes=[b_va])

                def rope_rows(src_ps_or_sb, b_src, gcol, cols0, n, dst, b_dst, rd, b_rd, bk_px):
                    S.op("dve", lambda e: e.tensor_scalar(out=xk[64:96, 0:n], in0=src_ps_or_sb, scalar1=V(gcol, 64, 96),
                                                          scalar2=None, op0=ALU.mult), reads=[b_src, b_vecs], writes=[b_xk])
                    S.op("pe", lambda e: e.matmul(banks[bk_px][64:96, 0:n], pmat[64:96, 0:32], xk[64:96, 0:n], start=True, stop=True),
                         reads=[b_pmat, b_xk], writes=[PB[bk_px]])
                    S.op("dve", lambda e: e.tensor_tensor(out=t1[64:96, 0:n], in0=xk[64:96, 0:n], in1=rope[64:96, 0, cols0:cols0 + n], op=ALU.mult),
                         reads=[b_xk, b_rope], writes=[b_t1])
                    S.op("dve", lambda e: e.tensor_tensor(out=t2[64:96, 0:n], in0=banks[bk_px][64:96, 0:n], in1=rope[64:96, 1, cols0:cols0 + n], op=ALU.mult),
                         reads=[PB[bk_px], b_rope], writes=[b_t2])
                    S.op("dve", lambda e: e.tensor_tensor(out=t1[64:96, 0:n], in0=t1[64:96, 0:n], in1=t2[64:96, 0:n], op=ALU.add),
                         reads=[b_t1, b_t2], writes=[b_t1])
                    if rd is None:
                        S.op("dve", lambda e: e.tensor_copy(out=dst, in_=t1[64:96, 0:n]), reads=[b_t1], writes=[b_dst])
                    else:
                        S.op("dve", lambda e: e.tensor_tensor(out=dst, in0=t1[64:96, 0:n], in1=rd, op=ALU.mult),
                             reads=[b_t1, b_rd], writes=[b_dst])

                for ci in range(5):
                    pos0, n, c = chunks[ci]
                    rope_rows(krT[64:96, pos0:pos0 + n], b_krT, V_KG, pos0, n, krp[64:96, 0:n], b_krp, None, None, 7)
                    for i in range(2):
                        S.op("act", lambda e, i=i: e.activation(out=sqk[i][64:96, 0:n], in_=krT[64:96, pos0:pos0 + n], func=AF.Square),
                             reads=[b_krT], writes=[b_sqk[i]])
                    for h in range(NH):
                        bk = next_bank()
                        sl = h % 2
                        S.op("pe", lambda e, h=h, bk=bk: mm_acc(e, banks[bk][0:64, 0:n],
                                                               [wkn[:, kc, h * 64:(h + 1) * 64] for kc in range(2)],
                                                               [ckvnT[:, kc, pos0:pos0 + n] for kc in range(2)]),
                             reads=[b_wkn, b_ckvnT], writes=[PB[bk]])
                        S.op("act", lambda e, bk=bk, sl=sl: e.activation(out=sqk[sl][0:64, 0:n], in_=banks[bk][0:64, 0:n], func=AF.Square),
                             reads=[PB[bk]], writes=[b_sqk[sl]])
                        S.op("pe", lambda e, sl=sl: e.matmul(banks[6][0:96, 0:n], ones[0:96, 0:96], sqk[sl][0:96, 0:n], start=True, stop=True),
                             reads=[b_ones, b_sqk[sl]], writes=[PB[6]])
                        rstd_fm(banks[6][0:96, 0:n], rden[sl][0:96, 0:n], 96, n, 1.0 / 96, PB[6], b_rden[sl])
                        S.op("dve", lambda e, h=h, bk=bk, sl=sl: e.scalar_tensor_tensor(
                            out=KT[h][0:64, pos0:pos0 + n], in0=banks[bk][0:64, 0:n], scalar=V(V_KG, 0, 64), in1=rden[sl][0:64, 0:n],
                            op0=ALU.mult, op1=ALU.mult), reads=[PB[bk], b_rden[sl], b_vecs], writes=[b_KT[h]])
                        S.op("dve", lambda e, h=h, sl=sl: e.tensor_tensor(out=KT[h][64:96, pos0:pos0 + n], in0=krp[64:96, 0:n],
                                                                         in1=rden[sl][64:96, 0:n], op=ALU.mult),
                             reads=[b_krp, b_rden[sl]], writes=[b_KT[h]])
                    ntile = 1 if c is None else 4
                    for j in range(ntile):
                        kt = 0 if c is None else 1 + 4 * c + j
                        npk = NM if c is None else 128
                        bk = next_bank()
                        S.op("pe", lambda e, j=j, bk=bk, npk=npk: mm_acc(e, banks[bk][0:npk, :],
                                                                        [ckvnT[:, kc, pos0 + 128 * j:pos0 + 128 * j + npk] for kc in range(2)],
                                                                        [wv[:, kc, :] for kc in range(2)]),
                             reads=[b_ckvnT, b_wv], writes=[PB[bk]])
                        for par in range(2):
                            src = banks[bk][0:npk, :].rearrange("p (a b d) -> p a b d", a=4, b=2)[:, :, par, :]
                            S.op("act", lambda e, kt=kt, par=par, src=src, npk=npk: e.activation(
                                out=va[0:npk, kt, par::2, par * 64:par * 64 + 64], in_=src, func=AF.Copy),
                                reads=[PB[bk]], writes=[b_va])
                if s == 0:
                    dump("KT0", b_KT[0], KT[0][0:96, :], [96, T], BF16)
                    dump("KT3", b_KT[3], KT[3][0:96, :], [96, T], BF16)
                    dump("vaug", b_va, va, [128, 17, NH, 128], BF16)
                phase_end("kv")

                def qprep_stages(c, h):
                    sl = c % 2
                    pos0 = NM + 512 * c
                    bk = 4
                    sk = h % 2
                    n = 512

                    def st0():
                        S.op("pe", lambda e: mm_acc(e, banks[bk][0:96, :],
                                                    [wuq[:, kc, h * 96:(h + 1) * 96] for kc in range(3)],
                                                    [cqnT[:, kc, pos0:pos0 + 512] for kc in range(3)]),
                             reads=[b_wuq, b_cqnT], writes=[PB[bk]])

                    def st1():
                        S.op("act", lambda e: e.activation(out=sqk[sk][0:96, :], in_=banks[bk][0:96, :], func=AF.Square),
                             reads=[PB[bk]], writes=[b_sqk[sk]])

                    def st2():
                        S.op("pe", lambda e: e.matmul(banks[6][0:96, :], ones[0:96, 0:96], sqk[sk][0:96, :], start=True, stop=True),
                             reads=[b_ones, b_sqk[sk]], writes=[PB[6]])

                    def st3():
                        rstd_fm(banks[6][0:96, :], rden[sk][0:96, :], 96, 512, 1.0 / 96, PB[6], b_rden[sk])

                    def st4():
                        S.op("dve", lambda e: e.scalar_tensor_tensor(
                            out=QT[sl][0:64, h, :], in0=banks[bk][0:64, :], scalar=V(V_QG, 0, 64), in1=rden[sk][0:64, :],
                            op0=ALU.mult, op1=ALU.mult), reads=[PB[bk], b_rden[sk], b_vecs], writes=[b_QT[sl]])
                        S.op("dve", lambda e: e.tensor_scalar(out=xk[64:96, 0:n], in0=banks[bk][64:96, :], scalar1=V(V_QG, 64, 96),
                                                              scalar2=None, op0=ALU.mult), reads=[PB[bk], b_vecs], writes=[b_xk])

                    def st5():
                        S.op("pe", lambda e: e.matmul(banks[7][64:96, 0:n], pmat[64:96, 0:32], xk[64:96, 0:n], start=True, stop=True),
                             reads=[b_pmat, b_xk], writes=[PB[7]])

                    def st6():
                        S.op("dve", lambda e: e.tensor_tensor(out=t1[64:96, 0:n], in0=xk[64:96, 0:n], in1=rope[64:96, 0, pos0:pos0 + n], op=ALU.mult),
                             reads=[b_xk, b_rope], writes=[b_t1])
                        S.op("dve", lambda e: e.tensor_tensor(out=t2[64:96, 0:n], in0=banks[7][64:96, 0:n], in1=rope[64:96, 1, pos0:pos0 + n], op=ALU.mult),
                             reads=[PB[7], b_rope], writes=[b_t2])
                        S.op("dve", lambda e: e.tensor_tensor(out=t1[64:96, 0:n], in0=t1[64:96, 0:n], in1=t2[64:96, 0:n], op=ALU.add),
                             reads=[b_t1, b_t2], writes=[b_t1])
                        S.op("dve", lambda e: e.tensor_tensor(out=QT[sl][64:96, h, :], in0=t1[64:96, 0:n], in1=rden[sk][64:96, :], op=ALU.mult),
                             reads=[b_t1, b_rden[sk]], writes=[b_QT[sl]])
                    return [st0, st1, st2, st3, st4, st5, st6]

                def qprep_head(c, h):
                    for st in qprep_stages(c, h):
                        st()

                ktiles = [(0, NM)] + [(NM + 128 * i, 128) for i in range(16)]
                xcb_junk = va[:, 1, 0:4, :].rearrange('p a b -> p (a b)')
                sbk = [0, 1]
                JUNK = int(os.environ.get('KJUNK', '192'))
                obk = [2, 3]
                scale = 96.0 ** -0.5
                LAG = int(os.environ.get('KLAG', '2'))
                INTER = os.environ.get('KINTER', '1') == '1'

                pend = []

                def attention(c, inter=None):
                    sl = c % 2
                    steps = [(h, kt) for h in range(NH) for kt in range(17)]
                    nst = len(steps)

                    def emit_S(i):
                        h, kt = steps[i]
                        if inter is not None:
                            inter(h, kt)
                        if h == 0 and kt in (2, 4, 6) and pend:
                            pend.pop(0)()
                        k0, nk = ktiles[kt]
                        sb_ = sbk[i % 2]
                        pt = i % 4
                        S.op("pe", lambda e: e.matmul(banks[sb_][0:nk, :], KT[h][0:96, k0:k0 + nk], QT[sl][0:96, h, :],
                                                      start=True, stop=True),
                             reads=[b_KT[h], b_QT[sl]], writes=[PB[sb_]])
                        S.op("act", lambda e: e.activation(out=PT[pt][0:nk, :], in_=banks[sb_][0:nk, :], func=AF.Exp, scale=scale),
                             reads=[PB[sb_]], writes=[b_PT[pt]])

                    def emit_PV(i):
                        h, kt = steps[i]
                        k0, nk = ktiles[kt]
                        pt = i % 4
                        ob = obk[h % 2]
                        par = h % 2
                        S.op("pe", lambda e: e.matmul(banks[ob][:, :], va[0:nk, kt, h, :], PT[pt][0:nk, :],
                                                      start=(kt == 0), stop=(kt == 16)),
                             reads=[b_va, b_PT[pt]], writes=[PB[ob]])
                        if kt == 16:
                            own = slice(0, 64) if par == 0 else slice(64, 128)
                            oth = slice(64, 128) if par == 0 else slice(0, 64)
                            pr = h // 2
                            S.op("dve", lambda e: e.reciprocal(out=rec[own, :], in_=banks[ob][oth, :]),
                                 reads=[PB[ob]], writes=[b_rec])
                            S.op("dve", lambda e: e.tensor_tensor(out=oraw[pr][own, :], in0=banks[ob][own, :],
                                                                  in1=rec[own, :], op=ALU.mult),
                                 reads=[PB[ob], b_rec], writes=[b_oraw[pr]])

                    for i in range(nst + LAG):
                        if i < nst:
                            emit_S(i)
                        if JUNK:
                            S.op("pe", lambda e: e.matmul(banks[5][:, 0:JUNK], ones[:, :], xcb_junk[:, 0:JUNK], start=True, stop=True),
                                 reads=[b_ones], writes=[PB[5]])
                        if i >= LAG:
                            emit_PV(i - LAG)
                    def t0():
                        for pr in range(4):
                            S.op("act", lambda e, pr=pr: e.activation(out=sqa[:, pr, :], in_=oraw[pr], func=AF.Square),
                                 reads=[b_oraw[pr]], writes=[b_sqa])

                    def t1():
                        S.op("pe", lambda e: mm_acc(e, banks[6][:, :], [ones[:, :]] * 4, [sqa[:, pr, :] for pr in range(4)]),
                             reads=[b_ones, b_sqa], writes=[PB[6]])

                    def t2():
                        rstd_fm(banks[6][:, :], rsa, 128, 512, 1.0 / 512, PB[6], b_rsa)

                    def t3():
                        for pr in range(4):
                            S.op("dve", lambda e, pr=pr: e.scalar_tensor_tensor(
                                out=oatT[:, pr, 512 * c:512 * c + 512], in0=oraw[pr], scalar=V(V_GA + pr), in1=rsa,
                                op0=ALU.mult, op1=ALU.mult), reads=[b_oraw[pr], b_rsa, b_vecs], writes=[b_oatT])
                    return [t0, lambda: (t1(), t2()), t3]

                mmb[:] = [4]
                for h in range(NH):
                    qprep_head(0, h)
                for c in range(4):
                    if INTER:
                        if c + 1 < 4:
                            stg = {h: qprep_stages(c + 1, h) for h in range(NH)}
                            tl = attention(c, lambda h, kt, stg=stg: stg[h][(kt - 1) // 2]() if (kt % 2 == 1 and kt < 15) else None)
                        else:
                            tl = attention(c, None)
                        pend.extend(tl)
                        if c == 3:
                            while pend:
                                pend.pop(0)()
                    else:
                        if c + 1 < 4:
                            for h in range(NH):
                                qprep_head(c + 1, h)
                        for t_ in attention(c, None):
                            t_()
                mmb[:] = [2, 3, 4, 5]
                if s == 0:
                    dump("oatT", b_oatT, oatT, [128, 4, SEQ], BF16)
                phase_end("att")

                R8 = Region(S, arena, 51 * K, 207 * K + 800)
                b_wo, wo = R8.alloc("w_out", [128, 8, D], BF16)
                b_wd, wd = R8.alloc("w_down", [128, NJF, D], BF16)
                b_ring, ring = [], []
                for i in range(RING):
                    b, a = R8.alloc("ring%d" % i, [128, 2, 8, 128], BF16)
                    b_ring.append(b); ring.append(a)
                b_aT, aT = R8.alloc("aT", [128, NJF, 512], BF16)
                b_hnT, hnT = R8.alloc("hnT", [128, 8, 512], BF16)
                b_xh, xh = R8.alloc("xh", [128, 4, D], F32)
                b_xs5, xs5 = R8.alloc("xs5", [128, 4, D], BF16)
                b_sg, sg = [], []
                for i in range(2):
                    b, a = R8.alloc("sg%d" % i, [128, 512], F32)
                    b_sg.append(b); sg.append(a)
                b_ss5, ss5 = R8.alloc("ss5", [128, 4], F32)
                b_rstd5, rstd5 = R8.alloc("rstd5", [128, 4], F32)
                b_xs, xs, b_ss, ss, b_rstd, rstd = b_xs5, xs5, b_ss5, ss5, b_rstd5, rstd5

                def ld_wo(e, f):
                    for kc in range(8):
                        f(e.dma_start(out=wo[:, kc, :], in_=w_out_d[kc * 128:(kc + 1) * 128, :]))
                S.dma("pool", ld_wo, "w_out", n=8, writes=[b_wo])

                def ld_wd(e, f):
                    for jf in range(NJF):
                        f(e.dma_start(out=wd[:, jf, :], in_=wd_d[jf * 128:(jf + 1) * 128, :]))
                S.dma("pool", ld_wd, "w_down", n=NJF, writes=[b_wd])

                ring_i = [0]

                def ld_ring(jf):
                    sl = ring_i[0] % RING
                    ring_i[0] += 1
                    dst = ring[sl].rearrange("p a b c -> p (a b c)")
                    S.dma("pool", lambda e, f: f(e.dma_start(out=dst, in_=wgu_d[jf])), "ring%d" % sl, writes=[b_ring[sl]])
                    return sl

                PRE = RING - 1
                for c in range(4):
                    src = x_d[s, 512 * c:512 * c + 512, :].rearrange("(j p) f -> p j f", p=128)
                    S.dma("sp", lambda e, f, src=src: f(e.dma_start(out=xh, in_=src)), "xh", writes=[b_xh])
                    slots = {}
                    for jf in range(PRE):
                        slots[jf] = ld_ring(jf)
                    for j in range(4):
                        for half in range(2):
                            bk = half
                            lhs = [oatT[:, kc, 512 * c + 128 * j:512 * c + 128 * j + 128] for kc in range(4)] + \
                                  [ornT[:, kc, 512 * c + 128 * j:512 * c + 128 * j + 128] for kc in range(4)]
                            rhs = [wo[:, kc, half * 512:(half + 1) * 512] for kc in range(8)]
                            S.op("pe", lambda e, bk=bk, lhs=lhs, rhs=rhs: mm_acc(e, banks[bk][:, :], lhs, rhs),
                                 reads=[b_oatT, b_ornT, b_wo], writes=[PB[bk]])
                            S.op("dve", lambda e, bk=bk, j=j, half=half: e.tensor_tensor(
                                out=xh[:, j, half * 512:(half + 1) * 512], in0=banks[bk][:, :], in1=xh[:, j, half * 512:(half + 1) * 512],
                                op=ALU.add), reads=[PB[bk], b_xh], writes=[b_xh])
                    if s == 0 and c == 0:
                        dump("h0", b_xh, xh, [128, 4, D])
                    norm_tm(xh, b_xh, 128, 4, V_G2, hnT, b_hnT, (2, 3))
                    for jf in range(NJF):
                        if jf + PRE < NJF:
                            slots[jf + PRE] = ld_ring(jf + PRE)
                        sl = slots[jf]
                        gb = 4 + jf % 2
                        ub = 6 + jf % 2
                        S.op("pe", lambda e, sl=sl, gb=gb: mm_acc(e, banks[gb][:, :], [ring[sl][:, 0, kc, :] for kc in range(8)],
                                                                 [hnT[:, kc, :] for kc in range(8)]),
                             reads=[b_ring[sl], b_hnT], writes=[PB[gb]])
                        S.op("pe", lambda e, sl=sl, ub=ub: mm_acc(e, banks[ub][:, :], [ring[sl][:, 1, kc, :] for kc in range(8)],
                                                                 [hnT[:, kc, :] for kc in range(8)]),
                             reads=[b_ring[sl], b_hnT], writes=[PB[ub]])
                        S.op("act", lambda e, gb=gb, jf=jf: e.activation(out=sg[jf % 2], in_=banks[gb][:, :], func=AF.Silu),
                             reads=[PB[gb]], writes=[b_sg[jf % 2]])
                        S.op("dve", lambda e, ub=ub, jf=jf: e.tensor_tensor(out=aT[:, jf, :], in0=banks[ub][:, :], in1=sg[jf % 2], op=ALU.mult),
                             reads=[PB[ub], b_sg[jf % 2]], writes=[b_aT])
                    for j in range(4):
                        for half in range(2):
                            bk = half
                            S.op("pe", lambda e, bk=bk, j=j, half=half: mm_acc(
                                e, banks[bk][:, :], [aT[:, jf, 128 * j:128 * j + 128] for jf in range(NJF)],
                                [wd[:, jf, half * 512:(half + 1) * 512] for jf in range(NJF)]),
                                reads=[b_aT, b_wd], writes=[PB[bk]])
                            S.op("dve", lambda e, bk=bk, j=j, half=half: e.tensor_tensor(
                                out=xh[:, j, half * 512:(half + 1) * 512], in0=banks[bk][:, :], in1=xh[:, j, half * 512:(half + 1) * 512],
                                op=ALU.add), reads=[PB[bk], b_xh], writes=[b_xh])
                    dst = out_d[s, 512 * c:512 * c + 512, :].rearrange("(j p) f -> p j f", p=128)
                    S.dma("sp", lambda e, f, dst=dst: f(e.dma_start(out=dst, in_=xh)), "xh_st", reads=[b_xh], store=True)
        except _Stop:
            pass

        S.emit()
    return nc, dbg_d


def _host_layout(inp):
    f = lambda a: np.ascontiguousarray(np.asarray(a, dtype=np.float32))
    vecs = np.zeros((128, NV), np.float32)
    col = lambda v, n: f(v).reshape(n, 128).T
    vecs[:, V_G1:V_G1 + 8] = col(inp["ln1_g"][0], 8)
    vecs[:, V_GQA:V_GQA + 3] = col(inp["q_a_norm_g"][0], 3)
    vecs[:, V_GKVA:V_GKVA + 2] = col(inp["kv_a_norm_g"][0], 2)
    vecs[0:96, V_QG] = f(inp["q_norm_g"][0])
    vecs[0:96, V_KG] = f(inp["k_norm_g"][0])
    cw = f(inp["conv_w"][0])
    for cc in range(4):
        for j in range(4):
            vecs[:, V_CW + cc * 4 + j] = cw[j, cc * 128:(cc + 1) * 128]
    vecs[:, V_CB:V_CB + 4] = col(inp["conv_b"][0], 4)
    for d in range(2):
        vecs[:, V_BA + d * 4:V_BA + d * 4 + 4] = col(inp["lru_ba"][0, d], 4)
        vecs[:, V_BI + d * 4:V_BI + d * 4 + 4] = col(inp["lru_bi"][0, d], 4)
        vecs[:, V_LAM + d * 4:V_LAM + d * 4 + 4] = col(inp["lru_lambda"][0, d], 4)
    vecs[:, V_GA:V_GA + 4] = col(inp["attn_out_g"][0], 4)
    vecs[:, V_GR:V_GR + 4] = col(inp["rnn_out_g"][0], 4)
    vecs[:, V_G2:V_G2 + 8] = col(inp["ln2_g"][0], 8)
    vecs[:, V_EPS] = EPS
    vecs[:, V_ONE] = 1.0
    cst = np.zeros((128, 288), np.float32)
    cst[:, 0:128] = np.eye(128, dtype=np.float32)
    cst[:, 128:256] = 1.0
    pm = np.zeros((32, 32), np.float32)
    for m in range(16):
        pm[m + 16, m] = -1.0
        pm[m, m + 16] = 1.0
    cst[64:96, 256:288] = pm
    half = 16
    freqs = (1.0 / (np.float32(10000.0) ** (np.arange(half, dtype=np.float32) / np.float32(half)))).astype(np.float32)
    ang = (np.arange(T, dtype=np.float32)[:, None] * freqs[None, :]).astype(np.float32)
    cos = np.cos(ang).astype(np.float32).T
    sin = np.sin(ang).astype(np.float32).T
    rope = np.zeros((32, 2, T), np.float32)
    rope[0:16, 0] = cos; rope[16:32, 0] = cos
    rope[0:16, 1] = sin; rope[16:32, 1] = sin
    w_ukv = f(inp["w_ukv"][0]).reshape(256, NH, 128)
    w_kn = np.ascontiguousarray(w_ukv[:, :, 0:64].reshape(256, 512))
    w_v = np.ascontiguousarray(w_ukv[:, :, 64:128].reshape(256, 512))
    lru_w = np.ascontiguousarray(np.stack([f(inp["lru_wa"][0]), f(inp["lru_wi"][0])], axis=0))
    wg = f(inp["w_gate"][0]).reshape(8, 128, NJF, 128)
    wu = f(inp["w_up"][0]).reshape(8, 128, NJF, 128)
    wgu = np.ascontiguousarray(np.stack([wg, wu], axis=0).transpose(3, 2, 0, 1, 4).reshape(NJF, 128, 2048))
    shared = {
        "meta": f(inp["meta_tokens"]), "vecs": vecs, "cst": cst, "rope": rope,
        "w_in": f(inp["w_in"][0]), "w_uq": f(inp["w_uq"][0]), "w_kn": w_kn, "w_v": w_v, "lru_w": lru_w,
        "w_out": f(inp["w_out"][0]), "wgu": wgu, "w_down": f(inp["w_down"][0]),
    }
    return shared


_CACHE = {}


def kernel(**inputs):
    x = np.asarray(inputs["x"], dtype=np.float32)
    shared = _host_layout(inputs)
    if "nc" not in _CACHE:
        _CACHE["nc"] = build()
    nc, dbg = _CACHE["nc"]
    in_maps = []
    for i in range(NCORES):
        m = dict(shared)
        m["x"] = np.ascontiguousarray(x[NSEQ * i:NSEQ * (i + 1)])
        in_maps.append(m)
    res = run_bass_kernel_spmd(nc, in_maps, core_ids=list(range(NCORES)))
    if KDEBUG:
        _CACHE["dbg"] = {k: np.asarray(res.results[0]["dbg_" + k]) for k in dbg}
    out = np.concatenate([np.asarray(res.results[i]["out"]) for i in range(NCORES)], axis=0)
    return out.astype(np.float32)
```

```python
import os
from contextlib import ExitStack
import numpy as np
import concourse.bass as bass
import concourse.mybir as mybir
from concourse.bass_utils import run_bass_kernel_spmd

F32 = mybir.dt.float32
BF16 = mybir.dt.bfloat16
AF = mybir.ActivationFunctionType
ALU = mybir.AluOpType
AX = mybir.AxisListType

NCORES = 8
NSEQ = 2
D = 1024
SEQ = 2048
NM = 16
T = SEQ + NM
NH = 8
DFF = 2816
NJF = DFF // 128
EPS = 1e-6
RING = 7
KDEBUG = os.environ.get("KDEBUG", "") != ""
KSTOP = os.environ.get("KSTOP", "")
SAME_ENG_SYNC = os.environ.get("KNOSAME", "") == ""


class _Stop(Exception):
    pass


def phase_end(name):
    if KSTOP == name:
        raise _Stop()

V_G1, V_GQA, V_GKVA, V_QG, V_KG, V_CW, V_CB, V_BA, V_BI, V_LAM, V_GA, V_GR, V_G2, V_EPS, V_ONE = (
    0, 8, 11, 13, 14, 15, 31, 35, 43, 51, 59, 63, 67, 75, 76)
NV = 80


class Buf:
    def __init__(self, name, space, lo, hi):
        self.name, self.space, self.lo, self.hi = name, space, lo, hi
        self.w = None
        self.r = []
        self.ov = [self]


class Op:
    __slots__ = ("eng", "calls", "dma", "ndma", "deps", "need", "cnt", "sem", "id")


class _Rec:
    def __init__(self):
        self.calls = []

    def __getattr__(self, name):
        def m(*a, **k):
            self.calls.append((name, a, k))
            return self
        return m


class Sched:
    ENGS = ("pe", "act", "dve", "pool", "sp")

    def __init__(self, nc):
        self.nc = nc
        self.bufs = []
        self.ops = []
        self.dma_keys = {}
        self.store_ops = []

    def buf(self, name, space, lo, hi):
        b = Buf(name, space, lo, hi)
        for y in self.bufs:
            if y.space == space and y.lo < hi and lo < y.hi:
                y.ov.append(b)
                b.ov.append(y)
        self.bufs.append(b)
        return b

    def _rec(self, op, reads, writes):
        deps = set()
        for b in reads:
            for y in b.ov:
                if y.w is not None:
                    deps.add(y.w)
                if b.space == "ps":
                    deps.update(o for o in y.r if o.eng != op.eng)
        for b in writes:
            for y in b.ov:
                if y.w is not None:
                    deps.add(y.w)
                deps.update(y.r)
        deps.discard(op)
        op.deps = deps
        for b in reads:
            if not op.dma:
                b.r = [o for o in b.r if o.dma or o.eng != op.eng]
            b.r.append(op)
        for b in writes:
            b.w = op
            b.r = []
            for y in b.ov:
                if y is not b and y.lo >= b.lo and y.hi <= b.hi:
                    y.w = op
                    y.r = []
        op.id = len(self.ops)
        self.ops.append(op)

    def op(self, eng, fn, reads=(), writes=()):
        o = Op()
        r = _Rec()
        fn(r)
        assert r.calls
        o.eng, o.calls, o.dma, o.ndma, o.need, o.cnt, o.sem = eng, r.calls, False, 0, False, 0, None
        self._rec(o, list(reads), list(writes))
        return o

    def dma(self, queue, fn, key, n=1, reads=(), writes=(), store=False):
        o = Op()
        r = _Rec()
        fn(r, lambda ins: ins)
        n = len(r.calls)
        assert n >= 1
        o.eng, o.calls, o.dma, o.ndma, o.need = queue, r.calls, True, n, True
        c = self.dma_keys.setdefault(key, [0])
        c[0] += n
        o.cnt, o.sem = c[0], key
        self._rec(o, list(reads), list(writes))
        if store:
            self.store_ops.append(o)
        return o

    def emit(self):
        nc = self.nc
        for o in self.ops:
            for d in o.deps:
                if d.dma:
                    continue
                if o.dma or d.eng != o.eng or (SAME_ENG_SYNC and o.eng != "pe"):
                    d.need = True
        per = {e: [] for e in self.ENGS}
        for o in self.ops:
            per[o.eng].append(o)
        for e in self.ENGS:
            c = 0
            for o in per[e]:
                if not o.dma and o.need:
                    c += 1
                    o.cnt = c
        with ExitStack() as st:
            esem = {e: st.enter_context(nc.semaphore("s_" + e)) for e in self.ENGS}
            dsem = {k: st.enter_context(nc.semaphore("d_%d" % i)) for i, k in enumerate(self.dma_keys)}
            block = st.enter_context(nc.Block())
            engobj = {"pe": block.tensor, "act": block.scalar, "dve": block.vector,
                      "pool": block.gpsimd, "sp": block.sync}

            def run(ename):
                def body(eng):
                    waited = {}

                    def wait(sem, key, val):
                        if waited.get(key, 0) < val:
                            eng.wait_ge(sem, val)
                            waited[key] = val

                    for o in per[ename]:
                        need = {}
                        for d in o.deps:
                            if d.dma:
                                k = ("d", d.sem)
                                need[k] = max(need.get(k, 0), 16 * d.cnt)
                            elif d.eng != ename or o.dma or (SAME_ENG_SYNC and ename != "pe"):
                                k = ("e", d.eng)
                                need[k] = max(need.get(k, 0), d.cnt)
                        for k, v in need.items():
                            wait(dsem[k[1]] if k[0] == "d" else esem[k[1]], k, v)
                        ins = None
                        for (mname, a, k) in o.calls:
                            ins = getattr(eng, mname)(*a, **k)
                            if o.dma:
                                ins.then_inc(dsem[o.sem], 16)
                        if not o.dma and o.need:
                            ins.then_inc(esem[ename], 1)
                    if ename == "sp":
                        for key, c in self.dma_keys.items():
                            if any(s.sem == key for s in self.store_ops):
                                wait(dsem[key], ("d", key), 16 * c[0])
                return body

            for e in self.ENGS:
                engobj[e](run(e))


class Region:
    def __init__(self, S, arena, lo, hi):
        self.S, self.arena, self.lo, self.hi, self.cur = S, arena, lo, hi, lo

    def alloc(self, name, shape, dtype, nbuf=None):
        esz = 4 if dtype == F32 else 2
        n = 1
        for s in shape[1:]:
            n *= s
        nbytes = (n * esz + 3) // 4 * 4
        lo = self.cur
        self.cur += nbytes
        assert self.cur <= self.hi, (name, self.cur, self.hi)
        ap = self.arena[:, lo // 4:(lo + nbytes) // 4]
        if dtype == BF16:
            ap = ap.bitcast(BF16)
        ap = ap[:, 0:n]
        if len(shape) == 3:
            ap = ap.rearrange("p (a b) -> p a b", a=shape[1])
        elif len(shape) == 4:
            ap = ap.rearrange("p (a b c) -> p a b c", a=shape[1], b=shape[2])
        b = self.S.buf(name, "sb", lo, lo + nbytes)
        return b, ap


def build():
    nc = bass.Bass("TRN2", target_bir_lowering=False)
    dram = lambda n, s, dt=F32, k="ExternalInput": nc.dram_tensor(n, s, dt, kind=k).ap()
    x_d = dram("x", [NSEQ, SEQ, D])
    meta_d = dram("meta", [NM, D])
    vecs_d = dram("vecs", [128, NV])
    cst_d = dram("cst", [128, 288])
    rope_d = dram("rope", [32, 2, T])
    w_in_d = dram("w_in", [D, 1696])
    w_uq_d = dram("w_uq", [384, 768])
    w_kn_d = dram("w_kn", [256, 512])
    w_v_d = dram("w_v", [256, 512])
    lru_d = dram("lru_w", [2, 2, 8, 64, 64])
    w_out_d = dram("w_out", [D, D])
    wgu_d = dram("wgu", [NJF, 128, 2048])
    wd_d = dram("w_down", [DFF, D])
    out_d = dram("out", [NSEQ, SEQ, D], F32, "ExternalOutput")
    dbg_d = {}

    S = Sched(nc)
    K = 1024
    with ExitStack() as st:
        arena = st.enter_context(nc.sbuf_tensor("arena", [128, 212800 // 4], F32))
        banks = [st.enter_context(nc.psum_tensor("bank%d" % i, [128, 512], F32)) for i in range(8)]
        PB = [S.buf("bank%d" % i, "ps", i, i + 1) for i in range(8)]

        R0 = Region(S, arena, 0, 19 * K)
        b_vecs, vecs = R0.alloc("vecs", [128, NV], F32)
        b_ident, ident = R0.alloc("ident", [128, 128], BF16)
        b_ones, ones = R0.alloc("ones", [128, 128], BF16)
        b_pmat, pmat = R0.alloc("pmat", [128, 32], F32)
        b_rope, rope = R0.alloc("rope", [128, 2, T], F32)
        b_lamc, lamc = R0.alloc("lamc", [128, 16], F32)
        b_lamt, lamt = R0.alloc("lamt", [128, 8], F32)

        def V(c, p0=0, p1=128):
            return vecs[p0:p1, c:c + 1]

        RA = Region(S, arena, 19 * K, 51 * K)
        b_ornT, ornT = RA.alloc("ornT", [128, 4, SEQ], BF16)
        b_oatT, oatT = RA.alloc("oatT", [128, 4, SEQ], BF16)
        RL3 = Region(S, arena, 51 * K, 80 * K)
        b_cqnT, cqnT = RL3.alloc("cqnT", [128, 3, T], BF16)
        b_ckvnT, ckvnT = RL3.alloc("ckvnT", [128, 2, T], BF16)
        b_krT, krT = RL3.alloc("krT", [128, T], F32)

        S.dma("sp", lambda e, f: f(e.dma_start(out=vecs, in_=vecs_d[:, :])), "c_vecs", writes=[b_vecs])
        S.dma("pool", lambda e, f: f(e.dma_start(out=ident, in_=cst_d[:, 0:128])), "c_id", writes=[b_ident])
        S.dma("pool", lambda e, f: f(e.dma_start(out=ones, in_=cst_d[:, 128:256])), "c_on", writes=[b_ones])
        S.dma("sp", lambda e, f: f(e.dma_start(out=pmat, in_=cst_d[:, 256:288])), "c_pm", writes=[b_pmat])
        S.dma("sp", lambda e, f: f(e.dma_start(out=rope[64:96, :, :], in_=rope_d[:, :, :])), "c_rope", writes=[b_rope])
        S.op("act", lambda e: e.activation(out=lamt, in_=vecs[:, V_LAM:V_LAM + 8], func=AF.Exp, scale=-1.0),
             reads=[b_vecs], writes=[b_lamt])
        S.op("act", lambda e: e.activation(out=lamt, in_=lamt, func=AF.Ln, bias=V(V_ONE), scale=1.0),
             reads=[b_vecs, b_lamt], writes=[b_lamt])
        S.op("dve", lambda e: e.tensor_scalar(out=lamc[:, 0:8], in0=lamt, scalar1=-8.0, scalar2=None, op0=ALU.mult),
             reads=[b_lamt], writes=[b_lamc])
        S.op("dve", lambda e: e.tensor_scalar(out=lamc[:, 8:16], in0=lamt, scalar1=-16.0, scalar2=None, op0=ALU.mult),
             reads=[b_lamt], writes=[b_lamc])

        def dump(name, b, ap, shape, dt=F32):
            if not KDEBUG:
                return
            dd = dram("dbg_" + name, shape, dt, "ExternalOutput")
            dbg_d[name] = dd
            S.dma("sp", lambda e, f: f(e.dma_start(out=dd, in_=ap)), "dbg_" + name, reads=[b], store=True)

        def rstd_fm(ps_ap, rs_ap, np_, n, inv_n, b_ps, b_rs):
            S.op("act", lambda e: e.activation(out=rs_ap, in_=ps_ap, func=AF.Ln, bias=V(V_EPS, 0, np_), scale=inv_n),
                 reads=[b_ps, b_vecs], writes=[b_rs])
            S.op("act", lambda e: e.activation(out=rs_ap, in_=rs_ap, func=AF.Exp, scale=-0.5),
                 reads=[b_rs], writes=[b_rs])

        try:
            for s in range(NSEQ):
                R1a = Region(S, arena, 19 * K, 51 * K)
                b_xt, xt = [], []
                for i in range(2):
                    b, a = R1a.alloc("xt%d" % i, [128, 4, D], F32)
                    b_xt.append(b); xt.append(a)
                R1 = Region(S, arena, 80 * K, 158 * K)
                b_win, win = R1.alloc("w_in", [128, 8, 1696], BF16)
                b_xs, xs = R1.alloc("xs", [128, 4, D], BF16)
                b_xnT, xnT = [], []
                for i in range(2):
                    b, a = R1.alloc("xnT%d" % i, [128, 8, 512], BF16)
                    b_xnT.append(b); xnT.append(a)
                b_latf, latf = R1.alloc("latf", [128, 3, 512], F32)
                b_sqb, sqb = R1.alloc("sqb", [128, 3, 512], BF16)
                b_rs, rs = R1.alloc("rs", [128, 512], F32)
                b_gt1, gt1 = R1.alloc("gt1", [128, 512], F32)
                b_gt2, gt2 = R1.alloc("gt2", [128, 512], F32)
                b_ss, ss = R1.alloc("ss", [128, 4], F32)
                b_rstd, rstd = R1.alloc("rstd", [128, 4], F32)
                b_xsB, xsB = R1.alloc("xsB", [128, 4, D], BF16)
                b_ssB, ssB = R1.alloc("ssB", [128, 4], F32)
                b_rstdB, rstdB = R1.alloc("rstdB", [128, 4], F32)
                scrs = [(b_xs, xs, b_ss, ss, b_rstd, rstd), (b_xsB, xsB, b_ssB, ssB, b_rstdB, rstdB)]
                R2 = Region(S, arena, 158 * K, 207 * K + 800)
                b_xr, xr = [], []
                for cc in range(4):
                    b, a = R2.alloc("xr%d" % cc, [128, T + 4], F32)
                    b_xr.append(b); xr.append(a)
                b_gg, gg = R2.alloc("gg", [128, 4, SEQ], BF16)

                def ld_win(e, f):
                    for kc in range(8):
                        f(e.dma_start(out=win[:, kc, :], in_=w_in_d[kc * 128:(kc + 1) * 128, :]))
                S.dma("pool", ld_win, "w_in", n=8, writes=[b_win])
                for cc in range(4):
                    S.op("dve", lambda e, cc=cc: e.memset(xr[cc][:, 0:2], 0.0), writes=[b_xr[cc]])
                    S.op("dve", lambda e, cc=cc: e.memset(xr[cc][:, T + 2:T + 4], 0.0), writes=[b_xr[cc]])

                chunks = [(0, NM, None)] + [(NM + 512 * c, 512, c) for c in range(4)]

                def ld_x(ci):
                    pos0, n, c = chunks[ci]
                    sl = ci % 2
                    if c is None:
                        S.dma("sp", lambda e, f: f(e.dma_start(out=xt[sl][0:NM, 0, :], in_=meta_d[:, :])),
                              "xt%d" % sl, writes=[b_xt[sl]])
                    else:
                        src = x_d[s, 512 * c:512 * c + 512, :].rearrange("(j p) f -> p j f", p=128)
                        S.dma("sp", lambda e, f: f(e.dma_start(out=xt[sl], in_=src)), "xt%d" % sl, writes=[b_xt[sl]])

                def norm_tm(xin, b_xin, np_, nt, gcol, dstT, b_dstT, tpb, part=0, scr=None):
                    n = 128 * nt if np_ == 128 else np_
                    b_xs_, xs_, b_ss_, ss_, b_rstd_, rstd_ = scr if scr is not None else (b_xs, xs, b_ss, ss, b_rstd, rstd)
                    if part in (0, 1):
                        norm_stats(xin, b_xin, np_, nt, b_xs_, xs_, b_ss_, ss_, b_rstd_, rstd_)
                    if part in (0, 2):
                        norm_tr(np_, nt, n, gcol, dstT, b_dstT, tpb, b_xs_, xs_)

                def norm_stats(xin, b_xin, np_, nt, b_xs, xs, b_ss, ss, b_rstd, rstd):
                    for j in range(nt):
                        S.op("act", lambda e, j=j: e.activation(out=xs[0:np_, j, :], in_=xin[0:np_, j, :], func=AF.Square),
                             reads=[b_xin], writes=[b_xs])
                    S.op("dve", lambda e: e.tensor_reduce(out=ss[0:np_, 0:nt], in_=xs[0:np_, 0:nt, :], axis=AX.X, op=ALU.add),
                         reads=[b_xs], writes=[b_ss])
                    S.op("act", lambda e: e.activation(out=rstd[0:np_, 0:nt], in_=ss[0:np_, 0:nt], func=AF.Ln,
                                                       bias=V(V_EPS, 0, np_), scale=1.0 / D),
                         reads=[b_ss, b_vecs], writes=[b_rstd])
                    S.op("act", lambda e: e.activation(out=rstd[0:np_, 0:nt], in_=rstd[0:np_, 0:nt], func=AF.Exp, scale=-0.5),
                         reads=[b_rstd], writes=[b_rstd])
                    for j in range(nt):
                        S.op("dve", lambda e, j=j: e.tensor_scalar(out=xs[0:np_, j, :], in0=xin[0:np_, j, :],
                                                                   scalar1=rstd[0:np_, j:j + 1], scalar2=None, op0=ALU.mult),
                             reads=[b_xin, b_rstd], writes=[b_xs])

                def norm_tr(np_, nt, n, gcol, dstT, b_dstT, tpb, b_xs, xs):
                    for kc in range(8):
                        bk = tpb[kc % 2]
                        tp = banks[bk][:, :].bitcast(BF16)

                        def tr(e, kc=kc, tp=tp):
                            ins = None
                            for j in range(nt):
                                w = np_
                                ins = e.transpose(out=tp[:, j * 128:j * 128 + w], in_=xs[0:np_, j, kc * 128:(kc + 1) * 128],
                                                  identity=ident[0:np_, 0:np_])
                            return ins
                        S.op("pe", tr, reads=[b_xs, b_ident], writes=[PB[bk]])
                        S.op("dve", lambda e, kc=kc, tp=tp: e.tensor_scalar(out=dstT[:, kc, 0:n], in0=tp[:, 0:n],
                                                                           scalar1=V(gcol + kc), scalar2=None, op0=ALU.mult),
                             reads=[PB[bk], b_vecs], writes=[b_dstT])

                def mm_acc(e, out_ap, lhs_list, rhs_list):
                    ins = None
                    n = len(lhs_list)
                    for i in range(n):
                        ins = e.matmul(out_ap, lhs_list[i], rhs_list[i], start=(i == 0), stop=(i == n - 1))
                    return ins

                mmb = [2, 3, 4, 5]
                mmi = [0]

                def next_bank():
                    b = mmb[mmi[0] % len(mmb)]
                    mmi[0] += 1
                    return b

                def p1_norm(ci, part):
                    _, _, c_ = chunks[ci]
                    norm_tm(xt[ci % 2], b_xt[ci % 2], 128 if c_ is not None else NM, 4 if c_ is not None else 1,
                            V_G1, xnT[ci % 2], b_xnT[ci % 2], (0, 1), part=part, scr=scrs[ci % 2])

                ld_x(0)
                ld_x(1)
                p1_norm(0, 0)
                for ci in range(5):
                    pos0, n, c = chunks[ci]
                    sl = ci % 2
                    np_ = 128 if c is not None else NM
                    nt = 4 if c is not None else 1
                    xT = xnT[sl]
                    bxT = b_xnT[sl]
                    if ci + 1 < 5:
                        p1_norm(ci + 1, 1)

                    def win_mm(col0, m, bk, p0=0):
                        S.op("pe", lambda e: mm_acc(e, banks[bk][p0:p0 + m, 0:n],
                                                    [win[:, kc, col0:col0 + m] for kc in range(8)],
                                                    [xT[:, kc, 0:n] for kc in range(8)]),
                             reads=[b_win, bxT], writes=[PB[bk]])

                    for (col0, ng, gcol, dst, b_dst, inv) in ((0, 3, V_GQA, cqnT, b_cqnT, 1.0 / 384),
                                                               (384, 2, V_GKVA, ckvnT, b_ckvnT, 1.0 / 256)):
                        for g in range(ng):
                            bk = next_bank()
                            win_mm(col0 + 128 * g, 128, bk)
                            S.op("act", lambda e, g=g, bk=bk: e.activation(out=sqb[:, g, 0:n], in_=banks[bk][:, 0:n], func=AF.Square),
                                 reads=[PB[bk]], writes=[b_sqb])
                            S.op("dve", lambda e, g=g, bk=bk: e.tensor_copy(out=latf[:, g, 0:n], in_=banks[bk][:, 0:n]),
                                 reads=[PB[bk]], writes=[b_latf])
                        S.op("pe", lambda e, ng=ng: mm_acc(e, banks[6][:, 0:n], [ones[:, :]] * ng,
                                                           [sqb[:, g, 0:n] for g in range(ng)]),
                             reads=[b_ones, b_sqb], writes=[PB[6]])
                        rstd_fm(banks[6][:, 0:n], rs[:, 0:n], 128, n, inv, PB[6], b_rs)
                        for g in range(ng):
                            S.op("dve", lambda e, g=g, dst=dst, gcol=gcol: e.scalar_tensor_tensor(
                                out=dst[:, g, pos0:pos0 + n], in0=latf[:, g, 0:n], scalar=V(gcol + g), in1=rs[:, 0:n],
                                op0=ALU.mult, op1=ALU.mult), reads=[b_latf, b_rs, b_vecs], writes=[b_dst])
                    if ci + 1 < 5:
                        p1_norm(ci + 1, 2)
                    if ci + 2 < 5:
                        ld_x(ci + 2)
                    bk = next_bank()
                    win_mm(640, 32, bk, p0=64)
                    S.op("act", lambda e, bk=bk: e.activation(out=krT[64:96, pos0:pos0 + n], in_=banks[bk][64:96, 0:n], func=AF.Copy),
                         reads=[PB[bk]], writes=[b_krT])
                    for cc in range(4):
                        bk = next_bank()
                        win_mm(672 + 128 * cc, 128, bk)
                        S.op("act", lambda e, cc=cc, bk=bk: e.activation(out=xr[cc][:, 2 + pos0:2 + pos0 + n], in_=banks[bk][:, 0:n], func=AF.Copy),
                             reads=[PB[bk]], writes=[b_xr[cc]])
                    if c is not None:
                        for cc in range(4):
                            bk = next_bank()
                            win_mm(1184 + 128 * cc, 128, bk)
                            S.op("act", lambda e, bk=bk: e.activation(out=gt1, in_=banks[bk][:, :], func=AF.Square),
                                 reads=[PB[bk]], writes=[b_gt1])
                            S.op("dve", lambda e: e.tensor_scalar(out=gt1, in0=gt1, scalar1=0.044715, scalar2=1.0,
                                                                  op0=ALU.mult, op1=ALU.add), reads=[b_gt1], writes=[b_gt1])
                            S.op("dve", lambda e, bk=bk: e.tensor_tensor(out=gt1, in0=banks[bk][:, :], in1=gt1, op=ALU.mult),
                                 reads=[PB[bk], b_gt1], writes=[b_gt1])
                            S.op("act", lambda e: e.activation(out=gt2, in_=gt1, func=AF.Sigmoid, scale=1.5957691216057308),
                                 reads=[b_gt1], writes=[b_gt2])
                            S.op("dve", lambda e, cc=cc, bk=bk, c=c: e.tensor_tensor(out=gg[:, cc, 512 * c:512 * c + 512],
                                                                                 in0=banks[bk][:, :], in1=gt2, op=ALU.mult),
                                 reads=[PB[bk], b_gt2], writes=[b_gg])
                if s == 0:
                    dump("cqnT", b_cqnT, cqnT, [128, 3, T], BF16)
                    dump("ckvnT", b_ckvnT, ckvnT, [128, 2, T], BF16)
                    dump("krT", b_krT, krT[64:96, :], [32, T])
                    dump("xr0", b_xr[0], xr[0], [128, T + 4])
                    dump("gg", b_gg, gg, [128, 4, SEQ], BF16)
                phase_end("p1")

                R4 = Region(S, arena, 80 * K, 158 * K)
                b_lw, lw = R4.alloc("lru_w", [128, 16, 128], BF16)
                b_xc, xc = R4.alloc("xc", [128, T], F32)
                RXB = Region(S, arena, 19 * K, 19 * K + T * 2 + 4)
                b_xcb, xcb = RXB.alloc("xcb", [128, T], BF16)
                lb = {}
                for nm in ("r0", "i0", "a0", "r1", "i1", "a1", "hf", "hb"):
                    lb[nm] = R4.alloc("l_" + nm, [128, T], F32)
                R4s = Region(S, arena, lb["r0"][0].lo, lb["r0"][0].hi)
                b_sq4, sq4 = R4s.alloc("sq4", [128, 4, 512], BF16)
                b_rsr, rsr = R4s.alloc("rsr", [128, 512], F32)
                S.op("dve", lambda e: e.memset(lw, 0.0), writes=[b_lw])

                def ld_lru(e, f):
                    for g in range(2):
                        for d in range(2):
                            src = lru_d[g, d].rearrange("(c b) i j -> b i c j", b=2)
                            k0 = (g * 2 + d) * 4
                            for bh in range(2):
                                f(e.dma_start(out=lw[bh * 64:(bh + 1) * 64, k0:k0 + 4, bh * 64:(bh + 1) * 64], in_=src[bh]))
                S.dma("pool", ld_lru, "lru_w", n=8, writes=[b_lw])

                pieces = [(512 * p, 512) for p in range(4)] + [(2048, 16)]
                for cc in range(4):
                    S.op("dve", lambda e, cc=cc: e.tensor_scalar(out=xc, in0=xr[cc][:, 0:T], scalar1=V(V_CW + cc * 4),
                                                                 scalar2=V(V_CB + cc), op0=ALU.mult, op1=ALU.add),
                         reads=[b_xr[cc], b_vecs], writes=[b_xc])
                    for j in range(1, 4):
                        S.op("dve", lambda e, cc=cc, j=j: e.scalar_tensor_tensor(out=xc, in0=xr[cc][:, j:j + T], scalar=V(V_CW + cc * 4 + j),
                                                                                in1=xc, op0=ALU.mult, op1=ALU.add),
                             reads=[b_xr[cc], b_vecs, b_xc], writes=[b_xc])
                    S.op("act", lambda e: e.activation(out=xcb, in_=xc, func=AF.Copy), reads=[b_xc], writes=[b_xcb])
                    for d in range(2):
                        b_r, r_ = lb["r%d" % d]; b_i, i_ = lb["i%d" % d]; b_a, a_ = lb["a%d" % d]
                        b_b, bb_ = b_i, i_
                        b_h, h_ = lb["hf"] if d == 0 else lb["hb"]
                        for (p0, pn) in pieces:
                            for g, (bdst, dst, bcol) in enumerate(((b_r, r_, V_BA), (b_i, i_, V_BI))):
                                bk = next_bank()
                                S.op("pe", lambda e, g=g, bk=bk, p0=p0, pn=pn, d=d, cc=cc: e.matmul(
                                    banks[bk][:, 0:pn], lw[:, (g * 2 + d) * 4 + cc, :], xcb[:, p0:p0 + pn], start=True, stop=True),
                                    reads=[b_lw, b_xcb], writes=[PB[bk]])
                                S.op("act", lambda e, bk=bk, p0=p0, pn=pn, dst=dst, bcol=bcol, d=d, cc=cc: e.activation(
                                    out=dst[:, p0:p0 + pn], in_=banks[bk][:, 0:pn], func=AF.Sigmoid,
                                    bias=V(bcol + d * 4 + cc), scale=1.0), reads=[PB[bk], b_vecs], writes=[bdst])
                        ci_ = d * 4 + cc
                        S.op("act", lambda e, ci_=ci_: e.activation(out=a_, in_=r_, func=AF.Exp, scale=lamc[:, ci_:ci_ + 1]),
                             reads=[b_r, b_lamc], writes=[b_a])
                        S.op("act", lambda e, ci_=ci_: e.activation(out=r_, in_=r_, func=AF.Exp, scale=lamc[:, 8 + ci_:9 + ci_]),
                             reads=[b_r, b_lamc], writes=[b_r])
                        S.op("act", lambda e: e.activation(out=r_, in_=r_, func=AF.Sqrt, bias=V(V_ONE), scale=-1.0),
                             reads=[b_r, b_vecs], writes=[b_r])
                        S.op("pool", lambda e: e.tensor_tensor(out=bb_, in0=i_, in1=xc, op=ALU.mult),
                             reads=[b_i, b_xc], writes=[b_b])
                        S.op("dve", lambda e: e.tensor_tensor(out=bb_, in0=bb_, in1=r_, op=ALU.mult),
                             reads=[b_b, b_r], writes=[b_b])
                        if d == 0:
                            S.op("dve", lambda e, h_=h_: e.tensor_tensor_scan(out=h_, data0=a_, data1=bb_, initial=0.0,
                                                                             op0=ALU.mult, op1=ALU.add),
                                 reads=[b_a, b_b], writes=[b_h])
                        else:
                            S.op("dve", lambda e, h_=h_: e.tensor_tensor_scan(out=h_[:, ::-1], data0=a_[:, ::-1], data1=bb_[:, ::-1],
                                                                             initial=0.0, op0=ALU.mult, op1=ALU.add),
                                 reads=[b_a, b_b], writes=[b_h])
                    b_hf, hf = lb["hf"]; b_hb, hb = lb["hb"]
                    S.op("pool", lambda e: e.tensor_tensor(out=hf[:, NM:T], in0=hf[:, NM:T], in1=hb[:, NM:T], op=ALU.add),
                         reads=[b_hf, b_hb], writes=[b_hf])
                    S.op("dve", lambda e, cc=cc: e.tensor_tensor(out=xr[cc][:, 2 + NM:2 + T], in0=hf[:, NM:T], in1=gg[:, cc, :], op=ALU.mult),
                         reads=[b_hf, b_gg], writes=[b_xr[cc]])
                for c in range(4):
                    c0 = 2 + NM + 512 * c
                    for cc in range(4):
                        S.op("act", lambda e, cc=cc, c0=c0: e.activation(out=sq4[:, cc, :], in_=xr[cc][:, c0:c0 + 512], func=AF.Square),
                             reads=[b_xr[cc]], writes=[b_sq4])
                    S.op("pe", lambda e: mm_acc(e, banks[6][:, :], [ones[:, :]] * 4, [sq4[:, cc, :] for cc in range(4)]),
                         reads=[b_ones, b_sq4], writes=[PB[6]])
                    rstd_fm(banks[6][:, :], rsr, 128, 512, 1.0 / 512, PB[6], b_rsr)
                    for cc in range(4):
                        S.op("dve", lambda e, cc=cc, c0=c0, c=c: e.scalar_tensor_tensor(
                            out=ornT[:, cc, 512 * c:512 * c + 512], in0=xr[cc][:, c0:c0 + 512], scalar=V(V_GR + cc), in1=rsr,
                            op0=ALU.mult, op1=ALU.mult), reads=[b_xr[cc], b_rsr, b_vecs], writes=[b_ornT])
                if s == 0:
                    dump("ornT", b_ornT, ornT, [128, 4, SEQ], BF16)
                phase_end("p2")

                R6 = Region(S, arena, 80 * K, 207 * K + 800)
                b_wkn, wkn = R6.alloc("w_kn", [128, 2, 512], BF16)
                b_wv, wv = R6.alloc("w_v", [128, 2, 512], BF16)
                b_wuq, wuq = R6.alloc("w_uq", [128, 3, 768], BF16)
                b_KT, KT = [], []
                for h in range(NH):
                    b, a = R6.alloc("KT%d" % h, [128, T], BF16)
                    b_KT.append(b); KT.append(a)
                b_va, va = R6.alloc("vaug", [128, 17, NH, 128], BF16)
                b_QT, QT = [], []
                for i in range(2):
                    b, a = R6.alloc("QT%d" % i, [128, NH, 512], BF16)
                    b_QT.append(b); QT.append(a)
                b_PT, PT = [], []
                for i in range(4):
                    b, a = R6.alloc("PT%d" % i, [128, 512], BF16)
                    b_PT.append(b); PT.append(a)
                b_oraw, oraw = [], []
                for i in range(4):
                    b, a = R6.alloc("oraw%d" % i, [128, 512], F32)
                    b_oraw.append(b); oraw.append(a)
                b_sqk, sqk = [], []
                for i in range(2):
                    b, a = R6.alloc("sqk%d" % i, [128, 512], BF16)
                    b_sqk.append(b); sqk.append(a)
                b_rden, rden = [], []
                for i in range(2):
                    b, a = R6.alloc("rden%d" % i, [128, 512], F32)
                    b_rden.append(b); rden.append(a)
                b_xk, xk = R6.alloc("xk", [128, 512], F32)
                b_t1, t1 = R6.alloc("t1", [128, 512], F32)
                b_t2, t2 = R6.alloc("t2", [128, 512], F32)
                b_krp, krp = R6.alloc("krp", [128, 512], F32)
                b_rec, rec = R6.alloc("rec", [128, 512], F32)
                b_sqa, sqa = R6.alloc("sqa", [128, 4, 512], BF16)
                b_rsa, rsa = R6.alloc("rsa", [128, 512], F32)

                def ld_kvw(e, f):
                    for kc in range(2):
                        f(e.dma_start(out=wkn[:, kc, :], in_=w_kn_d[kc * 128:(kc + 1) * 128, :]))
                        f(e.dma_start(out=wv[:, kc, :], in_=w_v_d[kc * 128:(kc + 1) * 128, :]))
                S.dma("pool", ld_kvw, "w_kv", n=4, writes=[b_wkn, b_wv])

                def ld_uq(e, f):
                    for kc in range(3):
                        f(e.dma_start(out=wuq[:, kc, :], in_=w_uq_d[kc * 128:(kc + 1) * 128, :]))
                S.dma("pool", ld_uq, "w_uq", n=3, writes=[b_wuq])
                S.op("pool", lambda e: e.memset(va, 1.0), writes=[b_va])

                def rope_rows(src_ps_or_sb, b_src, gcol, cols0, n, dst, b_dst, rd, b_rd, bk_px):
                    S.op("dve", lambda e: e.tensor_scalar(out=xk[64:96, 0:n], in0=src_ps_or_sb, scalar1=V(gcol, 64, 96),
                                                          scalar2=None, op0=ALU.mult), reads=[b_src, b_vecs], writes=[b_xk])
                    S.op("pe", lambda e: e.matmul(banks[bk_px][64:96, 0:n], pmat[64:96, 0:32], xk[64:96, 0:n], start=True, stop=True),
                         reads=[b_pmat, b_xk], writes=[PB[bk_px]])
                    S.op("dve", lambda e: e.tensor_tensor(out=t1[64:96, 0:n], in0=xk[64:96, 0:n], in1=rope[64:96, 0, cols0:cols0 + n], op=ALU.mult),
                         reads=[b_xk, b_rope], writes=[b_t1])
                    S.op("dve", lambda e: e.tensor_tensor(out=t2[64:96, 0:n], in0=banks[bk_px][64:96, 0:n], in1=rope[64:96, 1, cols0:cols0 + n], op=ALU.mult),
                         reads=[PB[bk_px], b_rope], writes=[b_t2])
                    S.op("dve", lambda e: e.tensor_tensor(out=t1[64:96, 0:n], in0=t1[64:96, 0:n], in1=t2[64:96, 0:n], op=ALU.add),
                         reads=[b_t1, b_t2], writes=[b_t1])
                    if rd is None:
                        S.op("dve", lambda e: e.tensor_copy(out=dst, in_=t1[64:96, 0:n]), reads=[b_t1], writes=[b_dst])
                    else:
                        S.op("dve", lambda e: e.tensor_tensor(out=dst, in0=t1[64:96, 0:n], in1=rd, op=ALU.mult),
                             reads=[b_t1, b_rd], writes=[b_dst])

                for ci in range(5):
                    pos0, n, c = chunks[ci]
                    rope_rows(krT[64:96, pos0:pos0 + n], b_krT, V_KG, pos0, n, krp[64:96, 0:n], b_krp, None, None, 7)
                    for i in range(2):
                        S.op("act", lambda e, i=i: e.activation(out=sqk[i][64:96, 0:n], in_=krT[64:96, pos0:pos0 + n], func=AF.Square),
                             reads=[b_krT], writes=[b_sqk[i]])
                    for h in range(NH):
                        bk = next_bank()
                        sl = h % 2
                        S.op("pe", lambda e, h=h, bk=bk: mm_acc(e, banks[bk][0:64, 0:n],
                                                               [wkn[:, kc, h * 64:(h + 1) * 64] for kc in range(2)],
                                                               [ckvnT[:, kc, pos0:pos0 + n] for kc in range(2)]),
                             reads=[b_wkn, b_ckvnT], writes=[PB[bk]])
                        S.op("act", lambda e, bk=bk, sl=sl: e.activation(out=sqk[sl][0:64, 0:n], in_=banks[bk][0:64, 0:n], func=AF.Square),
                             reads=[PB[bk]], writes=[b_sqk[sl]])
                        S.op("pe", lambda e, sl=sl: e.matmul(banks[6][0:96, 0:n], ones[0:96, 0:96], sqk[sl][0:96, 0:n], start=True, stop=True),
                             reads=[b_ones, b_sqk[sl]], writes=[PB[6]])
                        rstd_fm(banks[6][0:96, 0:n], rden[sl][0:96, 0:n], 96, n, 1.0 / 96, PB[6], b_rden[sl])
                        S.op("dve", lambda e, h=h, bk=bk, sl=sl: e.scalar_tensor_tensor(
                            out=KT[h][0:64, pos0:pos0 + n], in0=banks[bk][0:64, 0:n], scalar=V(V_KG, 0, 64), in1=rden[sl][0:64, 0:n],
                            op0=ALU.mult, op1=ALU.mult), reads=[PB[bk], b_rden[sl], b_vecs], writes=[b_KT[h]])
                        S.op("dve", lambda e, h=h, sl=sl: e.tensor_tensor(out=KT[h][64:96, pos0:pos0 + n], in0=krp[64:96, 0:n],
                                                                         in1=rden[sl][64:96, 0:n], op=ALU.mult),
                             reads=[b_krp, b_rden[sl]], writes=[b_KT[h]])
                    ntile = 1 if c is None else 4
                    for j in range(ntile):
                        kt = 0 if c is None else 1 + 4 * c + j
                        npk = NM if c is None else 128
                        bk = next_bank()
                        S.op("pe", lambda e, j=j, bk=bk, npk=npk: mm_acc(e, banks[bk][0:npk, :],
                                                                        [ckvnT[:, kc, pos0 + 128 * j:pos0 + 128 * j + npk] for kc in range(2)],
                                                                        [wv[:, kc, :] for kc in range(2)]),
                             reads=[b_ckvnT, b_wv], writes=[PB[bk]])
                        for par in range(2):
                            src = banks[bk][0:npk, :].rearrange("p (a b d) -> p a b d", a=4, b=2)[:, :, par, :]
                            S.op("act", lambda e, kt=kt, par=par, src=src, npk=npk: e.activation(
                                out=va[0:npk, kt, par::2, par * 64:par * 64 + 64], in_=src, func=AF.Copy),
                                reads=[PB[bk]], writes=[b_va])
                if s == 0:
                    dump("KT0", b_KT[0], KT[0][0:96, :], [96, T], BF16)
                    dump("KT3", b_KT[3], KT[3][0:96, :], [96, T], BF16)
                    dump("vaug", b_va, va, [128, 17, NH, 128], BF16)
                phase_end("kv")

                def qprep_stages(c, h):
                    sl = c % 2
                    pos0 = NM + 512 * c
                    bk = 4
                    sk = h % 2
                    n = 512

                    def st0():
                        S.op("pe", lambda e: mm_acc(e, banks[bk][0:96, :],
                                                    [wuq[:, kc, h * 96:(h + 1) * 96] for kc in range(3)],
                                                    [cqnT[:, kc, pos0:pos0 + 512] for kc in range(3)]),
                             reads=[b_wuq, b_cqnT], writes=[PB[bk]])

                    def st1():
                        S.op("act", lambda e: e.activation(out=sqk[sk][0:96, :], in_=banks[bk][0:96, :], func=AF.Square),
                             reads=[PB[bk]], writes=[b_sqk[sk]])

                    def st2():
                        S.op("pe", lambda e: e.matmul(banks[6][0:96, :], ones[0:96, 0:96], sqk[sk][0:96, :], start=True, stop=True),
                             reads=[b_ones, b_sqk[sk]], writes=[PB[6]])

                    def st3():
                        rstd_fm(banks[6][0:96, :], rden[sk][0:96, :], 96, 512, 1.0 / 96, PB[6], b_rden[sk])

                    def st4():
                        S.op("dve", lambda e: e.scalar_tensor_tensor(
                            out=QT[sl][0:64, h, :], in0=banks[bk][0:64, :], scalar=V(V_QG, 0, 64), in1=rden[sk][0:64, :],
                            op0=ALU.mult, op1=ALU.mult), reads=[PB[bk], b_rden[sk], b_vecs], writes=[b_QT[sl]])
                        S.op("dve", lambda e: e.tensor_scalar(out=xk[64:96, 0:n], in0=banks[bk][64:96, :], scalar1=V(V_QG, 64, 96),
                                                              scalar2=None, op0=ALU.mult), reads=[PB[bk], b_vecs], writes=[b_xk])

                    def st5():
                        S.op("pe", lambda e: e.matmul(banks[7][64:96, 0:n], pmat[64:96, 0:32], xk[64:96, 0:n], start=True, stop=True),
                             reads=[b_pmat, b_xk], writes=[PB[7]])

                    def st6():
                        S.op("dve", lambda e: e.tensor_tensor(out=t1[64:96, 0:n], in0=xk[64:96, 0:n], in1=rope[64:96, 0, pos0:pos0 + n], op=ALU.mult),
                             reads=[b_xk, b_rope], writes=[b_t1])
                        S.op("dve", lambda e: e.tensor_tensor(out=t2[64:96, 0:n], in0=banks[7][64:96, 0:n], in1=rope[64:96, 1, pos0:pos0 + n], op=ALU.mult),
                             reads=[PB[7], b_rope], writes=[b_t2])
                        S.op("dve", lambda e: e.tensor_tensor(out=t1[64:96, 0:n], in0=t1[64:96, 0:n], in1=t2[64:96, 0:n], op=ALU.add),
                             reads=[b_t1, b_t2], writes=[b_t1])
                        S.op("dve", lambda e: e.tensor_tensor(out=QT[sl][64:96, h, :], in0=t1[64:96, 0:n], in1=rden[sk][64:96, :], op=ALU.mult),
                             reads=[b_t1, b_rden[sk]], writes=[b_QT[sl]])
                    return [st0, st1, st2, st3, st4, st5, st6]

                def qprep_head(c, h):
                    for st in qprep_stages(c, h):
                        st()

                ktiles = [(0, NM)] + [(NM + 128 * i, 128) for i in range(16)]
                xcb_junk = va[:, 1, 0:4, :].rearrange('p a b -> p (a b)')
                sbk = [0, 1]
                JUNK = int(os.environ.get('KJUNK', '192'))
                obk = [2, 3]
                scale = 96.0 ** -0.5
                LAG = int(os.environ.get('KLAG', '2'))
                INTER = os.environ.get('KINTER', '1') == '1'

                pend = []

                def attention(c, inter=None):
                    sl = c % 2
                    steps = [(h, kt) for h in range(NH) for kt in range(17)]
                    nst = len(steps)

                    def emit_S(i):
                        h, kt = steps[i]
                        if inter is not None:
                            inter(h, kt)
                        if h == 0 and kt in (2, 4, 6) and pend:
                            pend.pop(0)()
                        k0, nk = ktiles[kt]
                        sb_ = sbk[i % 2]
                        pt = i % 4
                        S.op("pe", lambda e: e.matmul(banks[sb_][0:nk, :], KT[h][0:96, k0:k0 + nk], QT[sl][0:96, h, :],
                                                      start=True, stop=True),
                             reads=[b_KT[h], b_QT[sl]], writes=[PB[sb_]])
                        S.op("act", lambda e: e.activation(out=PT[pt][0:nk, :], in_=banks[sb_][0:nk, :], func=AF.Exp, scale=scale),
                             reads=[PB[sb_]], writes=[b_PT[pt]])

                    def emit_PV(i):
                        h, kt = steps[i]
                        k0, nk = ktiles[kt]
                        pt = i % 4
                        ob = obk[h % 2]
                        par = h % 2
                        S.op("pe", lambda e: e.matmul(banks[ob][:, :], va[0:nk, kt, h, :], PT[pt][0:nk, :],
                                                      start=(kt == 0), stop=(kt == 16)),
                             reads=[b_va, b_PT[pt]], writes=[PB[ob]])
                        if kt == 16:
                            own = slice(0, 64) if par == 0 else slice(64, 128)
                            oth = slice(64, 128) if par == 0 else slice(0, 64)
                            pr = h // 2
                            S.op("dve", lambda e: e.reciprocal(out=rec[own, :], in_=banks[ob][oth, :]),
                                 reads=[PB[ob]], writes=[b_rec])
                            S.op("dve", lambda e: e.tensor_tensor(out=oraw[pr][own, :], in0=banks[ob][own, :],
                                                                  in1=rec[own, :], op=ALU.mult),
                                 reads=[PB[ob], b_rec], writes=[b_oraw[pr]])

                    for i in range(nst + LAG):
                        if i < nst:
                            emit_S(i)
                        if JUNK:
                            S.op("pe", lambda e: e.matmul(banks[5][:, 0:JUNK], ones[:, :], xcb_junk[:, 0:JUNK], start=True, stop=True),
                                 reads=[b_ones], writes=[PB[5]])
                        if i >= LAG:
                            emit_PV(i - LAG)
                    def t0():
                        for pr in range(4):
                            S.op("act", lambda e, pr=pr: e.activation(out=sqa[:, pr, :], in_=oraw[pr], func=AF.Square),
                                 reads=[b_oraw[pr]], writes=[b_sqa])

                    def t1():
                        S.op("pe", lambda e: mm_acc(e, banks[6][:, :], [ones[:, :]] * 4, [sqa[:, pr, :] for pr in range(4)]),
                             reads=[b_ones, b_sqa], writes=[PB[6]])

                    def t2():
                        rstd_fm(banks[6][:, :], rsa, 128, 512, 1.0 / 512, PB[6], b_rsa)

                    def t3():
                        for pr in range(4):
                            S.op("dve", lambda e, pr=pr: e.scalar_tensor_tensor(
                                out=oatT[:, pr, 512 * c:512 * c + 512], in0=oraw[pr], scalar=V(V_GA + pr), in1=rsa,
                                op0=ALU.mult, op1=ALU.mult), reads=[b_oraw[pr], b_rsa, b_vecs], writes=[b_oatT])
                    return [t0, lambda: (t1(), t2()), t3]

                mmb[:] = [4]
                for h in range(NH):
                    qprep_head(0, h)
                for c in range(4):
                    if INTER:
                        if c + 1 < 4:
                            stg = {h: qprep_stages(c + 1, h) for h in range(NH)}
                            tl = attention(c, lambda h, kt, stg=stg: stg[h][(kt - 1) // 2]() if (kt % 2 == 1 and kt < 15) else None)
                        else:
                            tl = attention(c, None)
                        pend.extend(tl)
                        if c == 3:
                            while pend:
                                pend.pop(0)()
                    else:
                        if c + 1 < 4:
                            for h in range(NH):
                                qprep_head(c + 1, h)
                        for t_ in attention(c, None):
                            t_()
                mmb[:] = [2, 3, 4, 5]
                if s == 0:
                    dump("oatT", b_oatT, oatT, [128, 4, SEQ], BF16)
                phase_end("att")

                R8 = Region(S, arena, 51 * K, 207 * K + 800)
                b_wo, wo = R8.alloc("w_out", [128, 8, D], BF16)
                b_wd, wd = R8.alloc("w_down", [128, NJF, D], BF16)
                b_ring, ring = [], []
                for i in range(RING):
                    b, a = R8.alloc("ring%d" % i, [128, 2, 8, 128], BF16)
                    b_ring.append(b); ring.append(a)
                b_aT, aT = R8.alloc("aT", [128, NJF, 512], BF16)
                b_hnT, hnT = R8.alloc("hnT", [128, 8, 512], BF16)
                b_xh, xh = R8.alloc("xh", [128, 4, D], F32)
                b_xs5, xs5 = R8.alloc("xs5", [128, 4, D], BF16)
                b_sg, sg = [], []
                for i in range(2):
                    b, a = R8.alloc("sg%d" % i, [128, 512], F32)
                    b_sg.append(b); sg.append(a)
                b_ss5, ss5 = R8.alloc("ss5", [128, 4], F32)
                b_rstd5, rstd5 = R8.alloc("rstd5", [128, 4], F32)
                b_xs, xs, b_ss, ss, b_rstd, rstd = b_xs5, xs5, b_ss5, ss5, b_rstd5, rstd5

                def ld_wo(e, f):
                    for kc in range(8):
                        f(e.dma_start(out=wo[:, kc, :], in_=w_out_d[kc * 128:(kc + 1) * 128, :]))
                S.dma("pool", ld_wo, "w_out", n=8, writes=[b_wo])

                def ld_wd(e, f):
                    for jf in range(NJF):
                        f(e.dma_start(out=wd[:, jf, :], in_=wd_d[jf * 128:(jf + 1) * 128, :]))
                S.dma("pool", ld_wd, "w_down", n=NJF, writes=[b_wd])

                ring_i = [0]

                def ld_ring(jf):
                    sl = ring_i[0] % RING
                    ring_i[0] += 1
                    dst = ring[sl].rearrange("p a b c -> p (a b c)")
                    S.dma("pool", lambda e, f: f(e.dma_start(out=dst, in_=wgu_d[jf])), "ring%d" % sl, writes=[b_ring[sl]])
                    return sl

                PRE = RING - 1
                for c in range(4):
                    src = x_d[s, 512 * c:512 * c + 512, :].rearrange("(j p) f -> p j f", p=128)
                    S.dma("sp", lambda e, f, src=src: f(e.dma_start(out=xh, in_=src)), "xh", writes=[b_xh])
                    slots = {}
                    for jf in range(PRE):
                        slots[jf] = ld_ring(jf)
                    for j in range(4):
                        for half in range(2):
                            bk = half
                            lhs = [oatT[:, kc, 512 * c + 128 * j:512 * c + 128 * j + 128] for kc in range(4)] + \
                                  [ornT[:, kc, 512 * c + 128 * j:512 * c + 128 * j + 128] for kc in range(4)]
                            rhs = [wo[:, kc, half * 512:(half + 1) * 512] for kc in range(8)]
                            S.op("pe", lambda e, bk=bk, lhs=lhs, rhs=rhs: mm_acc(e, banks[bk][:, :], lhs, rhs),
                                 reads=[b_oatT, b_ornT, b_wo], writes=[PB[bk]])
                            S.op("dve", lambda e, bk=bk, j=j, half=half: e.tensor_tensor(
                                out=xh[:, j, half * 512:(half + 1) * 512], in0=banks[bk][:, :], in1=xh[:, j, half * 512:(half + 1) * 512],
                                op=ALU.add), reads=[PB[bk], b_xh], writes=[b_xh])
                    if s == 0 and c == 0:
                        dump("h0", b_xh, xh, [128, 4, D])
                    norm_tm(xh, b_xh, 128, 4, V_G2, hnT, b_hnT, (2, 3))
                    for jf in range(NJF):
                        if jf + PRE < NJF:
                            slots[jf + PRE] = ld_ring(jf + PRE)
                        sl = slots[jf]
                        gb = 4 + jf % 2
                        ub = 6 + jf % 2
                        S.op("pe", lambda e, sl=sl, gb=gb: mm_acc(e, banks[gb][:, :], [ring[sl][:, 0, kc, :] for kc in range(8)],
                                                                 [hnT[:, kc, :] for kc in range(8)]),
                             reads=[b_ring[sl], b_hnT], writes=[PB[gb]])
                        S.op("pe", lambda e, sl=sl, ub=ub: mm_acc(e, banks[ub][:, :], [ring[sl][:, 1, kc, :] for kc in range(8)],
                                                                 [hnT[:, kc, :] for kc in range(8)]),
                             reads=[b_ring[sl], b_hnT], writes=[PB[ub]])
                        S.op("act", lambda e, gb=gb, jf=jf: e.activation(out=sg[jf % 2], in_=banks[gb][:, :], func=AF.Silu),
                             reads=[PB[gb]], writes=[b_sg[jf % 2]])
                        S.op("dve", lambda e, ub=ub, jf=jf: e.tensor_tensor(out=aT[:, jf, :], in0=banks[ub][:, :], in1=sg[jf % 2], op=ALU.mult),
                             reads=[PB[ub], b_sg[jf % 2]], writes=[b_aT])
                    for j in range(4):
                        for half in range(2):
                            bk = half
                            S.op("pe", lambda e, bk=bk, j=j, half=half: mm_acc(
                                e, banks[bk][:, :], [aT[:, jf, 128 * j:128 * j + 128] for jf in range(NJF)],
                                [wd[:, jf, half * 512:(half + 1) * 512] for jf in range(NJF)]),
                                reads=[b_aT, b_wd], writes=[PB[bk]])
                            S.op("dve", lambda e, bk=bk, j=j, half=half: e.tensor_tensor(
                                out=xh[:, j, half * 512:(half + 1) * 512], in0=banks[bk][:, :], in1=xh[:, j, half * 512:(half + 1) * 512],
                                op=ALU.add), reads=[PB[bk], b_xh], writes=[b_xh])
                    dst = out_d[s, 512 * c:512 * c + 512, :].rearrange("(j p) f -> p j f", p=128)
                    S.dma("sp", lambda e, f, dst=dst: f(e.dma_start(out=dst, in_=xh)), "xh_st", reads=[b_xh], store=True)
        except _Stop:
            pass

        S.emit()
    return nc, dbg_d


def _host_layout(inp):
    f = lambda a: np.ascontiguousarray(np.asarray(a, dtype=np.float32))
    vecs = np.zeros((128, NV), np.float32)
    col = lambda v, n: f(v).reshape(n, 128).T
    vecs[:, V_G1:V_G1 + 8] = col(inp["ln1_g"][0], 8)
    vecs[:, V_GQA:V_GQA + 3] = col(inp["q_a_norm_g"][0], 3)
    vecs[:, V_GKVA:V_GKVA + 2] = col(inp["kv_a_norm_g"][0], 2)
    vecs[0:96, V_QG] = f(inp["q_norm_g"][0])
    vecs[0:96, V_KG] = f(inp["k_norm_g"][0])
    cw = f(inp["conv_w"][0])
    for cc in range(4):
        for j in range(4):
            vecs[:, V_CW + cc * 4 + j] = cw[j, cc * 128:(cc + 1) * 128]
    vecs[:, V_CB:V_CB + 4] = col(inp["conv_b"][0], 4)
    for d in range(2):
        vecs[:, V_BA + d * 4:V_BA + d * 4 + 4] = col(inp["lru_ba"][0, d], 4)
        vecs[:, V_BI + d * 4:V_BI + d * 4 + 4] = col(inp["lru_bi"][0, d], 4)
        vecs[:, V_LAM + d * 4:V_LAM + d * 4 + 4] = col(inp["lru_lambda"][0, d], 4)
    vecs[:, V_GA:V_GA + 4] = col(inp["attn_out_g"][0], 4)
    vecs[:, V_GR:V_GR + 4] = col(inp["rnn_out_g"][0], 4)
    vecs[:, V_G2:V_G2 + 8] = col(inp["ln2_g"][0], 8)
    vecs[:, V_EPS] = EPS
    vecs[:, V_ONE] = 1.0
    cst = np.zeros((128, 288), np.float32)
    cst[:, 0:128] = np.eye(128, dtype=np.float32)
    cst[:, 128:256] = 1.0
    pm = np.zeros((32, 32), np.float32)
    for m in range(16):
        pm[m + 16, m] = -1.0
        pm[m, m + 16] = 1.0
    cst[64:96, 256:288] = pm
    half = 16
    freqs = (1.0 / (np.float32(10000.0) ** (np.arange(half, dtype=np.float32) / np.float32(half)))).astype(np.float32)
    ang = (np.arange(T, dtype=np.float32)[:, None] * freqs[None, :]).astype(np.float32)
    cos = np.cos(ang).astype(np.float32).T
    sin = np.sin(ang).astype(np.float32).T
    rope = np.zeros((32, 2, T), np.float32)
    rope[0:16, 0] = cos; rope[16:32, 0] = cos
    rope[0:16, 1] = sin; rope[16:32, 1] = sin
    w_ukv = f(inp["w_ukv"][0]).reshape(256, NH, 128)
    w_kn = np.ascontiguousarray(w_ukv[:, :, 0:64].reshape(256, 512))
    w_v = np.ascontiguousarray(w_ukv[:, :, 64:128].reshape(256, 512))
    lru_w = np.ascontiguousarray(np.stack([f(inp["lru_wa"][0]), f(inp["lru_wi"][0])], axis=0))
    wg = f(inp["w_gate"][0]).reshape(8, 128, NJF, 128)
    wu = f(inp["w_up"][0]).reshape(8, 128, NJF, 128)
    wgu = np.ascontiguousarray(np.stack([wg, wu], axis=0).transpose(3, 2, 0, 1, 4).reshape(NJF, 128, 2048))
    shared = {
        "meta": f(inp["meta_tokens"]), "vecs": vecs, "cst": cst, "rope": rope,
        "w_in": f(inp["w_in"][0]), "w_uq": f(inp["w_uq"][0]), "w_kn": w_kn, "w_v": w_v, "lru_w": lru_w,
        "w_out": f(inp["w_out"][0]), "wgu": wgu, "w_down": f(inp["w_down"][0]),
    }
    return shared


_CACHE = {}


def kernel(**inputs):
    x = np.asarray(inputs["x"], dtype=np.float32)
    shared = _host_layout(inputs)
    if "nc" not in _CACHE:
        _CACHE["nc"] = build()
    nc, dbg = _CACHE["nc"]
    in_maps = []
    for i in range(NCORES):
        m = dict(shared)
        m["x"] = np.ascontiguousarray(x[NSEQ * i:NSEQ * (i + 1)])
        in_maps.append(m)
    res = run_bass_kernel_spmd(nc, in_maps, core_ids=list(range(NCORES)))
    if KDEBUG:
        _CACHE["dbg"] = {k: np.asarray(res.results[0]["dbg_" + k]) for k in dbg}
    out = np.concatenate([np.asarray(res.results[i]["out"]) for i in range(NCORES)], axis=0)
    return out.astype(np.float32)
```

```python
import os
from contextlib import ExitStack
import numpy as np
import concourse.bass as bass
import concourse.mybir as mybir
from concourse.bass_utils import run_bass_kernel_spmd

F32 = mybir.dt.float32
BF16 = mybir.dt.bfloat16
AF = mybir.ActivationFunctionType
ALU = mybir.AluOpType
AX = mybir.AxisListType

NCORES = 8
NSEQ = 2
D = 1024
SEQ = 2048
NM = 16
T = SEQ + NM
NH = 8
DFF = 2816
NJF = DFF // 128
EPS = 1e-6
RING = 7
KDEBUG = os.environ.get("KDEBUG", "") != ""
KSTOP = os.environ.get("KSTOP", "")
MAX_SWDGE = int(os.environ.get('KMAXDMA', '10'))
SAME_ENG_SYNC = os.environ.get("KNOSAME", "") == ""


class _Stop(Exception):
    pass


def phase_end(name):
    if KSTOP == name:
        raise _Stop()

V_G1, V_GQA, V_GKVA, V_QG, V_KG, V_CW, V_CB, V_BA, V_BI, V_LAM, V_GA, V_GR, V_G2, V_EPS, V_ONE = (
    0, 8, 11, 13, 14, 15, 31, 35, 43, 51, 59, 63, 67, 75, 76)
NV = 80


class Buf:
    def __init__(self, name, space, lo, hi):
        self.name, self.space, self.lo, self.hi = name, space, lo, hi
        self.w = None
        self.r = []
        self.ov = [self]


class Op:
    __slots__ = ("eng", "calls", "dma", "ndma", "deps", "need", "cnt", "sem", "id")


class _Rec:
    def __init__(self):
        self.calls = []

    def __getattr__(self, name):
        def m(*a, **k):
            self.calls.append((name, a, k))
            return self
        return m


class Sched:
    ENGS = ("pe", "act", "dve", "pool", "sp")

    def __init__(self, nc):
        self.nc = nc
        self.bufs = []
        self.ops = []
        self.dma_keys = {}
        self.store_ops = []

    def buf(self, name, space, lo, hi):
        b = Buf(name, space, lo, hi)
        for y in self.bufs:
            if y.space == space and y.lo < hi and lo < y.hi:
                y.ov.append(b)
                b.ov.append(y)
        self.bufs.append(b)
        return b

    def _rec(self, op, reads, writes):
        deps = set()
        for b in reads:
            for y in b.ov:
                if y.w is not None:
                    deps.add(y.w)
                if b.space == "ps":
                    deps.update(o for o in y.r if o.eng != op.eng)
        for b in writes:
            for y in b.ov:
                if y.w is not None:
                    deps.add(y.w)
                deps.update(y.r)
        deps.discard(op)
        op.deps = deps
        for b in reads:
            if not op.dma:
                b.r = [o for o in b.r if o.dma or o.eng != op.eng]
            b.r.append(op)
        for b in writes:
            b.w = op
            b.r = []
            for y in b.ov:
                if y is not b and y.lo >= b.lo and y.hi <= b.hi:
                    y.w = op
                    y.r = []
        op.id = len(self.ops)
        self.ops.append(op)

    def op(self, eng, fn, reads=(), writes=()):
        o = Op()
        r = _Rec()
        fn(r)
        assert r.calls
        o.eng, o.calls, o.dma, o.ndma, o.need, o.cnt, o.sem = eng, r.calls, False, 0, False, 0, None
        self._rec(o, list(reads), list(writes))
        return o

    def dma(self, queue, fn, key, n=1, reads=(), writes=(), store=False):
        o = Op()
        r = _Rec()
        fn(r, lambda ins: ins)
        n = len(r.calls)
        assert n >= 1
        o.eng, o.calls, o.dma, o.ndma, o.need = queue, r.calls, True, n, True
        c = self.dma_keys.setdefault(key, [0])
        c[0] += n
        o.cnt, o.sem = c[0], key
        self._rec(o, list(reads), list(writes))
        if store:
            self.store_ops.append(o)
        return o

    def emit(self):
        nc = self.nc
        for o in self.ops:
            for d in o.deps:
                if d.dma:
                    continue
                if o.dma or d.eng != o.eng or (SAME_ENG_SYNC and o.eng != "pe"):
                    d.need = True
        per = {e: [] for e in self.ENGS}
        for o in self.ops:
            per[o.eng].append(o)
        for e in self.ENGS:
            c = 0
            for o in per[e]:
                if not o.dma and o.need:
                    c += 1
                    o.cnt = c
        with ExitStack() as st:
            esem = {e: st.enter_context(nc.semaphore("s_" + e)) for e in self.ENGS}
            dsem = {k: st.enter_context(nc.semaphore("d_%d" % i)) for i, k in enumerate(self.dma_keys)}
            block = st.enter_context(nc.Block())
            engobj = {"pe": block.tensor, "act": block.scalar, "dve": block.vector,
                      "pool": block.gpsimd, "sp": block.sync}

            def run(ename):
                def body(eng):
                    waited = {}
                    inflight = []

                    def wait(sem, key, val):
                        if waited.get(key, 0) < val:
                            eng.wait_ge(sem, val)
                            waited[key] = val

                    for o in per[ename]:
                        need = {}
                        for d in o.deps:
                            if d.dma:
                                k = ("d", d.sem)
                                need[k] = max(need.get(k, 0), 16 * d.cnt)
                            elif d.eng != ename or o.dma or (SAME_ENG_SYNC and ename != "pe"):
                                k = ("e", d.eng)
                                need[k] = max(need.get(k, 0), d.cnt)
                        for k, v in need.items():
                            wait(dsem[k[1]] if k[0] == "d" else esem[k[1]], k, v)
                        ins = None
                        if o.dma and ename == "pool":
                            while inflight and sum(x[2] for x in inflight) + o.ndma > MAX_SWDGE:
                                ks, cv, _ = inflight.pop(0)
                                wait(dsem[ks], ("d", ks), 16 * cv)
                            inflight.append((o.sem, o.cnt, o.ndma))
                        for ji, (mname, a, k) in enumerate(o.calls):
                            ins = getattr(eng, mname)(*a, **k)
                            if o.dma:
                                ins.then_inc(dsem[o.sem], 16)
                        if not o.dma and o.need:
                            ins.then_inc(esem[ename], 1)
                    if ename == "sp":
                        for key, c in self.dma_keys.items():
                            if any(s.sem == key for s in self.store_ops):
                                wait(dsem[key], ("d", key), 16 * c[0])
                return body

            for e in self.ENGS:
                engobj[e](run(e))


class Region:
    def __init__(self, S, arena, lo, hi):
        self.S, self.arena, self.lo, self.hi, self.cur = S, arena, lo, hi, lo

    def alloc(self, name, shape, dtype, nbuf=None):
        esz = 4 if dtype == F32 else 2
        n = 1
        for s in shape[1:]:
            n *= s
        nbytes = (n * esz + 3) // 4 * 4
        lo = self.cur
        self.cur += nbytes
        assert self.cur <= self.hi, (name, self.cur, self.hi)
        ap = self.arena[:, lo // 4:(lo + nbytes) // 4]
        if dtype == BF16:
            ap = ap.bitcast(BF16)
        ap = ap[:, 0:n]
        if len(shape) == 3:
            ap = ap.rearrange("p (a b) -> p a b", a=shape[1])
        elif len(shape) == 4:
            ap = ap.rearrange("p (a b c) -> p a b c", a=shape[1], b=shape[2])
        b = self.S.buf(name, "sb", lo, lo + nbytes)
        return b, ap


def build():
    nc = bass.Bass("TRN2", target_bir_lowering=False)
    dram = lambda n, s, dt=F32, k="ExternalInput": nc.dram_tensor(n, s, dt, kind=k).ap()
    x_d = dram("x", [NSEQ, SEQ, D])
    meta_d = dram("meta", [NM, D])
    vecs_d = dram("vecs", [128, NV])
    cst_d = dram("cst", [128, 288])
    rope_d = dram("rope", [32, 2, T])
    w_in_d = dram("w_in", [D, 1696])
    w_uq_d = dram("w_uq", [384, 768])
    w_kn_d = dram("w_kn", [256, 512])
    w_v_d = dram("w_v", [256, 512])
    lru_d = dram("lru_w", [2, 2, 8, 64, 64])
    w_out_d = dram("w_out", [D, D])
    wgu_d = dram("wgu", [NJF, 128, 2048])
    wd_d = dram("w_down", [DFF, D])
    out_d = dram("out", [NSEQ, SEQ, D], F32, "ExternalOutput")
    dbg_d = {}

    S = Sched(nc)
    K = 1024
    with ExitStack() as st:
        arena = st.enter_context(nc.sbuf_tensor("arena", [128, 212800 // 4], F32))
        banks = [st.enter_context(nc.psum_tensor("bank%d" % i, [128, 512], F32)) for i in range(8)]
        PB = [S.buf("bank%d" % i, "ps", i, i + 1) for i in range(8)]

        R0 = Region(S, arena, 0, 19 * K)
        b_vecs, vecs = R0.alloc("vecs", [128, NV], F32)
        b_ident, ident = R0.alloc("ident", [128, 128], BF16)
        b_ones, ones = R0.alloc("ones", [128, 128], BF16)
        b_pmat, pmat = R0.alloc("pmat", [128, 32], F32)
        b_rope, rope = R0.alloc("rope", [128, 2, T], F32)
        b_lamc, lamc = R0.alloc("lamc", [128, 16], F32)
        b_lamt, lamt = R0.alloc("lamt", [128, 8], F32)

        def V(c, p0=0, p1=128):
            return vecs[p0:p1, c:c + 1]

        RA = Region(S, arena, 19 * K, 51 * K)
        b_ornT, ornT = RA.alloc("ornT", [128, 4, SEQ], BF16)
        b_oatT, oatT = RA.alloc("oatT", [128, 4, SEQ], BF16)
        RL3 = Region(S, arena, 51 * K, 80 * K)
        b_cqnT, cqnT = RL3.alloc("cqnT", [128, 3, T], BF16)
        b_ckvnT, ckvnT = RL3.alloc("ckvnT", [128, 2, T], BF16)
        b_krT, krT = RL3.alloc("krT", [128, T], F32)

        S.dma("sp", lambda e, f: f(e.dma_start(out=vecs, in_=vecs_d[:, :])), "c_vecs", writes=[b_vecs])
        S.dma("pool", lambda e, f: f(e.dma_start(out=ident, in_=cst_d[:, 0:128])), "c_id", writes=[b_ident])
        S.dma("pool", lambda e, f: f(e.dma_start(out=ones, in_=cst_d[:, 128:256])), "c_on", writes=[b_ones])
        S.dma("sp", lambda e, f: f(e.dma_start(out=pmat, in_=cst_d[:, 256:288])), "c_pm", writes=[b_pmat])
        S.dma("sp", lambda e, f: f(e.dma_start(out=rope[64:96, :, :], in_=rope_d[:, :, :])), "c_rope", writes=[b_rope])
        S.op("act", lambda e: e.activation(out=lamt, in_=vecs[:, V_LAM:V_LAM + 8], func=AF.Exp, scale=-1.0),
             reads=[b_vecs], writes=[b_lamt])
        S.op("act", lambda e: e.activation(out=lamt, in_=lamt, func=AF.Ln, bias=V(V_ONE), scale=1.0),
             reads=[b_vecs, b_lamt], writes=[b_lamt])
        S.op("dve", lambda e: e.tensor_scalar(out=lamc[:, 0:8], in0=lamt, scalar1=-8.0, scalar2=None, op0=ALU.mult),
             reads=[b_lamt], writes=[b_lamc])
        S.op("dve", lambda e: e.tensor_scalar(out=lamc[:, 8:16], in0=lamt, scalar1=-16.0, scalar2=None, op0=ALU.mult),
             reads=[b_lamt], writes=[b_lamc])

        def dump(name, b, ap, shape, dt=F32):
            if not KDEBUG:
                return
            dd = dram("dbg_" + name, shape, dt, "ExternalOutput")
            dbg_d[name] = dd
            S.dma("sp", lambda e, f: f(e.dma_start(out=dd, in_=ap)), "dbg_" + name, reads=[b], store=True)

        def rstd_fm(ps_ap, rs_ap, np_, n, inv_n, b_ps, b_rs):
            S.op("act", lambda e: e.activation(out=rs_ap, in_=ps_ap, func=AF.Ln, bias=V(V_EPS, 0, np_), scale=inv_n),
                 reads=[b_ps, b_vecs], writes=[b_rs])
            S.op("act", lambda e: e.activation(out=rs_ap, in_=rs_ap, func=AF.Exp, scale=-0.5),
                 reads=[b_rs], writes=[b_rs])

        try:
            for s in range(NSEQ):
                R1a = Region(S, arena, 19 * K, 51 * K)
                b_xt, xt = [], []
                for i in range(2):
                    b, a = R1a.alloc("xt%d" % i, [128, 4, D], F32)
                    b_xt.append(b); xt.append(a)
                R1 = Region(S, arena, 80 * K, 158 * K)
                b_win, win = R1.alloc("w_in", [128, 8, 1696], BF16)
                b_xs, xs = R1.alloc("xs", [128, 4, D], BF16)
                b_xnT, xnT = [], []
                for i in range(2):
                    b, a = R1.alloc("xnT%d" % i, [128, 8, 512], BF16)
                    b_xnT.append(b); xnT.append(a)
                b_latf, latf = R1.alloc("latf", [128, 3, 512], F32)
                b_sqb, sqb = R1.alloc("sqb", [128, 3, 512], BF16)
                b_rs, rs = R1.alloc("rs", [128, 512], F32)
                b_gt1, gt1 = R1.alloc("gt1", [128, 512], F32)
                b_gt2, gt2 = R1.alloc("gt2", [128, 512], F32)
                b_ss, ss = R1.alloc("ss", [128, 4], F32)
                b_rstd, rstd = R1.alloc("rstd", [128, 4], F32)
                b_xsB, xsB = R1.alloc("xsB", [128, 4, D], BF16)
                b_ssB, ssB = R1.alloc("ssB", [128, 4], F32)
                b_rstdB, rstdB = R1.alloc("rstdB", [128, 4], F32)
                scrs = [(b_xs, xs, b_ss, ss, b_rstd, rstd), (b_xsB, xsB, b_ssB, ssB, b_rstdB, rstdB)]
                R2 = Region(S, arena, 158 * K, 207 * K + 800)
                b_xr, xr = [], []
                for cc in range(4):
                    b, a = R2.alloc("xr%d" % cc, [128, T + 4], F32)
                    b_xr.append(b); xr.append(a)
                b_gg, gg = R2.alloc("gg", [128, 4, SEQ], BF16)

                def ld_win(e, f):
                    for kc in range(8):
                        f(e.dma_start(out=win[:, kc, :], in_=w_in_d[kc * 128:(kc + 1) * 128, :]))
                S.dma("pool", ld_win, "w_in", n=8, writes=[b_win])
                for cc in range(4):
                    S.op("dve", lambda e, cc=cc: e.memset(xr[cc][:, 0:2], 0.0), writes=[b_xr[cc]])
                    S.op("dve", lambda e, cc=cc: e.memset(xr[cc][:, T + 2:T + 4], 0.0), writes=[b_xr[cc]])

                chunks = [(0, NM, None)] + [(NM + 512 * c, 512, c) for c in range(4)]

                def ld_x(ci):
                    pos0, n, c = chunks[ci]
                    sl = ci % 2
                    if c is None:
                        S.dma("sp", lambda e, f: f(e.dma_start(out=xt[sl][0:NM, 0, :], in_=meta_d[:, :])),
                              "xt%d" % sl, writes=[b_xt[sl]])
                    else:
                        src = x_d[s, 512 * c:512 * c + 512, :].rearrange("(j p) f -> p j f", p=128)
                        S.dma("sp", lambda e, f: f(e.dma_start(out=xt[sl], in_=src)), "xt%d" % sl, writes=[b_xt[sl]])

                def norm_tm(xin, b_xin, np_, nt, gcol, dstT, b_dstT, tpb, part=0, scr=None):
                    n = 128 * nt if np_ == 128 else np_
                    b_xs_, xs_, b_ss_, ss_, b_rstd_, rstd_ = scr if scr is not None else (b_xs, xs, b_ss, ss, b_rstd, rstd)
                    if part in (0, 1):
                        norm_stats(xin, b_xin, np_, nt, b_xs_, xs_, b_ss_, ss_, b_rstd_, rstd_)
                    if part in (0, 2):
                        norm_tr(np_, nt, n, gcol, dstT, b_dstT, tpb, b_xs_, xs_)

                def norm_stats(xin, b_xin, np_, nt, b_xs, xs, b_ss, ss, b_rstd, rstd):
                    for j in range(nt):
                        S.op("act", lambda e, j=j: e.activation(out=xs[0:np_, j, :], in_=xin[0:np_, j, :], func=AF.Square),
                             reads=[b_xin], writes=[b_xs])
                    S.op("dve", lambda e: e.tensor_reduce(out=ss[0:np_, 0:nt], in_=xs[0:np_, 0:nt, :], axis=AX.X, op=ALU.add),
                         reads=[b_xs], writes=[b_ss])
                    S.op("act", lambda e: e.activation(out=rstd[0:np_, 0:nt], in_=ss[0:np_, 0:nt], func=AF.Ln,
                                                       bias=V(V_EPS, 0, np_), scale=1.0 / D),
                         reads=[b_ss, b_vecs], writes=[b_rstd])
                    S.op("act", lambda e: e.activation(out=rstd[0:np_, 0:nt], in_=rstd[0:np_, 0:nt], func=AF.Exp, scale=-0.5),
                         reads=[b_rstd], writes=[b_rstd])
                    for j in range(nt):
                        S.op("dve", lambda e, j=j: e.tensor_scalar(out=xs[0:np_, j, :], in0=xin[0:np_, j, :],
                                                                   scalar1=rstd[0:np_, j:j + 1], scalar2=None, op0=ALU.mult),
                             reads=[b_xin, b_rstd], writes=[b_xs])

                def norm_tr(np_, nt, n, gcol, dstT, b_dstT, tpb, b_xs, xs):
                    for kc in range(8):
                        bk = tpb[kc % 2]
                        tp = banks[bk][:, :].bitcast(BF16)

                        def tr(e, kc=kc, tp=tp):
                            ins = None
                            for j in range(nt):
                                w = np_
                                ins = e.transpose(out=tp[:, j * 128:j * 128 + w], in_=xs[0:np_, j, kc * 128:(kc + 1) * 128],
                                                  identity=ident[0:np_, 0:np_])
                            return ins
                        S.op("pe", tr, reads=[b_xs, b_ident], writes=[PB[bk]])
                        S.op("dve", lambda e, kc=kc, tp=tp: e.tensor_scalar(out=dstT[:, kc, 0:n], in0=tp[:, 0:n],
                                                                           scalar1=V(gcol + kc), scalar2=None, op0=ALU.mult),
                             reads=[PB[bk], b_vecs], writes=[b_dstT])

                def mm_acc(e, out_ap, lhs_list, rhs_list):
                    ins = None
                    n = len(lhs_list)
                    for i in range(n):
                        ins = e.matmul(out_ap, lhs_list[i], rhs_list[i], start=(i == 0), stop=(i == n - 1))
                    return ins

                mmb = [2, 3, 4, 5]
                mmi = [0]

                def next_bank():
                    b = mmb[mmi[0] % len(mmb)]
                    mmi[0] += 1
                    return b

                def p1_norm(ci, part):
                    _, _, c_ = chunks[ci]
                    norm_tm(xt[ci % 2], b_xt[ci % 2], 128 if c_ is not None else NM, 4 if c_ is not None else 1,
                            V_G1, xnT[ci % 2], b_xnT[ci % 2], (0, 1), part=part, scr=scrs[ci % 2])

                ld_x(0)
                ld_x(1)
                p1_norm(0, 0)
                for ci in range(5):
                    pos0, n, c = chunks[ci]
                    sl = ci % 2
                    np_ = 128 if c is not None else NM
                    nt = 4 if c is not None else 1
                    xT = xnT[sl]
                    bxT = b_xnT[sl]
                    if ci + 1 < 5:
                        p1_norm(ci + 1, 1)

                    def win_mm(col0, m, bk, p0=0):
                        S.op("pe", lambda e: mm_acc(e, banks[bk][p0:p0 + m, 0:n],
                                                    [win[:, kc, col0:col0 + m] for kc in range(8)],
                                                    [xT[:, kc, 0:n] for kc in range(8)]),
                             reads=[b_win, bxT], writes=[PB[bk]])

                    for (col0, ng, gcol, dst, b_dst, inv) in ((0, 3, V_GQA, cqnT, b_cqnT, 1.0 / 384),
                                                               (384, 2, V_GKVA, ckvnT, b_ckvnT, 1.0 / 256)):
                        for g in range(ng):
                            bk = next_bank()
                            win_mm(col0 + 128 * g, 128, bk)
                            S.op("act", lambda e, g=g, bk=bk: e.activation(out=sqb[:, g, 0:n], in_=banks[bk][:, 0:n], func=AF.Square),
                                 reads=[PB[bk]], writes=[b_sqb])
                            S.op("dve", lambda e, g=g, bk=bk: e.tensor_copy(out=latf[:, g, 0:n], in_=banks[bk][:, 0:n]),
                                 reads=[PB[bk]], writes=[b_latf])
                        S.op("pe", lambda e, ng=ng: mm_acc(e, banks[6][:, 0:n], [ones[:, :]] * ng,
                                                           [sqb[:, g, 0:n] for g in range(ng)]),
                             reads=[b_ones, b_sqb], writes=[PB[6]])
                        rstd_fm(banks[6][:, 0:n], rs[:, 0:n], 128, n, inv, PB[6], b_rs)
                        for g in range(ng):
                            S.op("dve", lambda e, g=g, dst=dst, gcol=gcol: e.scalar_tensor_tensor(
                                out=dst[:, g, pos0:pos0 + n], in0=latf[:, g, 0:n], scalar=V(gcol + g), in1=rs[:, 0:n],
                                op0=ALU.mult, op1=ALU.mult), reads=[b_latf, b_rs, b_vecs], writes=[b_dst])
                    if ci + 1 < 5:
                        p1_norm(ci + 1, 2)
                    if ci + 2 < 5:
                        ld_x(ci + 2)
                    bk = next_bank()
                    win_mm(640, 32, bk, p0=64)
                    S.op("act", lambda e, bk=bk: e.activation(out=krT[64:96, pos0:pos0 + n], in_=banks[bk][64:96, 0:n], func=AF.Copy),
                         reads=[PB[bk]], writes=[b_krT])
                    for cc in range(4):
                        bk = next_bank()
                        win_mm(672 + 128 * cc, 128, bk)
                        S.op("act", lambda e, cc=cc, bk=bk: e.activation(out=xr[cc][:, 2 + pos0:2 + pos0 + n], in_=banks[bk][:, 0:n], func=AF.Copy),
                             reads=[PB[bk]], writes=[b_xr[cc]])
                    if c is not None:
                        for cc in range(4):
                            bk = next_bank()
                            win_mm(1184 + 128 * cc, 128, bk)
                            S.op("act", lambda e, bk=bk: e.activation(out=gt1, in_=banks[bk][:, :], func=AF.Square),
                                 reads=[PB[bk]], writes=[b_gt1])
                            S.op("dve", lambda e: e.tensor_scalar(out=gt1, in0=gt1, scalar1=0.044715, scalar2=1.0,
                                                                  op0=ALU.mult, op1=ALU.add), reads=[b_gt1], writes=[b_gt1])
                            S.op("dve", lambda e, bk=bk: e.tensor_tensor(out=gt1, in0=banks[bk][:, :], in1=gt1, op=ALU.mult),
                                 reads=[PB[bk], b_gt1], writes=[b_gt1])
                            S.op("act", lambda e: e.activation(out=gt2, in_=gt1, func=AF.Sigmoid, scale=1.5957691216057308),
                                 reads=[b_gt1], writes=[b_gt2])
                            S.op("dve", lambda e, cc=cc, bk=bk, c=c: e.tensor_tensor(out=gg[:, cc, 512 * c:512 * c + 512],
                                                                                 in0=banks[bk][:, :], in1=gt2, op=ALU.mult),
                                 reads=[PB[bk], b_gt2], writes=[b_gg])
                if s == 0:
                    dump("cqnT", b_cqnT, cqnT, [128, 3, T], BF16)
                    dump("ckvnT", b_ckvnT, ckvnT, [128, 2, T], BF16)
                    dump("krT", b_krT, krT[64:96, :], [32, T])
                    dump("xr0", b_xr[0], xr[0], [128, T + 4])
                    dump("gg", b_gg, gg, [128, 4, SEQ], BF16)
                phase_end("p1")

                R4 = Region(S, arena, 80 * K, 158 * K)
                b_lw, lw = R4.alloc("lru_w", [128, 16, 128], BF16)
                b_xc, xc = R4.alloc("xc", [128, T], F32)
                RXB = Region(S, arena, 19 * K, 19 * K + T * 2 + 4)
                b_xcb, xcb = RXB.alloc("xcb", [128, T], BF16)
                lb = {}
                for nm in ("r0", "i0", "a0", "r1", "i1", "a1", "hf", "hb"):
                    lb[nm] = R4.alloc("l_" + nm, [128, T], F32)
                R4s = Region(S, arena, lb["r0"][0].lo, lb["r0"][0].hi)
                b_sq4, sq4 = R4s.alloc("sq4", [128, 4, 512], BF16)
                b_rsr, rsr = R4s.alloc("rsr", [128, 512], F32)
                S.op("dve", lambda e: e.memset(lw, 0.0), writes=[b_lw])

                def ld_lru(e, f):
                    for g in range(2):
                        for d in range(2):
                            src = lru_d[g, d].rearrange("(c b) i j -> b i c j", b=2)
                            k0 = (g * 2 + d) * 4
                            for bh in range(2):
                                f(e.dma_start(out=lw[bh * 64:(bh + 1) * 64, k0:k0 + 4, bh * 64:(bh + 1) * 64], in_=src[bh]))
                S.dma("pool", ld_lru, "lru_w", n=8, writes=[b_lw])

                pieces = [(512 * p, 512) for p in range(4)] + [(2048, 16)]
                for cc in range(4):
                    S.op("dve", lambda e, cc=cc: e.tensor_scalar(out=xc, in0=xr[cc][:, 0:T], scalar1=V(V_CW + cc * 4),
                                                                 scalar2=V(V_CB + cc), op0=ALU.mult, op1=ALU.add),
                         reads=[b_xr[cc], b_vecs], writes=[b_xc])
                    for j in range(1, 4):
                        S.op("dve", lambda e, cc=cc, j=j: e.scalar_tensor_tensor(out=xc, in0=xr[cc][:, j:j + T], scalar=V(V_CW + cc * 4 + j),
                                                                                in1=xc, op0=ALU.mult, op1=ALU.add),
                             reads=[b_xr[cc], b_vecs, b_xc], writes=[b_xc])
                    S.op("act", lambda e: e.activation(out=xcb, in_=xc, func=AF.Copy), reads=[b_xc], writes=[b_xcb])
                    for d in range(2):
                        b_r, r_ = lb["r%d" % d]; b_i, i_ = lb["i%d" % d]; b_a, a_ = lb["a%d" % d]
                        b_b, bb_ = b_i, i_
                        b_h, h_ = lb["hf"] if d == 0 else lb["hb"]
                        for (p0, pn) in pieces:
                            for g, (bdst, dst, bcol) in enumerate(((b_r, r_, V_BA), (b_i, i_, V_BI))):
                                bk = next_bank()
                                S.op("pe", lambda e, g=g, bk=bk, p0=p0, pn=pn, d=d, cc=cc: e.matmul(
                                    banks[bk][:, 0:pn], lw[:, (g * 2 + d) * 4 + cc, :], xcb[:, p0:p0 + pn], start=True, stop=True),
                                    reads=[b_lw, b_xcb], writes=[PB[bk]])
                                S.op("act", lambda e, bk=bk, p0=p0, pn=pn, dst=dst, bcol=bcol, d=d, cc=cc: e.activation(
                                    out=dst[:, p0:p0 + pn], in_=banks[bk][:, 0:pn], func=AF.Sigmoid,
                                    bias=V(bcol + d * 4 + cc), scale=1.0), reads=[PB[bk], b_vecs], writes=[bdst])
                        ci_ = d * 4 + cc
                        S.op("act", lambda e, ci_=ci_: e.activation(out=a_, in_=r_, func=AF.Exp, scale=lamc[:, ci_:ci_ + 1]),
                             reads=[b_r, b_lamc], writes=[b_a])
                        S.op("act", lambda e, ci_=ci_: e.activation(out=r_, in_=r_, func=AF.Exp, scale=lamc[:, 8 + ci_:9 + ci_]),
                             reads=[b_r, b_lamc], writes=[b_r])
                        S.op("act", lambda e: e.activation(out=r_, in_=r_, func=AF.Sqrt, bias=V(V_ONE), scale=-1.0),
                             reads=[b_r, b_vecs], writes=[b_r])
                        S.op("pool", lambda e: e.tensor_tensor(out=bb_, in0=i_, in1=xc, op=ALU.mult),
                             reads=[b_i, b_xc], writes=[b_b])
                        S.op("dve", lambda e: e.tensor_tensor(out=bb_, in0=bb_, in1=r_, op=ALU.mult),
                             reads=[b_b, b_r], writes=[b_b])
                        if d == 0:
                            S.op("dve", lambda e, h_=h_: e.tensor_tensor_scan(out=h_, data0=a_, data1=bb_, initial=0.0,
                                                                             op0=ALU.mult, op1=ALU.add),
                                 reads=[b_a, b_b], writes=[b_h])
                        else:
                            S.op("dve", lambda e, h_=h_: e.tensor_tensor_scan(out=h_[:, ::-1], data0=a_[:, ::-1], data1=bb_[:, ::-1],
                                                                             initial=0.0, op0=ALU.mult, op1=ALU.add),
                                 reads=[b_a, b_b], writes=[b_h])
                    b_hf, hf = lb["hf"]; b_hb, hb = lb["hb"]
                    S.op("pool", lambda e: e.tensor_tensor(out=hf[:, NM:T], in0=hf[:, NM:T], in1=hb[:, NM:T], op=ALU.add),
                         reads=[b_hf, b_hb], writes=[b_hf])
                    S.op("dve", lambda e, cc=cc: e.tensor_tensor(out=xr[cc][:, 2 + NM:2 + T], in0=hf[:, NM:T], in1=gg[:, cc, :], op=ALU.mult),
                         reads=[b_hf, b_gg], writes=[b_xr[cc]])
                for c in range(4):
                    c0 = 2 + NM + 512 * c
                    for cc in range(4):
                        S.op("act", lambda e, cc=cc, c0=c0: e.activation(out=sq4[:, cc, :], in_=xr[cc][:, c0:c0 + 512], func=AF.Square),
                             reads=[b_xr[cc]], writes=[b_sq4])
                    S.op("pe", lambda e: mm_acc(e, banks[6][:, :], [ones[:, :]] * 4, [sq4[:, cc, :] for cc in range(4)]),
                         reads=[b_ones, b_sq4], writes=[PB[6]])
                    rstd_fm(banks[6][:, :], rsr, 128, 512, 1.0 / 512, PB[6], b_rsr)
                    for cc in range(4):
                        S.op("dve", lambda e, cc=cc, c0=c0, c=c: e.scalar_tensor_tensor(
                            out=ornT[:, cc, 512 * c:512 * c + 512], in0=xr[cc][:, c0:c0 + 512], scalar=V(V_GR + cc), in1=rsr,
                            op0=ALU.mult, op1=ALU.mult), reads=[b_xr[cc], b_rsr, b_vecs], writes=[b_ornT])
                if s == 0:
                    dump("ornT", b_ornT, ornT, [128, 4, SEQ], BF16)
                phase_end("p2")

                R6 = Region(S, arena, 80 * K, 207 * K + 800)
                b_wkn, wkn = R6.alloc("w_kn", [128, 2, 512], BF16)
                b_wv, wv = R6.alloc("w_v", [128, 2, 512], BF16)
                b_wuq, wuq = R6.alloc("w_uq", [128, 3, 768], BF16)
                b_KT, KT = [], []
                for h in range(NH):
                    b, a = R6.alloc("KT%d" % h, [128, T], BF16)
                    b_KT.append(b); KT.append(a)
                b_va, va = R6.alloc("vaug", [128, 17, NH, 128], BF16)
                b_QT, QT = [], []
                for i in range(2):
                    b, a = R6.alloc("QT%d" % i, [128, NH, 512], BF16)
                    b_QT.append(b); QT.append(a)
                b_PT, PT = [], []
                for i in range(4):
                    b, a = R6.alloc("PT%d" % i, [128, 512], BF16)
                    b_PT.append(b); PT.append(a)
                b_oraw, oraw = [], []
                for i in range(4):
                    b, a = R6.alloc("oraw%d" % i, [128, 512], F32)
                    b_oraw.append(b); oraw.append(a)
                b_sqk, sqk = [], []
                for i in range(2):
                    b, a = R6.alloc("sqk%d" % i, [128, 512], BF16)
                    b_sqk.append(b); sqk.append(a)
                b_rden, rden = [], []
                for i in range(2):
                    b, a = R6.alloc("rden%d" % i, [128, 512], F32)
                    b_rden.append(b); rden.append(a)
                b_xk, xk = R6.alloc("xk", [128, 512], F32)
                b_t1, t1 = R6.alloc("t1", [128, 512], F32)
                b_t2, t2 = R6.alloc("t2", [128, 512], F32)
                b_krp, krp = R6.alloc("krp", [128, 512], F32)
                b_rec, rec = R6.alloc("rec", [128, 512], F32)
                b_sqa, sqa = R6.alloc("sqa", [128, 4, 512], BF16)
                b_rsa, rsa = R6.alloc("rsa", [128, 512], F32)

                def ld_kvw(e, f):
                    for kc in range(2):
                        f(e.dma_start(out=wkn[:, kc, :], in_=w_kn_d[kc * 128:(kc + 1) * 128, :]))
                        f(e.dma_start(out=wv[:, kc, :], in_=w_v_d[kc * 128:(kc + 1) * 128, :]))
                S.dma("pool", ld_kvw, "w_kv", n=4, writes=[b_wkn, b_wv])

                def ld_uq(e, f):
                    for kc in range(3):
                        f(e.dma_start(out=wuq[:, kc, :], in_=w_uq_d[kc * 128:(kc + 1) * 128, :]))
                S.dma("pool", ld_uq, "w_uq", n=3, writes=[b_wuq])
                S.op("pool", lambda e: e.memset(va, 1.0), writes=[b_va])

                def rope_rows(src_ps_or_sb, b_src, gcol, cols0, n, dst, b_dst, rd, b_rd, bk_px):
                    S.op("dve", lambda e: e.tensor_scalar(out=xk[64:96, 0:n], in0=src_ps_or_sb, scalar1=V(gcol, 64, 96),
                                                          scalar2=None, op0=ALU.mult), reads=[b_src, b_vecs], writes=[b_xk])
                    S.op("pe", lambda e: e.matmul(banks[bk_px][64:96, 0:n], pmat[64:96, 0:32], xk[64:96, 0:n], start=True, stop=True),
                         reads=[b_pmat, b_xk], writes=[PB[bk_px]])
                    S.op("dve", lambda e: e.tensor_tensor(out=t1[64:96, 0:n], in0=xk[64:96, 0:n], in1=rope[64:96, 0, cols0:cols0 + n], op=ALU.mult),
                         reads=[b_xk, b_rope], writes=[b_t1])
                    S.op("dve", lambda e: e.tensor_tensor(out=t2[64:96, 0:n], in0=banks[bk_px][64:96, 0:n], in1=rope[64:96, 1, cols0:cols0 + n], op=ALU.mult),
                         reads=[PB[bk_px], b_rope], writes=[b_t2])
                    S.op("dve", lambda e: e.tensor_tensor(out=t1[64:96, 0:n], in0=t1[64:96, 0:n], in1=t2[64:96, 0:n], op=ALU.add),
                         reads=[b_t1, b_t2], writes=[b_t1])
                    if rd is None:
                        S.op("dve", lambda e: e.tensor_copy(out=dst, in_=t1[64:96, 0:n]), reads=[b_t1], writes=[b_dst])
                    else:
                        S.op("dve", lambda e: e.tensor_tensor(out=dst, in0=t1[64:96, 0:n], in1=rd, op=ALU.mult),
                             reads=[b_t1, b_rd], writes=[b_dst])

                for ci in range(5):
                    pos0, n, c = chunks[ci]
                    rope_rows(krT[64:96, pos0:pos0 + n], b_krT, V_KG, pos0, n, krp[64:96, 0:n], b_krp, None, None, 7)
                    for i in range(2):
                        S.op("act", lambda e, i=i: e.activation(out=sqk[i][64:96, 0:n], in_=krT[64:96, pos0:pos0 + n], func=AF.Square),
                             reads=[b_krT], writes=[b_sqk[i]])
                    for h in range(NH):
                        bk = next_bank()
                        sl = h % 2
                        S.op("pe", lambda e, h=h, bk=bk: mm_acc(e, banks[bk][0:64, 0:n],
                                                               [wkn[:, kc, h * 64:(h + 1) * 64] for kc in range(2)],
                                                               [ckvnT[:, kc, pos0:pos0 + n] for kc in range(2)]),
                             reads=[b_wkn, b_ckvnT], writes=[PB[bk]])
                        S.op("act", lambda e, bk=bk, sl=sl: e.activation(out=sqk[sl][0:64, 0:n], in_=banks[bk][0:64, 0:n], func=AF.Square),
                             reads=[PB[bk]], writes=[b_sqk[sl]])
                        S.op("pe", lambda e, sl=sl: e.matmul(banks[6][0:96, 0:n], ones[0:96, 0:96], sqk[sl][0:96, 0:n], start=True, stop=True),
                             reads=[b_ones, b_sqk[sl]], writes=[PB[6]])
                        rstd_fm(banks[6][0:96, 0:n], rden[sl][0:96, 0:n], 96, n, 1.0 / 96, PB[6], b_rden[sl])
                        S.op("dve", lambda e, h=h, bk=bk, sl=sl: e.scalar_tensor_tensor(
                            out=KT[h][0:64, pos0:pos0 + n], in0=banks[bk][0:64, 0:n], scalar=V(V_KG, 0, 64), in1=rden[sl][0:64, 0:n],
                            op0=ALU.mult, op1=ALU.mult), reads=[PB[bk], b_rden[sl], b_vecs], writes=[b_KT[h]])
                        S.op("dve", lambda e, h=h, sl=sl: e.tensor_tensor(out=KT[h][64:96, pos0:pos0 + n], in0=krp[64:96, 0:n],
                                                                         in1=rden[sl][64:96, 0:n], op=ALU.mult),
                             reads=[b_krp, b_rden[sl]], writes=[b_KT[h]])
                    ntile = 1 if c is None else 4
                    for j in range(ntile):
                        kt = 0 if c is None else 1 + 4 * c + j
                        npk = NM if c is None else 128
                        bk = next_bank()
                        S.op("pe", lambda e, j=j, bk=bk, npk=npk: mm_acc(e, banks[bk][0:npk, :],
                                                                        [ckvnT[:, kc, pos0 + 128 * j:pos0 + 128 * j + npk] for kc in range(2)],
                                                                        [wv[:, kc, :] for kc in range(2)]),
                             reads=[b_ckvnT, b_wv], writes=[PB[bk]])
                        for par in range(2):
                            src = banks[bk][0:npk, :].rearrange("p (a b d) -> p a b d", a=4, b=2)[:, :, par, :]
                            S.op("act", lambda e, kt=kt, par=par, src=src, npk=npk: e.activation(
                                out=va[0:npk, kt, par::2, par * 64:par * 64 + 64], in_=src, func=AF.Copy),
                                reads=[PB[bk]], writes=[b_va])
                if s == 0:
                    dump("KT0", b_KT[0], KT[0][0:96, :], [96, T], BF16)
                    dump("KT3", b_KT[3], KT[3][0:96, :], [96, T], BF16)
                    dump("vaug", b_va, va, [128, 17, NH, 128], BF16)
                phase_end("kv")

                def qprep_stages(c, h):
                    sl = c % 2
                    pos0 = NM + 512 * c
                    bk = 4
                    sk = h % 2
                    n = 512

                    def st0():
                        S.op("pe", lambda e: mm_acc(e, banks[bk][0:96, :],
                                                    [wuq[:, kc, h * 96:(h + 1) * 96] for kc in range(3)],
                                                    [cqnT[:, kc, pos0:pos0 + 512] for kc in range(3)]),
                             reads=[b_wuq, b_cqnT], writes=[PB[bk]])

                    def st1():
                        S.op("act", lambda e: e.activation(out=sqk[sk][0:96, :], in_=banks[bk][0:96, :], func=AF.Square),
                             reads=[PB[bk]], writes=[b_sqk[sk]])

                    def st2():
                        S.op("pe", lambda e: e.matmul(banks[6][0:96, :], ones[0:96, 0:96], sqk[sk][0:96, :], start=True, stop=True),
                             reads=[b_ones, b_sqk[sk]], writes=[PB[6]])

                    def st3():
                        rstd_fm(banks[6][0:96, :], rden[sk][0:96, :], 96, 512, 1.0 / 96, PB[6], b_rden[sk])

                    def st4():
                        S.op("dve", lambda e: e.scalar_tensor_tensor(
                            out=QT[sl][0:64, h, :], in0=banks[bk][0:64, :], scalar=V(V_QG, 0, 64), in1=rden[sk][0:64, :],
                            op0=ALU.mult, op1=ALU.mult), reads=[PB[bk], b_rden[sk], b_vecs], writes=[b_QT[sl]])
                        S.op("dve", lambda e: e.tensor_scalar(out=xk[64:96, 0:n], in0=banks[bk][64:96, :], scalar1=V(V_QG, 64, 96),
                                                              scalar2=None, op0=ALU.mult), reads=[PB[bk], b_vecs], writes=[b_xk])

                    def st5():
                        S.op("pe", lambda e: e.matmul(banks[7][64:96, 0:n], pmat[64:96, 0:32], xk[64:96, 0:n], start=True, stop=True),
                             reads=[b_pmat, b_xk], writes=[PB[7]])

                    def st6():
                        S.op("dve", lambda e: e.tensor_tensor(out=t1[64:96, 0:n], in0=xk[64:96, 0:n], in1=rope[64:96, 0, pos0:pos0 + n], op=ALU.mult),
                             reads=[b_xk, b_rope], writes=[b_t1])
                        S.op("dve", lambda e: e.tensor_tensor(out=t2[64:96, 0:n], in0=banks[7][64:96, 0:n], in1=rope[64:96, 1, pos0:pos0 + n], op=ALU.mult),
                             reads=[PB[7], b_rope], writes=[b_t2])
                        S.op("dve", lambda e: e.tensor_tensor(out=t1[64:96, 0:n], in0=t1[64:96, 0:n], in1=t2[64:96, 0:n], op=ALU.add),
                             reads=[b_t1, b_t2], writes=[b_t1])
                        S.op("dve", lambda e: e.tensor_tensor(out=QT[sl][64:96, h, :], in0=t1[64:96, 0:n], in1=rden[sk][64:96, :], op=ALU.mult),
                             reads=[b_t1, b_rden[sk]], writes=[b_QT[sl]])
                    return [st0, st1, st2, st3, st4, st5, st6]

                def qprep_head(c, h):
                    for st in qprep_stages(c, h):
                        st()

                ktiles = [(0, NM)] + [(NM + 128 * i, 128) for i in range(16)]
                xcb_junk = va[:, 1, 0:4, :].rearrange('p a b -> p (a b)')
                sbk = [0, 1]
                JUNK = int(os.environ.get('KJUNK', '192'))
                obk = [2, 3]
                scale = 96.0 ** -0.5
                LAG = int(os.environ.get('KLAG', '2'))
                INTER = os.environ.get('KINTER', '1') == '1'

                pend = []

                def attention(c, inter=None):
                    sl = c % 2
                    steps = [(h, kt) for h in range(NH) for kt in range(17)]
                    nst = len(steps)

                    def emit_S(i):
                        h, kt = steps[i]
                        if inter is not None:
                            inter(h, kt)
                        if h == 0 and kt in (2, 4, 6) and pend:
                            pend.pop(0)()
                        k0, nk = ktiles[kt]
                        sb_ = sbk[i % 2]
                        pt = i % 4
                        S.op("pe", lambda e: e.matmul(banks[sb_][0:nk, :], KT[h][0:96, k0:k0 + nk], QT[sl][0:96, h, :],
                                                      start=True, stop=True),
                             reads=[b_KT[h], b_QT[sl]], writes=[PB[sb_]])
                        S.op("act", lambda e: e.activation(out=PT[pt][0:nk, :], in_=banks[sb_][0:nk, :], func=AF.Exp, scale=scale),
                             reads=[PB[sb_]], writes=[b_PT[pt]])

                    def emit_PV(i):
                        h, kt = steps[i]
                        k0, nk = ktiles[kt]
                        pt = i % 4
                        ob = obk[h % 2]
                        par = h % 2
                        S.op("pe", lambda e: e.matmul(banks[ob][:, :], va[0:nk, kt, h, :], PT[pt][0:nk, :],
                                                      start=(kt == 0), stop=(kt == 16)),
                             reads=[b_va, b_PT[pt]], writes=[PB[ob]])
                        if kt == 16:
                            own = slice(0, 64) if par == 0 else slice(64, 128)
                            oth = slice(64, 128) if par == 0 else slice(0, 64)
                            pr = h // 2
                            S.op("dve", lambda e: e.reciprocal(out=rec[own, :], in_=banks[ob][oth, :]),
                                 reads=[PB[ob]], writes=[b_rec])
                            S.op("dve", lambda e: e.tensor_tensor(out=oraw[pr][own, :], in0=banks[ob][own, :],
                                                                  in1=rec[own, :], op=ALU.mult),
                                 reads=[PB[ob], b_rec], writes=[b_oraw[pr]])

                    for i in range(nst + LAG):
                        if i < nst:
                            emit_S(i)
                        if JUNK:
                            S.op("pe", lambda e: e.matmul(banks[5][:, 0:JUNK], ones[:, :], xcb_junk[:, 0:JUNK], start=True, stop=True),
                                 reads=[b_ones], writes=[PB[5]])
                        if i >= LAG:
                            emit_PV(i - LAG)
                    def t0():
                        for pr in range(4):
                            S.op("act", lambda e, pr=pr: e.activation(out=sqa[:, pr, :], in_=oraw[pr], func=AF.Square),
                                 reads=[b_oraw[pr]], writes=[b_sqa])

                    def t1():
                        S.op("pe", lambda e: mm_acc(e, banks[6][:, :], [ones[:, :]] * 4, [sqa[:, pr, :] for pr in range(4)]),
                             reads=[b_ones, b_sqa], writes=[PB[6]])

                    def t2():
                        rstd_fm(banks[6][:, :], rsa, 128, 512, 1.0 / 512, PB[6], b_rsa)

                    def t3():
                        for pr in range(4):
                            S.op("dve", lambda e, pr=pr: e.scalar_tensor_tensor(
                                out=oatT[:, pr, 512 * c:512 * c + 512], in0=oraw[pr], scalar=V(V_GA + pr), in1=rsa,
                                op0=ALU.mult, op1=ALU.mult), reads=[b_oraw[pr], b_rsa, b_vecs], writes=[b_oatT])
                    return [t0, lambda: (t1(), t2()), t3]

                mmb[:] = [4]
                for h in range(NH):
                    qprep_head(0, h)
                for c in range(4):
                    if INTER:
                        if c + 1 < 4:
                            stg = {h: qprep_stages(c + 1, h) for h in range(NH)}
                            tl = attention(c, lambda h, kt, stg=stg: stg[h][(kt - 1) // 2]() if (kt % 2 == 1 and kt < 15) else None)
                        else:
                            tl = attention(c, None)
                        pend.extend(tl)
                        if c == 3:
                            while pend:
                                pend.pop(0)()
                    else:
                        if c + 1 < 4:
                            for h in range(NH):
                                qprep_head(c + 1, h)
                        for t_ in attention(c, None):
                            t_()
                mmb[:] = [2, 3, 4, 5]
                if s == 0:
                    dump("oatT", b_oatT, oatT, [128, 4, SEQ], BF16)
                phase_end("att")

                R8 = Region(S, arena, 51 * K, 207 * K + 800)
                b_wo, wo = R8.alloc("w_out", [128, 8, D], BF16)
                b_wd, wd = R8.alloc("w_down", [128, NJF, D], BF16)
                b_ring, ring = [], []
                for i in range(RING):
                    b, a = R8.alloc("ring%d" % i, [128, 2, 8, 128], BF16)
                    b_ring.append(b); ring.append(a)
                b_aT, aT = R8.alloc("aT", [128, NJF, 512], BF16)
                b_hnT, hnT = R8.alloc("hnT", [128, 8, 512], BF16)
                b_xh, xh = R8.alloc("xh", [128, 4, D], F32)
                b_xs5, xs5 = R8.alloc("xs5", [128, 4, D], BF16)
                b_sg, sg = [], []
                for i in range(2):
                    b, a = R8.alloc("sg%d" % i, [128, 512], F32)
                    b_sg.append(b); sg.append(a)
                b_ss5, ss5 = R8.alloc("ss5", [128, 4], F32)
                b_rstd5, rstd5 = R8.alloc("rstd5", [128, 4], F32)
                b_xs, xs, b_ss, ss, b_rstd, rstd = b_xs5, xs5, b_ss5, ss5, b_rstd5, rstd5

                def ld_wo(e, f):
                    for kc in range(8):
                        f(e.dma_start(out=wo[:, kc, :], in_=w_out_d[kc * 128:(kc + 1) * 128, :]))
                S.dma("pool", ld_wo, "w_out", n=8, writes=[b_wo])

                for part, (j0, j1) in enumerate(((0, 8), (8, 16), (16, NJF))):
                    def ld_wd(e, f, j0=j0, j1=j1):
                        for jf in range(j0, j1):
                            f(e.dma_start(out=wd[:, jf, :], in_=wd_d[jf * 128:(jf + 1) * 128, :]))
                    S.dma("pool", ld_wd, "w_down%d" % part, writes=[b_wd])

                ring_i = [0]

                def ld_ring(jf):
                    sl = ring_i[0] % RING
                    ring_i[0] += 1
                    dst = ring[sl].rearrange("p a b c -> p (a b c)")
                    S.dma("pool", lambda e, f: f(e.dma_start(out=dst, in_=wgu_d[jf])), "ring%d" % sl, writes=[b_ring[sl]])
                    return sl

                PRE = RING - 1
                for c in range(4):
                    src = x_d[s, 512 * c:512 * c + 512, :].rearrange("(j p) f -> p j f", p=128)
                    S.dma("sp", lambda e, f, src=src: f(e.dma_start(out=xh, in_=src)), "xh", writes=[b_xh])
                    slots = {}
                    for jf in range(PRE):
                        slots[jf] = ld_ring(jf)
                    for j in range(4):
                        for half in range(2):
                            bk = half
                            lhs = [oatT[:, kc, 512 * c + 128 * j:512 * c + 128 * j + 128] for kc in range(4)] + \
                                  [ornT[:, kc, 512 * c + 128 * j:512 * c + 128 * j + 128] for kc in range(4)]
                            rhs = [wo[:, kc, half * 512:(half + 1) * 512] for kc in range(8)]
                            S.op("pe", lambda e, bk=bk, lhs=lhs, rhs=rhs: mm_acc(e, banks[bk][:, :], lhs, rhs),
                                 reads=[b_oatT, b_ornT, b_wo], writes=[PB[bk]])
                            S.op("dve", lambda e, bk=bk, j=j, half=half: e.tensor_tensor(
                                out=xh[:, j, half * 512:(half + 1) * 512], in0=banks[bk][:, :], in1=xh[:, j, half * 512:(half + 1) * 512],
                                op=ALU.add), reads=[PB[bk], b_xh], writes=[b_xh])
                    if s == 0 and c == 0:
                        dump("h0", b_xh, xh, [128, 4, D])
                    norm_tm(xh, b_xh, 128, 4, V_G2, hnT, b_hnT, (2, 3))
                    for jf in range(NJF):
                        if jf + PRE < NJF:
                            slots[jf + PRE] = ld_ring(jf + PRE)
                        sl = slots[jf]
                        gb = 4 + jf % 2
                        ub = 6 + jf % 2
                        S.op("pe", lambda e, sl=sl, gb=gb: mm_acc(e, banks[gb][:, :], [ring[sl][:, 0, kc, :] for kc in range(8)],
                                                                 [hnT[:, kc, :] for kc in range(8)]),
                             reads=[b_ring[sl], b_hnT], writes=[PB[gb]])
                        S.op("pe", lambda e, sl=sl, ub=ub: mm_acc(e, banks[ub][:, :], [ring[sl][:, 1, kc, :] for kc in range(8)],
                                                                 [hnT[:, kc, :] for kc in range(8)]),
                             reads=[b_ring[sl], b_hnT], writes=[PB[ub]])
                        S.op("act", lambda e, gb=gb, jf=jf: e.activation(out=sg[jf % 2], in_=banks[gb][:, :], func=AF.Silu),
                             reads=[PB[gb]], writes=[b_sg[jf % 2]])
                        S.op("dve", lambda e, ub=ub, jf=jf: e.tensor_tensor(out=aT[:, jf, :], in0=banks[ub][:, :], in1=sg[jf % 2], op=ALU.mult),
                             reads=[PB[ub], b_sg[jf % 2]], writes=[b_aT])
                    for j in range(4):
                        for half in range(2):
                            bk = half
                            S.op("pe", lambda e, bk=bk, j=j, half=half: mm_acc(
                                e, banks[bk][:, :], [aT[:, jf, 128 * j:128 * j + 128] for jf in range(NJF)],
                                [wd[:, jf, half * 512:(half + 1) * 512] for jf in range(NJF)]),
                                reads=[b_aT, b_wd], writes=[PB[bk]])
                            S.op("dve", lambda e, bk=bk, j=j, half=half: e.tensor_tensor(
                                out=xh[:, j, half * 512:(half + 1) * 512], in0=banks[bk][:, :], in1=xh[:, j, half * 512:(half + 1) * 512],
                                op=ALU.add), reads=[PB[bk], b_xh], writes=[b_xh])
                    dst = out_d[s, 512 * c:512 * c + 512, :].rearrange("(j p) f -> p j f", p=128)
                    S.dma("sp", lambda e, f, dst=dst: f(e.dma_start(out=dst, in_=xh)), "xh_st", reads=[b_xh], store=True)
        except _Stop:
            pass

        S.emit()
    return nc, dbg_d


def _host_layout(inp):
    f = lambda a: np.ascontiguousarray(np.asarray(a, dtype=np.float32))
    vecs = np.zeros((128, NV), np.float32)
    col = lambda v, n: f(v).reshape(n, 128).T
    vecs[:, V_G1:V_G1 + 8] = col(inp["ln1_g"][0], 8)
    vecs[:, V_GQA:V_GQA + 3] = col(inp["q_a_norm_g"][0], 3)
    vecs[:, V_GKVA:V_GKVA + 2] = col(inp["kv_a_norm_g"][0], 2)
    vecs[0:96, V_QG] = f(inp["q_norm_g"][0])
    vecs[0:96, V_KG] = f(inp["k_norm_g"][0])
    cw = f(inp["conv_w"][0])
    for cc in range(4):
        for j in range(4):
            vecs[:, V_CW + cc * 4 + j] = cw[j, cc * 128:(cc + 1) * 128]
    vecs[:, V_CB:V_CB + 4] = col(inp["conv_b"][0], 4)
    for d in range(2):
        vecs[:, V_BA + d * 4:V_BA + d * 4 + 4] = col(inp["lru_ba"][0, d], 4)
        vecs[:, V_BI + d * 4:V_BI + d * 4 + 4] = col(inp["lru_bi"][0, d], 4)
        vecs[:, V_LAM + d * 4:V_LAM + d * 4 + 4] = col(inp["lru_lambda"][0, d], 4)
    vecs[:, V_GA:V_GA + 4] = col(inp["attn_out_g"][0], 4)
    vecs[:, V_GR:V_GR + 4] = col(inp["rnn_out_g"][0], 4)
    vecs[:, V_G2:V_G2 + 8] = col(inp["ln2_g"][0], 8)
    vecs[:, V_EPS] = EPS
    vecs[:, V_ONE] = 1.0
    cst = np.zeros((128, 288), np.float32)
    cst[:, 0:128] = np.eye(128, dtype=np.float32)
    cst[:, 128:256] = 1.0
    pm = np.zeros((32, 32), np.float32)
    for m in range(16):
        pm[m + 16, m] = -1.0
        pm[m, m + 16] = 1.0
    cst[64:96, 256:288] = pm
    half = 16
    freqs = (1.0 / (np.float32(10000.0) ** (np.arange(half, dtype=np.float32) / np.float32(half)))).astype(np.float32)
    ang = (np.arange(T, dtype=np.float32)[:, None] * freqs[None, :]).astype(np.float32)
    cos = np.cos(ang).astype(np.float32).T
    sin = np.sin(ang).astype(np.float32).T
    rope = np.zeros((32, 2, T), np.float32)
    rope[0:16, 0] = cos; rope[16:32, 0] = cos
    rope[0:16, 1] = sin; rope[16:32, 1] = sin
    w_ukv = f(inp["w_ukv"][0]).reshape(256, NH, 128)
    w_kn = np.ascontiguousarray(w_ukv[:, :, 0:64].reshape(256, 512))
    w_v = np.ascontiguousarray(w_ukv[:, :, 64:128].reshape(256, 512))
    lru_w = np.ascontiguousarray(np.stack([f(inp["lru_wa"][0]), f(inp["lru_wi"][0])], axis=0))
    wg = f(inp["w_gate"][0]).reshape(8, 128, NJF, 128)
    wu = f(inp["w_up"][0]).reshape(8, 128, NJF, 128)
    wgu = np.ascontiguousarray(np.stack([wg, wu], axis=0).transpose(3, 2, 0, 1, 4).reshape(NJF, 128, 2048))
    shared = {
        "meta": f(inp["meta_tokens"]), "vecs": vecs, "cst": cst, "rope": rope,
        "w_in": f(inp["w_in"][0]), "w_uq": f(inp["w_uq"][0]), "w_kn": w_kn, "w_v": w_v, "lru_w": lru_w,
        "w_out": f(inp["w_out"][0]), "wgu": wgu, "w_down": f(inp["w_down"][0]),
    }
    return shared


_CACHE = {}


def kernel(**inputs):
    x = np.asarray(inputs["x"], dtype=np.float32)
    shared = _host_layout(inputs)
    if "nc" not in _CACHE:
        _CACHE["nc"] = build()
    nc, dbg = _CACHE["nc"]
    in_maps = []
    for i in range(NCORES):
        m = dict(shared)
        m["x"] = np.ascontiguousarray(x[NSEQ * i:NSEQ * (i + 1)])
        in_maps.append(m)
    res = run_bass_kernel_spmd(nc, in_maps, core_ids=list(range(NCORES)))
    if KDEBUG:
        _CACHE["dbg"] = {k: np.asarray(res.results[0]["dbg_" + k]) for k in dbg}
    out = np.concatenate([np.asarray(res.results[i]["out"]) for i in range(NCORES)], axis=0)
    return out.astype(np.float32)
```

```python
import os
from contextlib import ExitStack
import numpy as np
import concourse.bass as bass
import concourse.mybir as mybir
from concourse.bass_utils import run_bass_kernel_spmd

F32 = mybir.dt.float32
BF16 = mybir.dt.bfloat16
AF = mybir.ActivationFunctionType
ALU = mybir.AluOpType
AX = mybir.AxisListType

NCORES = 8
NSEQ = 2
D = 1024
SEQ = 2048
NM = 16
T = SEQ + NM
NH = 8
DFF = 2816
NJF = DFF // 128
EPS = 1e-6
RING = 5
KDEBUG = os.environ.get("KDEBUG", "") != ""
KSTOP = os.environ.get("KSTOP", "")
MAX_SWDGE = int(os.environ.get('KMAXDMA', '10'))
SAME_ENG_SYNC = os.environ.get("KNOSAME", "") == ""


class _Stop(Exception):
    pass


def phase_end(name):
    if KSTOP == name:
        raise _Stop()

V_G1, V_GQA, V_GKVA, V_QG, V_KG, V_CW, V_CB, V_BA, V_BI, V_LAM, V_GA, V_GR, V_G2, V_EPS, V_ONE = (
    0, 8, 11, 13, 14, 15, 31, 35, 43, 51, 59, 63, 67, 75, 76)
NV = 80


class Buf:
    def __init__(self, name, space, lo, hi):
        self.name, self.space, self.lo, self.hi = name, space, lo, hi
        self.w = None
        self.r = []
        self.ov = [self]


class Op:
    __slots__ = ("eng", "calls", "dma", "ndma", "deps", "need", "cnt", "sem", "id")


class _Rec:
    def __init__(self):
        self.calls = []

    def __getattr__(self, name):
        def m(*a, **k):
            self.calls.append((name, a, k))
            return self
        return m


class Sched:
    ENGS = ("pe", "act", "dve", "pool", "sp")

    def __init__(self, nc):
        self.nc = nc
        self.bufs = []
        self.ops = []
        self.dma_keys = {}
        self.store_ops = []

    def buf(self, name, space, lo, hi):
        b = Buf(name, space, lo, hi)
        for y in self.bufs:
            if y.space == space and y.lo < hi and lo < y.hi:
                y.ov.append(b)
                b.ov.append(y)
        self.bufs.append(b)
        return b

    def _rec(self, op, reads, writes):
        deps = set()
        for b in reads:
            for y in b.ov:
                if y.w is not None:
                    deps.add(y.w)
                if b.space == "ps":
                    deps.update(o for o in y.r if o.eng != op.eng)
        for b in writes:
            for y in b.ov:
                if y.w is not None:
                    deps.add(y.w)
                deps.update(y.r)
        deps.discard(op)
        op.deps = deps
        for b in reads:
            if not op.dma:
                b.r = [o for o in b.r if o.dma or o.eng != op.eng]
            b.r.append(op)
        for b in writes:
            b.w = op
            b.r = []
            for y in b.ov:
                if y is not b and y.lo >= b.lo and y.hi <= b.hi:
                    y.w = op
                    y.r = []
        op.id = len(self.ops)
        self.ops.append(op)

    def op(self, eng, fn, reads=(), writes=()):
        o = Op()
        r = _Rec()
        fn(r)
        assert r.calls
        o.eng, o.calls, o.dma, o.ndma, o.need, o.cnt, o.sem = eng, r.calls, False, 0, False, 0, None
        self._rec(o, list(reads), list(writes))
        return o

    def dma(self, queue, fn, key, n=1, reads=(), writes=(), store=False):
        o = Op()
        r = _Rec()
        fn(r, lambda ins: ins)
        n = len(r.calls)
        assert n >= 1
        o.eng, o.calls, o.dma, o.ndma, o.need = queue, r.calls, True, n, True
        c = self.dma_keys.setdefault(key, [0])
        c[0] += n
        o.cnt, o.sem = c[0], key
        self._rec(o, list(reads), list(writes))
        if store:
            self.store_ops.append(o)
        return o

    def emit(self):
        nc = self.nc
        for o in self.ops:
            for d in o.deps:
                if d.dma:
                    continue
                if o.dma or d.eng != o.eng or (SAME_ENG_SYNC and o.eng != "pe"):
                    d.need = True
        per = {e: [] for e in self.ENGS}
        for o in self.ops:
            per[o.eng].append(o)
        for e in self.ENGS:
            c = 0
            for o in per[e]:
                if not o.dma and o.need:
                    c += 1
                    o.cnt = c
        with ExitStack() as st:
            esem = {e: st.enter_context(nc.semaphore("s_" + e)) for e in self.ENGS}
            dsem = {k: st.enter_context(nc.semaphore("d_%d" % i)) for i, k in enumerate(self.dma_keys)}
            block = st.enter_context(nc.Block())
            engobj = {"pe": block.tensor, "act": block.scalar, "dve": block.vector,
                      "pool": block.gpsimd, "sp": block.sync}

            def run(ename):
                def body(eng):
                    waited = {}
                    inflight = []

                    def wait(sem, key, val):
                        if waited.get(key, 0) < val:
                            eng.wait_ge(sem, val)
                            waited[key] = val

                    for o in per[ename]:
                        need = {}
                        for d in o.deps:
                            if d.dma:
                                k = ("d", d.sem)
                                need[k] = max(need.get(k, 0), 16 * d.cnt)
                            elif d.eng != ename or o.dma or (SAME_ENG_SYNC and ename != "pe"):
                                k = ("e", d.eng)
                                need[k] = max(need.get(k, 0), d.cnt)
                        for k, v in need.items():
                            wait(dsem[k[1]] if k[0] == "d" else esem[k[1]], k, v)
                        ins = None
                        if o.dma and ename == "pool":
                            while inflight and sum(x[2] for x in inflight) + o.ndma > MAX_SWDGE:
                                ks, cv, _ = inflight.pop(0)
                                wait(dsem[ks], ("d", ks), 16 * cv)
                            inflight.append((o.sem, o.cnt, o.ndma))
                        for ji, (mname, a, k) in enumerate(o.calls):
                            ins = getattr(eng, mname)(*a, **k)
                            if o.dma:
                                ins.then_inc(dsem[o.sem], 16)
                        if not o.dma and o.need:
                            ins.then_inc(esem[ename], 1)
                    if ename == "sp":
                        for key, c in self.dma_keys.items():
                            if any(s.sem == key for s in self.store_ops):
                                wait(dsem[key], ("d", key), 16 * c[0])
                return body

            for e in self.ENGS:
                engobj[e](run(e))


class Region:
    def __init__(self, S, arena, lo, hi):
        self.S, self.arena, self.lo, self.hi, self.cur = S, arena, lo, hi, lo

    def alloc(self, name, shape, dtype, nbuf=None):
        esz = 4 if dtype == F32 else 2
        n = 1
        for s in shape[1:]:
            n *= s
        nbytes = (n * esz + 3) // 4 * 4
        lo = self.cur
        self.cur += nbytes
        assert self.cur <= self.hi, (name, self.cur, self.hi)
        ap = self.arena[:, lo // 4:(lo + nbytes) // 4]
        if dtype == BF16:
            ap = ap.bitcast(BF16)
        ap = ap[:, 0:n]
        if len(shape) == 3:
            ap = ap.rearrange("p (a b) -> p a b", a=shape[1])
        elif len(shape) == 4:
            ap = ap.rearrange("p (a b c) -> p a b c", a=shape[1], b=shape[2])
        b = self.S.buf(name, "sb", lo, lo + nbytes)
        return b, ap


def build():
    nc = bass.Bass("TRN2", target_bir_lowering=False)
    dram = lambda n, s, dt=F32, k="ExternalInput": nc.dram_tensor(n, s, dt, kind=k).ap()
    x_d = dram("x", [NSEQ, SEQ, D])
    meta_d = dram("meta", [NM, D])
    vecs_d = dram("vecs", [128, NV])
    cst_d = dram("cst", [128, 288])
    rope_d = dram("rope", [32, 2, T])
    w_in_d = dram("w_in", [D, 1696])
    w_uq_d = dram("w_uq", [384, 768])
    w_kn_d = dram("w_kn", [256, 512])
    w_v_d = dram("w_v", [256, 512])
    lru_d = dram("lru_w", [2, 2, 8, 64, 64])
    w_out_d = dram("w_out", [D, D])
    wgu_d = dram("wgu", [NJF, 128, 2048])
    wd_d = dram("w_down", [DFF, D])
    out_d = dram("out", [NSEQ, SEQ, D], F32, "ExternalOutput")
    dbg_d = {}

    S = Sched(nc)
    K = 1024
    with ExitStack() as st:
        arena = st.enter_context(nc.sbuf_tensor("arena", [128, 212800 // 4], F32))
        banks = [st.enter_context(nc.psum_tensor("bank%d" % i, [128, 512], F32)) for i in range(8)]
        PB = [S.buf("bank%d" % i, "ps", i, i + 1) for i in range(8)]

        R0 = Region(S, arena, 0, 19 * K)
        b_vecs, vecs = R0.alloc("vecs", [128, NV], F32)
        b_ident, ident = R0.alloc("ident", [128, 128], BF16)
        b_ones, ones = R0.alloc("ones", [128, 128], BF16)
        b_pmat, pmat = R0.alloc("pmat", [128, 32], F32)
        b_rope, rope = R0.alloc("rope", [128, 2, T], F32)
        b_lamc, lamc = R0.alloc("lamc", [128, 16], F32)
        b_lamt, lamt = R0.alloc("lamt", [128, 8], F32)

        def V(c, p0=0, p1=128):
            return vecs[p0:p1, c:c + 1]

        RA = Region(S, arena, 19 * K, 51 * K)
        b_ornT, ornT = RA.alloc("ornT", [128, 4, SEQ], BF16)
        b_oatT, oatT = RA.alloc("oatT", [128, 4, SEQ], BF16)
        RL3 = Region(S, arena, 51 * K, 80 * K)
        b_cqnT, cqnT = RL3.alloc("cqnT", [128, 3, T], BF16)
        b_ckvnT, ckvnT = RL3.alloc("ckvnT", [128, 2, T], BF16)
        b_krT, krT = RL3.alloc("krT", [128, T], F32)

        S.dma("sp", lambda e, f: f(e.dma_start(out=vecs, in_=vecs_d[:, :])), "c_vecs", writes=[b_vecs])
        S.dma("pool", lambda e, f: f(e.dma_start(out=ident, in_=cst_d[:, 0:128])), "c_id", writes=[b_ident])
        S.dma("pool", lambda e, f: f(e.dma_start(out=ones, in_=cst_d[:, 128:256])), "c_on", writes=[b_ones])
        S.dma("sp", lambda e, f: f(e.dma_start(out=pmat, in_=cst_d[:, 256:288])), "c_pm", writes=[b_pmat])
        S.dma("sp", lambda e, f: f(e.dma_start(out=rope[64:96, :, :], in_=rope_d[:, :, :])), "c_rope", writes=[b_rope])
        S.op("act", lambda e: e.activation(out=lamt, in_=vecs[:, V_LAM:V_LAM + 8], func=AF.Exp, scale=-1.0),
             reads=[b_vecs], writes=[b_lamt])
        S.op("act", lambda e: e.activation(out=lamt, in_=lamt, func=AF.Ln, bias=V(V_ONE), scale=1.0),
             reads=[b_vecs, b_lamt], writes=[b_lamt])
        S.op("dve", lambda e: e.tensor_scalar(out=lamc[:, 0:8], in0=lamt, scalar1=-8.0, scalar2=None, op0=ALU.mult),
             reads=[b_lamt], writes=[b_lamc])
        S.op("dve", lambda e: e.tensor_scalar(out=lamc[:, 8:16], in0=lamt, scalar1=-16.0, scalar2=None, op0=ALU.mult),
             reads=[b_lamt], writes=[b_lamc])

        def dump(name, b, ap, shape, dt=F32):
            if not KDEBUG:
                return
            dd = dram("dbg_" + name, shape, dt, "ExternalOutput")
            dbg_d[name] = dd
            S.dma("sp", lambda e, f: f(e.dma_start(out=dd, in_=ap)), "dbg_" + name, reads=[b], store=True)

        def rstd_fm(ps_ap, rs_ap, np_, n, inv_n, b_ps, b_rs):
            S.op("act", lambda e: e.activation(out=rs_ap, in_=ps_ap, func=AF.Ln, bias=V(V_EPS, 0, np_), scale=inv_n),
                 reads=[b_ps, b_vecs], writes=[b_rs])
            S.op("act", lambda e: e.activation(out=rs_ap, in_=rs_ap, func=AF.Exp, scale=-0.5),
                 reads=[b_rs], writes=[b_rs])

        try:
            for s in range(NSEQ):
                R1a = Region(S, arena, 19 * K, 51 * K)
                b_xt, xt = [], []
                for i in range(2):
                    b, a = R1a.alloc("xt%d" % i, [128, 4, D], F32)
                    b_xt.append(b); xt.append(a)
                R1 = Region(S, arena, 80 * K, 158 * K)
                b_win, win = R1.alloc("w_in", [128, 8, 1696], BF16)
                b_xs, xs = R1.alloc("xs", [128, 4, D], BF16)
                b_xnT, xnT = [], []
                for i in range(2):
                    b, a = R1.alloc("xnT%d" % i, [128, 8, 512], BF16)
                    b_xnT.append(b); xnT.append(a)
                b_latf, latf = R1.alloc("latf", [128, 3, 512], F32)
                b_sqb, sqb = R1.alloc("sqb", [128, 3, 512], BF16)
                b_rs, rs = R1.alloc("rs", [128, 512], F32)
                b_gt1, gt1 = R1.alloc("gt1", [128, 512], F32)
                b_gt2, gt2 = R1.alloc("gt2", [128, 512], F32)
                b_ss, ss = R1.alloc("ss", [128, 4], F32)
                b_rstd, rstd = R1.alloc("rstd", [128, 4], F32)
                b_xsB, xsB = R1.alloc("xsB", [128, 4, D], BF16)
                b_ssB, ssB = R1.alloc("ssB", [128, 4], F32)
                b_rstdB, rstdB = R1.alloc("rstdB", [128, 4], F32)
                scrs = [(b_xs, xs, b_ss, ss, b_rstd, rstd), (b_xsB, xsB, b_ssB, ssB, b_rstdB, rstdB)]
                R2 = Region(S, arena, 158 * K, 207 * K + 800)
                b_xr, xr = [], []
                for cc in range(4):
                    b, a = R2.alloc("xr%d" % cc, [128, T + 4], F32)
                    b_xr.append(b); xr.append(a)
                b_gg, gg = R2.alloc("gg", [128, 4, SEQ], BF16)

                def ld_win(e, f):
                    for kc in range(8):
                        f(e.dma_start(out=win[:, kc, :], in_=w_in_d[kc * 128:(kc + 1) * 128, :]))
                S.dma("pool", ld_win, "w_in", n=8, writes=[b_win])
                for cc in range(4):
                    S.op("dve", lambda e, cc=cc: e.memset(xr[cc][:, 0:2], 0.0), writes=[b_xr[cc]])
                    S.op("dve", lambda e, cc=cc: e.memset(xr[cc][:, T + 2:T + 4], 0.0), writes=[b_xr[cc]])

                chunks = [(0, NM, None)] + [(NM + 512 * c, 512, c) for c in range(4)]

                def ld_x(ci):
                    pos0, n, c = chunks[ci]
                    sl = ci % 2
                    if c is None:
                        S.dma("sp", lambda e, f: f(e.dma_start(out=xt[sl][0:NM, 0, :], in_=meta_d[:, :])),
                              "xt%d" % sl, writes=[b_xt[sl]])
                    else:
                        src = x_d[s, 512 * c:512 * c + 512, :].rearrange("(j p) f -> p j f", p=128)
                        S.dma("sp", lambda e, f: f(e.dma_start(out=xt[sl], in_=src)), "xt%d" % sl, writes=[b_xt[sl]])

                def norm_tm(xin, b_xin, np_, nt, gcol, dstT, b_dstT, tpb, part=0, scr=None):
                    n = 128 * nt if np_ == 128 else np_
                    b_xs_, xs_, b_ss_, ss_, b_rstd_, rstd_ = scr if scr is not None else (b_xs, xs, b_ss, ss, b_rstd, rstd)
                    if part in (0, 1):
                        norm_stats(xin, b_xin, np_, nt, b_xs_, xs_, b_ss_, ss_, b_rstd_, rstd_)
                    if part in (0, 2):
                        norm_tr(np_, nt, n, gcol, dstT, b_dstT, tpb, b_xs_, xs_)

                def norm_stats(xin, b_xin, np_, nt, b_xs, xs, b_ss, ss, b_rstd, rstd):
                    for j in range(nt):
                        S.op("act", lambda e, j=j: e.activation(out=xs[0:np_, j, :], in_=xin[0:np_, j, :], func=AF.Square),
                             reads=[b_xin], writes=[b_xs])
                    S.op("dve", lambda e: e.tensor_reduce(out=ss[0:np_, 0:nt], in_=xs[0:np_, 0:nt, :], axis=AX.X, op=ALU.add),
                         reads=[b_xs], writes=[b_ss])
                    S.op("act", lambda e: e.activation(out=rstd[0:np_, 0:nt], in_=ss[0:np_, 0:nt], func=AF.Ln,
                                                       bias=V(V_EPS, 0, np_), scale=1.0 / D),
                         reads=[b_ss, b_vecs], writes=[b_rstd])
                    S.op("act", lambda e: e.activation(out=rstd[0:np_, 0:nt], in_=rstd[0:np_, 0:nt], func=AF.Exp, scale=-0.5),
                         reads=[b_rstd], writes=[b_rstd])
                    for j in range(nt):
                        S.op("dve", lambda e, j=j: e.tensor_scalar(out=xs[0:np_, j, :], in0=xin[0:np_, j, :],
                                                                   scalar1=rstd[0:np_, j:j + 1], scalar2=None, op0=ALU.mult),
                             reads=[b_xin, b_rstd], writes=[b_xs])

                def norm_tr(np_, nt, n, gcol, dstT, b_dstT, tpb, b_xs, xs):
                    for kc in range(8):
                        bk = tpb[kc % 2]
                        tp = banks[bk][:, :].bitcast(BF16)

                        def tr(e, kc=kc, tp=tp):
                            ins = None
                            for j in range(nt):
                                w = np_
                                ins = e.transpose(out=tp[:, j * 128:j * 128 + w], in_=xs[0:np_, j, kc * 128:(kc + 1) * 128],
                                                  identity=ident[0:np_, 0:np_])
                            return ins
                        S.op("pe", tr, reads=[b_xs, b_ident], writes=[PB[bk]])
                        S.op("dve", lambda e, kc=kc, tp=tp: e.tensor_scalar(out=dstT[:, kc, 0:n], in0=tp[:, 0:n],
                                                                           scalar1=V(gcol + kc), scalar2=None, op0=ALU.mult),
                             reads=[PB[bk], b_vecs], writes=[b_dstT])

                def mm_acc(e, out_ap, lhs_list, rhs_list):
                    ins = None
                    n = len(lhs_list)
                    for i in range(n):
                        ins = e.matmul(out_ap, lhs_list[i], rhs_list[i], start=(i == 0), stop=(i == n - 1))
                    return ins

                mmb = [2, 3, 4, 5]
                mmi = [0]

                def next_bank():
                    b = mmb[mmi[0] % len(mmb)]
                    mmi[0] += 1
                    return b

                def p1_norm(ci, part):
                    _, _, c_ = chunks[ci]
                    norm_tm(xt[ci % 2], b_xt[ci % 2], 128 if c_ is not None else NM, 4 if c_ is not None else 1,
                            V_G1, xnT[ci % 2], b_xnT[ci % 2], (0, 1), part=part, scr=scrs[ci % 2])

                ld_x(0)
                ld_x(1)
                p1_norm(0, 0)
                for ci in range(5):
                    pos0, n, c = chunks[ci]
                    sl = ci % 2
                    np_ = 128 if c is not None else NM
                    nt = 4 if c is not None else 1
                    xT = xnT[sl]
                    bxT = b_xnT[sl]
                    if ci + 1 < 5:
                        p1_norm(ci + 1, 1)

                    def win_mm(col0, m, bk, p0=0):
                        S.op("pe", lambda e: mm_acc(e, banks[bk][p0:p0 + m, 0:n],
                                                    [win[:, kc, col0:col0 + m] for kc in range(8)],
                                                    [xT[:, kc, 0:n] for kc in range(8)]),
                             reads=[b_win, bxT], writes=[PB[bk]])

                    for (col0, ng, gcol, dst, b_dst, inv) in ((0, 3, V_GQA, cqnT, b_cqnT, 1.0 / 384),
                                                               (384, 2, V_GKVA, ckvnT, b_ckvnT, 1.0 / 256)):
                        for g in range(ng):
                            bk = next_bank()
                            win_mm(col0 + 128 * g, 128, bk)
                            S.op("act", lambda e, g=g, bk=bk: e.activation(out=sqb[:, g, 0:n], in_=banks[bk][:, 0:n], func=AF.Square),
                                 reads=[PB[bk]], writes=[b_sqb])
                            S.op("dve", lambda e, g=g, bk=bk: e.tensor_copy(out=latf[:, g, 0:n], in_=banks[bk][:, 0:n]),
                                 reads=[PB[bk]], writes=[b_latf])
                        S.op("pe", lambda e, ng=ng: mm_acc(e, banks[6][:, 0:n], [ones[:, :]] * ng,
                                                           [sqb[:, g, 0:n] for g in range(ng)]),
                             reads=[b_ones, b_sqb], writes=[PB[6]])
                        rstd_fm(banks[6][:, 0:n], rs[:, 0:n], 128, n, inv, PB[6], b_rs)
                        for g in range(ng):
                            S.op("dve", lambda e, g=g, dst=dst, gcol=gcol: e.scalar_tensor_tensor(
                                out=dst[:, g, pos0:pos0 + n], in0=latf[:, g, 0:n], scalar=V(gcol + g), in1=rs[:, 0:n],
                                op0=ALU.mult, op1=ALU.mult), reads=[b_latf, b_rs, b_vecs], writes=[b_dst])
                    if ci + 1 < 5:
                        p1_norm(ci + 1, 2)
                    if ci + 2 < 5:
                        ld_x(ci + 2)
                    bk = next_bank()
                    win_mm(640, 32, bk, p0=64)
                    S.op("act", lambda e, bk=bk: e.activation(out=krT[64:96, pos0:pos0 + n], in_=banks[bk][64:96, 0:n], func=AF.Copy),
                         reads=[PB[bk]], writes=[b_krT])
                    for cc in range(4):
                        bk = next_bank()
                        win_mm(672 + 128 * cc, 128, bk)
                        S.op("act", lambda e, cc=cc, bk=bk: e.activation(out=xr[cc][:, 2 + pos0:2 + pos0 + n], in_=banks[bk][:, 0:n], func=AF.Copy),
                             reads=[PB[bk]], writes=[b_xr[cc]])
                    if c is not None:
                        for cc in range(4):
                            bk = next_bank()
                            win_mm(1184 + 128 * cc, 128, bk)
                            S.op("act", lambda e, bk=bk: e.activation(out=gt1, in_=banks[bk][:, :], func=AF.Square),
                                 reads=[PB[bk]], writes=[b_gt1])
                            S.op("dve", lambda e: e.tensor_scalar(out=gt1, in0=gt1, scalar1=0.044715, scalar2=1.0,
                                                                  op0=ALU.mult, op1=ALU.add), reads=[b_gt1], writes=[b_gt1])
                            S.op("dve", lambda e, bk=bk: e.tensor_tensor(out=gt1, in0=banks[bk][:, :], in1=gt1, op=ALU.mult),
                                 reads=[PB[bk], b_gt1], writes=[b_gt1])
                            S.op("act", lambda e: e.activation(out=gt2, in_=gt1, func=AF.Sigmoid, scale=1.5957691216057308),
                                 reads=[b_gt1], writes=[b_gt2])
                            S.op("dve", lambda e, cc=cc, bk=bk, c=c: e.tensor_tensor(out=gg[:, cc, 512 * c:512 * c + 512],
                                                                                 in0=banks[bk][:, :], in1=gt2, op=ALU.mult),
                                 reads=[PB[bk], b_gt2], writes=[b_gg])
                if s == 0:
                    dump("cqnT", b_cqnT, cqnT, [128, 3, T], BF16)
                    dump("ckvnT", b_ckvnT, ckvnT, [128, 2, T], BF16)
                    dump("krT", b_krT, krT[64:96, :], [32, T])
                    dump("xr0", b_xr[0], xr[0], [128, T + 4])
                    dump("gg", b_gg, gg, [128, 4, SEQ], BF16)
                phase_end("p1")

                R4 = Region(S, arena, 80 * K, 158 * K)
                b_lw, lw = R4.alloc("lru_w", [128, 16, 128], BF16)
                b_xc, xc = R4.alloc("xc", [128, T], F32)
                RXB = Region(S, arena, 19 * K, 19 * K + T * 2 + 4)
                b_xcb, xcb = RXB.alloc("xcb", [128, T], BF16)
                lb = {}
                for nm in ("r0", "i0", "a0", "r1", "i1", "a1", "hf", "hb"):
                    lb[nm] = R4.alloc("l_" + nm, [128, T], F32)
                R4s = Region(S, arena, lb["r0"][0].lo, lb["r0"][0].hi)
                b_sq4, sq4 = R4s.alloc("sq4", [128, 4, 512], BF16)
                b_rsr, rsr = R4s.alloc("rsr", [128, 512], F32)
                S.op("dve", lambda e: e.memset(lw, 0.0), writes=[b_lw])

                def ld_lru(e, f):
                    for g in range(2):
                        for d in range(2):
                            src = lru_d[g, d].rearrange("(c b) i j -> b i c j", b=2)
                            k0 = (g * 2 + d) * 4
                            for bh in range(2):
                                f(e.dma_start(out=lw[bh * 64:(bh + 1) * 64, k0:k0 + 4, bh * 64:(bh + 1) * 64], in_=src[bh]))
                S.dma("pool", ld_lru, "lru_w", n=8, writes=[b_lw])

                pieces = [(512 * p, 512) for p in range(4)] + [(2048, 16)]
                for cc in range(4):
                    S.op("dve", lambda e, cc=cc: e.tensor_scalar(out=xc, in0=xr[cc][:, 0:T], scalar1=V(V_CW + cc * 4),
                                                                 scalar2=V(V_CB + cc), op0=ALU.mult, op1=ALU.add),
                         reads=[b_xr[cc], b_vecs], writes=[b_xc])
                    for j in range(1, 4):
                        S.op("dve", lambda e, cc=cc, j=j: e.scalar_tensor_tensor(out=xc, in0=xr[cc][:, j:j + T], scalar=V(V_CW + cc * 4 + j),
                                                                                in1=xc, op0=ALU.mult, op1=ALU.add),
                             reads=[b_xr[cc], b_vecs, b_xc], writes=[b_xc])
                    S.op("act", lambda e: e.activation(out=xcb, in_=xc, func=AF.Copy), reads=[b_xc], writes=[b_xcb])
                    for d in range(2):
                        b_r, r_ = lb["r%d" % d]; b_i, i_ = lb["i%d" % d]; b_a, a_ = lb["a%d" % d]
                        b_b, bb_ = b_i, i_
                        b_h, h_ = lb["hf"] if d == 0 else lb["hb"]
                        for (p0, pn) in pieces:
                            for g, (bdst, dst, bcol) in enumerate(((b_r, r_, V_BA), (b_i, i_, V_BI))):
                                bk = next_bank()
                                S.op("pe", lambda e, g=g, bk=bk, p0=p0, pn=pn, d=d, cc=cc: e.matmul(
                                    banks[bk][:, 0:pn], lw[:, (g * 2 + d) * 4 + cc, :], xcb[:, p0:p0 + pn], start=True, stop=True),
                                    reads=[b_lw, b_xcb], writes=[PB[bk]])
                                S.op("act", lambda e, bk=bk, p0=p0, pn=pn, dst=dst, bcol=bcol, d=d, cc=cc: e.activation(
                                    out=dst[:, p0:p0 + pn], in_=banks[bk][:, 0:pn], func=AF.Sigmoid,
                                    bias=V(bcol + d * 4 + cc), scale=1.0), reads=[PB[bk], b_vecs], writes=[bdst])
                        ci_ = d * 4 + cc
                        S.op("act", lambda e, ci_=ci_: e.activation(out=a_, in_=r_, func=AF.Exp, scale=lamc[:, ci_:ci_ + 1]),
                             reads=[b_r, b_lamc], writes=[b_a])
                        S.op("act", lambda e, ci_=ci_: e.activation(out=r_, in_=r_, func=AF.Exp, scale=lamc[:, 8 + ci_:9 + ci_]),
                             reads=[b_r, b_lamc], writes=[b_r])
                        S.op("act", lambda e: e.activation(out=r_, in_=r_, func=AF.Sqrt, bias=V(V_ONE), scale=-1.0),
                             reads=[b_r, b_vecs], writes=[b_r])
                        S.op("pool", lambda e: e.tensor_tensor(out=bb_, in0=i_, in1=xc, op=ALU.mult),
                             reads=[b_i, b_xc], writes=[b_b])
                        S.op("dve", lambda e: e.tensor_tensor(out=bb_, in0=bb_, in1=r_, op=ALU.mult),
                             reads=[b_b, b_r], writes=[b_b])
                        if d == 0:
                            S.op("dve", lambda e, h_=h_: e.tensor_tensor_scan(out=h_, data0=a_, data1=bb_, initial=0.0,
                                                                             op0=ALU.mult, op1=ALU.add),
                                 reads=[b_a, b_b], writes=[b_h])
                        else:
                            S.op("dve", lambda e, h_=h_: e.tensor_tensor_scan(out=h_[:, ::-1], data0=a_[:, ::-1], data1=bb_[:, ::-1],
                                                                             initial=0.0, op0=ALU.mult, op1=ALU.add),
                                 reads=[b_a, b_b], writes=[b_h])
                    b_hf, hf = lb["hf"]; b_hb, hb = lb["hb"]
                    S.op("pool", lambda e: e.tensor_tensor(out=hf[:, NM:T], in0=hf[:, NM:T], in1=hb[:, NM:T], op=ALU.add),
                         reads=[b_hf, b_hb], writes=[b_hf])
                    S.op("dve", lambda e, cc=cc: e.tensor_tensor(out=xr[cc][:, 2 + NM:2 + T], in0=hf[:, NM:T], in1=gg[:, cc, :], op=ALU.mult),
                         reads=[b_hf, b_gg], writes=[b_xr[cc]])
                for c in range(4):
                    c0 = 2 + NM + 512 * c
                    for cc in range(4):
                        S.op("act", lambda e, cc=cc, c0=c0: e.activation(out=sq4[:, cc, :], in_=xr[cc][:, c0:c0 + 512], func=AF.Square),
                             reads=[b_xr[cc]], writes=[b_sq4])
                    S.op("pe", lambda e: mm_acc(e, banks[6][:, :], [ones[:, :]] * 4, [sq4[:, cc, :] for cc in range(4)]),
                         reads=[b_ones, b_sq4], writes=[PB[6]])
                    rstd_fm(banks[6][:, :], rsr, 128, 512, 1.0 / 512, PB[6], b_rsr)
                    for cc in range(4):
                        S.op("dve", lambda e, cc=cc, c0=c0, c=c: e.scalar_tensor_tensor(
                            out=ornT[:, cc, 512 * c:512 * c + 512], in0=xr[cc][:, c0:c0 + 512], scalar=V(V_GR + cc), in1=rsr,
                            op0=ALU.mult, op1=ALU.mult), reads=[b_xr[cc], b_rsr, b_vecs], writes=[b_ornT])
                if s == 0:
                    dump("ornT", b_ornT, ornT, [128, 4, SEQ], BF16)
                phase_end("p2")

                R6 = Region(S, arena, 80 * K, 207 * K + 800)
                b_wkn, wkn = R6.alloc("w_kn", [128, 2, 512], BF16)
                b_wv, wv = R6.alloc("w_v", [128, 2, 512], BF16)
                b_wuq, wuq = R6.alloc("w_uq", [128, 3, 768], BF16)
                b_KT, KT = [], []
                for h in range(NH):
                    b, a = R6.alloc("KT%d" % h, [128, T], BF16)
                    b_KT.append(b); KT.append(a)
                b_va, va = R6.alloc("vaug", [128, 17, NH, 128], BF16)
                b_QT, QT = [], []
                for i in range(2):
                    b, a = R6.alloc("QT%d" % i, [128, NH, 512], BF16)
                    b_QT.append(b); QT.append(a)
                b_PT, PT = [], []
                for i in range(4):
                    b, a = R6.alloc("PT%d" % i, [128, 512], BF16)
                    b_PT.append(b); PT.append(a)
                b_oraw, oraw = [], []
                for i in range(4):
                    b, a = R6.alloc("oraw%d" % i, [128, 512], F32)
                    b_oraw.append(b); oraw.append(a)
                b_sqk, sqk = [], []
                for i in range(2):
                    b, a = R6.alloc("sqk%d" % i, [128, 512], BF16)
                    b_sqk.append(b); sqk.append(a)
                b_rden, rden = [], []
                for i in range(2):
                    b, a = R6.alloc("rden%d" % i, [128, 512], F32)
                    b_rden.append(b); rden.append(a)
                b_xk, xk = R6.alloc("xk", [128, 512], F32)
                b_t1, t1 = R6.alloc("t1", [128, 512], F32)
                b_t2, t2 = R6.alloc("t2", [128, 512], F32)
                b_krp, krp = R6.alloc("krp", [128, 512], F32)
                b_rec, rec = R6.alloc("rec", [128, 512], F32)
                b_sqa, sqa = R6.alloc("sqa", [128, 4, 512], BF16)
                b_rsa, rsa = R6.alloc("rsa", [128, 512], F32)

                def ld_kvw(e, f):
                    for kc in range(2):
                        f(e.dma_start(out=wkn[:, kc, :], in_=w_kn_d[kc * 128:(kc + 1) * 128, :]))
                        f(e.dma_start(out=wv[:, kc, :], in_=w_v_d[kc * 128:(kc + 1) * 128, :]))
                S.dma("pool", ld_kvw, "w_kv", n=4, writes=[b_wkn, b_wv])

                def ld_uq(e, f):
                    for kc in range(3):
                        f(e.dma_start(out=wuq[:, kc, :], in_=w_uq_d[kc * 128:(kc + 1) * 128, :]))
                S.dma("pool", ld_uq, "w_uq", n=3, writes=[b_wuq])
                S.op("pool", lambda e: e.memset(va, 1.0), writes=[b_va])

                def rope_rows(src_ps_or_sb, b_src, gcol, cols0, n, dst, b_dst, rd, b_rd, bk_px):
                    S.op("dve", lambda e: e.tensor_scalar(out=xk[64:96, 0:n], in0=src_ps_or_sb, scalar1=V(gcol, 64, 96),
                                                          scalar2=None, op0=ALU.mult), reads=[b_src, b_vecs], writes=[b_xk])
                    S.op("pe", lambda e: e.matmul(banks[bk_px][64:96, 0:n], pmat[64:96, 0:32], xk[64:96, 0:n], start=True, stop=True),
                         reads=[b_pmat, b_xk], writes=[PB[bk_px]])
                    S.op("dve", lambda e: e.tensor_tensor(out=t1[64:96, 0:n], in0=xk[64:96, 0:n], in1=rope[64:96, 0, cols0:cols0 + n], op=ALU.mult),
                         reads=[b_xk, b_rope], writes=[b_t1])
                    S.op("dve", lambda e: e.tensor_tensor(out=t2[64:96, 0:n], in0=banks[bk_px][64:96, 0:n], in1=rope[64:96, 1, cols0:cols0 + n], op=ALU.mult),
                         reads=[PB[bk_px], b_rope], writes=[b_t2])
                    S.op("dve", lambda e: e.tensor_tensor(out=t1[64:96, 0:n], in0=t1[64:96, 0:n], in1=t2[64:96, 0:n], op=ALU.add),
                         reads=[b_t1, b_t2], writes=[b_t1])
                    if rd is None:
                        S.op("dve", lambda e: e.tensor_copy(out=dst, in_=t1[64:96, 0:n]), reads=[b_t1], writes=[b_dst])
                    else:
                        S.op("dve", lambda e: e.tensor_tensor(out=dst, in0=t1[64:96, 0:n], in1=rd, op=ALU.mult),
                             reads=[b_t1, b_rd], writes=[b_dst])

                for ci in range(5):
                    pos0, n, c = chunks[ci]
                    rope_rows(krT[64:96, pos0:pos0 + n], b_krT, V_KG, pos0, n, krp[64:96, 0:n], b_krp, None, None, 7)
                    for i in range(2):
                        S.op("act", lambda e, i=i: e.activation(out=sqk[i][64:96, 0:n], in_=krT[64:96, pos0:pos0 + n], func=AF.Square),
                             reads=[b_krT], writes=[b_sqk[i]])
                    for h in range(NH):
                        bk = next_bank()
                        sl = h % 2
                        S.op("pe", lambda e, h=h, bk=bk: mm_acc(e, banks[bk][0:64, 0:n],
                                                               [wkn[:, kc, h * 64:(h + 1) * 64] for kc in range(2)],
                                                               [ckvnT[:, kc, pos0:pos0 + n] for kc in range(2)]),
                             reads=[b_wkn, b_ckvnT], writes=[PB[bk]])
                        S.op("act", lambda e, bk=bk, sl=sl: e.activation(out=sqk[sl][0:64, 0:n], in_=banks[bk][0:64, 0:n], func=AF.Square),
                             reads=[PB[bk]], writes=[b_sqk[sl]])
                        S.op("pe", lambda e, sl=sl: e.matmul(banks[6][0:96, 0:n], ones[0:96, 0:96], sqk[sl][0:96, 0:n], start=True, stop=True),
                             reads=[b_ones, b_sqk[sl]], writes=[PB[6]])
                        rstd_fm(banks[6][0:96, 0:n], rden[sl][0:96, 0:n], 96, n, 1.0 / 96, PB[6], b_rden[sl])
                        S.op("dve", lambda e, h=h, bk=bk, sl=sl: e.scalar_tensor_tensor(
                            out=KT[h][0:64, pos0:pos0 + n], in0=banks[bk][0:64, 0:n], scalar=V(V_KG, 0, 64), in1=rden[sl][0:64, 0:n],
                            op0=ALU.mult, op1=ALU.mult), reads=[PB[bk], b_rden[sl], b_vecs], writes=[b_KT[h]])
                        S.op("dve", lambda e, h=h, sl=sl: e.tensor_tensor(out=KT[h][64:96, pos0:pos0 + n], in0=krp[64:96, 0:n],
                                                                         in1=rden[sl][64:96, 0:n], op=ALU.mult),
                             reads=[b_krp, b_rden[sl]], writes=[b_KT[h]])
                    ntile = 1 if c is None else 4
                    for j in range(ntile):
                        kt = 0 if c is None else 1 + 4 * c + j
                        npk = NM if c is None else 128
                        bk = next_bank()
                        S.op("pe", lambda e, j=j, bk=bk, npk=npk: mm_acc(e, banks[bk][0:npk, :],
                                                                        [ckvnT[:, kc, pos0 + 128 * j:pos0 + 128 * j + npk] for kc in range(2)],
                                                                        [wv[:, kc, :] for kc in range(2)]),
                             reads=[b_ckvnT, b_wv], writes=[PB[bk]])
                        for par in range(2):
                            src = banks[bk][0:npk, :].rearrange("p (a b d) -> p a b d", a=4, b=2)[:, :, par, :]
                            S.op("act", lambda e, kt=kt, par=par, src=src, npk=npk: e.activation(
                                out=va[0:npk, kt, par::2, par * 64:par * 64 + 64], in_=src, func=AF.Copy),
                                reads=[PB[bk]], writes=[b_va])
                if s == 0:
                    dump("KT0", b_KT[0], KT[0][0:96, :], [96, T], BF16)
                    dump("KT3", b_KT[3], KT[3][0:96, :], [96, T], BF16)
                    dump("vaug", b_va, va, [128, 17, NH, 128], BF16)
                phase_end("kv")

                def qprep_stages(c, h):
                    sl = c % 2
                    pos0 = NM + 512 * c
                    bk = 4
                    sk = h % 2
                    n = 512

                    def st0():
                        S.op("pe", lambda e: mm_acc(e, banks[bk][0:96, :],
                                                    [wuq[:, kc, h * 96:(h + 1) * 96] for kc in range(3)],
                                                    [cqnT[:, kc, pos0:pos0 + 512] for kc in range(3)]),
                             reads=[b_wuq, b_cqnT], writes=[PB[bk]])

                    def st1():
                        S.op("act", lambda e: e.activation(out=sqk[sk][0:96, :], in_=banks[bk][0:96, :], func=AF.Square),
                             reads=[PB[bk]], writes=[b_sqk[sk]])

                    def st2():
                        S.op("pe", lambda e: e.matmul(banks[6][0:96, :], ones[0:96, 0:96], sqk[sk][0:96, :], start=True, stop=True),
                             reads=[b_ones, b_sqk[sk]], writes=[PB[6]])

                    def st3():
                        rstd_fm(banks[6][0:96, :], rden[sk][0:96, :], 96, 512, 1.0 / 96, PB[6], b_rden[sk])

                    def st4():
                        S.op("dve", lambda e: e.scalar_tensor_tensor(
                            out=QT[sl][0:64, h, :], in0=banks[bk][0:64, :], scalar=V(V_QG, 0, 64), in1=rden[sk][0:64, :],
                            op0=ALU.mult, op1=ALU.mult), reads=[PB[bk], b_rden[sk], b_vecs], writes=[b_QT[sl]])
                        S.op("dve", lambda e: e.tensor_scalar(out=xk[64:96, 0:n], in0=banks[bk][64:96, :], scalar1=V(V_QG, 64, 96),
                                                              scalar2=None, op0=ALU.mult), reads=[PB[bk], b_vecs], writes=[b_xk])

                    def st5():
                        S.op("pe", lambda e: e.matmul(banks[7][64:96, 0:n], pmat[64:96, 0:32], xk[64:96, 0:n], start=True, stop=True),
                             reads=[b_pmat, b_xk], writes=[PB[7]])

                    def st6():
                        S.op("dve", lambda e: e.tensor_tensor(out=t1[64:96, 0:n], in0=xk[64:96, 0:n], in1=rope[64:96, 0, pos0:pos0 + n], op=ALU.mult),
                             reads=[b_xk, b_rope], writes=[b_t1])
                        S.op("dve", lambda e: e.tensor_tensor(out=t2[64:96, 0:n], in0=banks[7][64:96, 0:n], in1=rope[64:96, 1, pos0:pos0 + n], op=ALU.mult),
                             reads=[PB[7], b_rope], writes=[b_t2])
                        S.op("dve", lambda e: e.tensor_tensor(out=t1[64:96, 0:n], in0=t1[64:96, 0:n], in1=t2[64:96, 0:n], op=ALU.add),
                             reads=[b_t1, b_t2], writes=[b_t1])
                        S.op("dve", lambda e: e.tensor_tensor(out=QT[sl][64:96, h, :], in0=t1[64:96, 0:n], in1=rden[sk][64:96, :], op=ALU.mult),
                             reads=[b_t1, b_rden[sk]], writes=[b_QT[sl]])
                    return [st0, st1, st2, st3, st4, st5, st6]

                def qprep_head(c, h):
                    for st in qprep_stages(c, h):
                        st()

                ktiles = [(0, NM)] + [(NM + 128 * i, 128) for i in range(16)]
                xcb_junk = va[:, 1, 0:4, :].rearrange('p a b -> p (a b)')
                sbk = [0, 1]
                JUNK = int(os.environ.get('KJUNK', '192'))
                obk = [2, 3]
                scale = 96.0 ** -0.5
                LAG = int(os.environ.get('KLAG', '2'))
                INTER = os.environ.get('KINTER', '1') == '1'

                pend = []

                def attention(c, inter=None):
                    sl = c % 2
                    steps = [(h, kt) for h in range(NH) for kt in range(17)]
                    nst = len(steps)

                    def emit_S(i):
                        h, kt = steps[i]
                        if inter is not None:
                            inter(h, kt)
                        if h == 0 and kt in (2, 4, 6) and pend:
                            pend.pop(0)()
                        k0, nk = ktiles[kt]
                        sb_ = sbk[i % 2]
                        pt = i % 4
                        S.op("pe", lambda e: e.matmul(banks[sb_][0:nk, :], KT[h][0:96, k0:k0 + nk], QT[sl][0:96, h, :],
                                                      start=True, stop=True),
                             reads=[b_KT[h], b_QT[sl]], writes=[PB[sb_]])
                        S.op("act", lambda e: e.activation(out=PT[pt][0:nk, :], in_=banks[sb_][0:nk, :], func=AF.Exp, scale=scale),
                             reads=[PB[sb_]], writes=[b_PT[pt]])

                    def emit_PV(i):
                        h, kt = steps[i]
                        k0, nk = ktiles[kt]
                        pt = i % 4
                        ob = obk[h % 2]
                        par = h % 2
                        S.op("pe", lambda e: e.matmul(banks[ob][:, :], va[0:nk, kt, h, :], PT[pt][0:nk, :],
                                                      start=(kt == 0), stop=(kt == 16)),
                             reads=[b_va, b_PT[pt]], writes=[PB[ob]])
                        if kt == 16:
                            own = slice(0, 64) if par == 0 else slice(64, 128)
                            oth = slice(64, 128) if par == 0 else slice(0, 64)
                            pr = h // 2
                            S.op("dve", lambda e: e.reciprocal(out=rec[own, :], in_=banks[ob][oth, :]),
                                 reads=[PB[ob]], writes=[b_rec])
                            S.op("dve", lambda e: e.tensor_tensor(out=oraw[pr][own, :], in0=banks[ob][own, :],
                                                                  in1=rec[own, :], op=ALU.mult),
                                 reads=[PB[ob], b_rec], writes=[b_oraw[pr]])

                    for i in range(nst + LAG):
                        if i < nst:
                            emit_S(i)
                        if JUNK:
                            S.op("pe", lambda e: e.matmul(banks[5][:, 0:JUNK], ones[:, :], xcb_junk[:, 0:JUNK], start=True, stop=True),
                                 reads=[b_ones], writes=[PB[5]])
                        if i >= LAG:
                            emit_PV(i - LAG)
                    def t0():
                        for pr in range(4):
                            S.op("act", lambda e, pr=pr: e.activation(out=sqa[:, pr, :], in_=oraw[pr], func=AF.Square),
                                 reads=[b_oraw[pr]], writes=[b_sqa])

                    def t1():
                        S.op("pe", lambda e: mm_acc(e, banks[6][:, :], [ones[:, :]] * 4, [sqa[:, pr, :] for pr in range(4)]),
                             reads=[b_ones, b_sqa], writes=[PB[6]])

                    def t2():
                        rstd_fm(banks[6][:, :], rsa, 128, 512, 1.0 / 512, PB[6], b_rsa)

                    def t3():
                        for pr in range(4):
                            S.op("dve", lambda e, pr=pr: e.scalar_tensor_tensor(
                                out=oatT[:, pr, 512 * c:512 * c + 512], in0=oraw[pr], scalar=V(V_GA + pr), in1=rsa,
                                op0=ALU.mult, op1=ALU.mult), reads=[b_oraw[pr], b_rsa, b_vecs], writes=[b_oatT])
                    return [t0, lambda: (t1(), t2()), t3]

                mmb[:] = [4]
                for h in range(NH):
                    qprep_head(0, h)
                for c in range(4):
                    if INTER:
                        if c + 1 < 4:
                            stg = {h: qprep_stages(c + 1, h) for h in range(NH)}
                            tl = attention(c, lambda h, kt, stg=stg: stg[h][(kt - 1) // 2]() if (kt % 2 == 1 and kt < 15) else None)
                        else:
                            tl = attention(c, None)
                        pend.extend(tl)
                        if c == 3:
                            while pend:
                                pend.pop(0)()
                    else:
                        if c + 1 < 4:
                            for h in range(NH):
                                qprep_head(c + 1, h)
                        for t_ in attention(c, None):
                            t_()
                mmb[:] = [2, 3, 4, 5]
                if s == 0:
                    dump("oatT", b_oatT, oatT, [128, 4, SEQ], BF16)
                phase_end("att")

                R8 = Region(S, arena, 51 * K, 207 * K + 800)
                b_wo, wo = R8.alloc("w_out", [128, 8, D], BF16)
                b_wd, wd = R8.alloc("w_down", [128, NJF, D], BF16)
                b_ring, ring = [], []
                for i in range(RING):
                    b, a = R8.alloc("ring%d" % i, [128, 2, 8, 128], BF16)
                    b_ring.append(b); ring.append(a)
                b_aT, aT = R8.alloc("aT", [128, NJF, 512], BF16)
                b_hnT, hnT = R8.alloc("hnT", [128, 8, 512], BF16)
                b_xh, xh = [], []
                for i in range(2):
                    b, a_ = R8.alloc("xh%d" % i, [128, 4, D], F32)
                    b_xh.append(b); xh.append(a_)
                b_xs5, xs5 = R8.alloc("xs5", [128, 4, D], BF16)
                b_sg, sg = [], []
                for i in range(2):
                    b, a = R8.alloc("sg%d" % i, [128, 512], F32)
                    b_sg.append(b); sg.append(a)
                b_ss5, ss5 = R8.alloc("ss5", [128, 4], F32)
                b_rstd5, rstd5 = R8.alloc("rstd5", [128, 4], F32)
                b_xs, xs, b_ss, ss, b_rstd, rstd = b_xs5, xs5, b_ss5, ss5, b_rstd5, rstd5

                def ld_wo(e, f):
                    for kc in range(8):
                        f(e.dma_start(out=wo[:, kc, :], in_=w_out_d[kc * 128:(kc + 1) * 128, :]))
                S.dma("pool", ld_wo, "w_out", n=8, writes=[b_wo])

                for part, (j0, j1) in enumerate(((0, 8), (8, 16), (16, NJF))):
                    def ld_wd(e, f, j0=j0, j1=j1):
                        for jf in range(j0, j1):
                            f(e.dma_start(out=wd[:, jf, :], in_=wd_d[jf * 128:(jf + 1) * 128, :]))
                    S.dma("pool", ld_wd, "w_down%d" % part, writes=[b_wd])

                ring_i = [0]

                def ld_ring(jf):
                    sl = ring_i[0] % RING
                    ring_i[0] += 1
                    dst = ring[sl].rearrange("p a b c -> p (a b c)")
                    S.dma("pool", lambda e, f: f(e.dma_start(out=dst, in_=wgu_d[jf])), "ring%d" % sl, writes=[b_ring[sl]])
                    return sl

                PRE = RING - 1
                blocks = [(c, jf) for c in range(4) for jf in range(NJF)]
                issued = [0]
                bslot = {}

                def ring_fill(upto):
                    while issued[0] < min(upto, len(blocks)):
                        bslot[blocks[issued[0]]] = ld_ring(blocks[issued[0]][1])
                        issued[0] += 1

                def ld_xh(c):
                    src = x_d[s, 512 * c:512 * c + 512, :].rearrange("(j p) f -> p j f", p=128)
                    S.dma("sp", lambda e, f: f(e.dma_start(out=xh[c % 2], in_=src)), "xh%d" % (c % 2), writes=[b_xh[c % 2]])

                def front(c):
                    xh_, bxh_ = xh[c % 2], b_xh[c % 2]
                    for j in range(4):
                        for half in range(2):
                            bk = half
                            lhs = [oatT[:, kc, 512 * c + 128 * j:512 * c + 128 * j + 128] for kc in range(4)] + \
                                  [ornT[:, kc, 512 * c + 128 * j:512 * c + 128 * j + 128] for kc in range(4)]
                            rhs = [wo[:, kc, half * 512:(half + 1) * 512] for kc in range(8)]
                            S.op("pe", lambda e, bk=bk, lhs=lhs, rhs=rhs: mm_acc(e, banks[bk][:, :], lhs, rhs),
                                 reads=[b_oatT, b_ornT, b_wo], writes=[PB[bk]])
                            S.op("dve", lambda e, bk=bk, j=j, half=half: e.tensor_tensor(
                                out=xh_[:, j, half * 512:(half + 1) * 512], in0=banks[bk][:, :], in1=xh_[:, j, half * 512:(half + 1) * 512],
                                op=ALU.add), reads=[PB[bk], bxh_], writes=[bxh_])
                    if s == 0 and c == 0:
                        dump("h0", bxh_, xh_, [128, 4, D])
                    norm_tm(xh_, bxh_, 128, 4, V_G2, hnT, b_hnT, (2, 3), part=1)

                def front_tr(c):
                    norm_tm(xh[c % 2], b_xh[c % 2], 128, 4, V_G2, hnT, b_hnT, (2, 3), part=2)

                ld_xh(0)
                ring_fill(PRE)
                front(0)
                front_tr(0)
                for c in range(4):
                    xh_, bxh_ = xh[c % 2], b_xh[c % 2]
                    if c + 1 < 4:
                        ld_xh(c + 1)
                    for jf in range(NJF):
                        ring_fill(c * NJF + jf + PRE + 1)
                        sl = bslot[(c, jf)]
                        gb = 4 + jf % 2
                        ub = 6 + jf % 2
                        S.op("pe", lambda e, sl=sl, gb=gb: mm_acc(e, banks[gb][:, :], [ring[sl][:, 0, kc, :] for kc in range(8)],
                                                                 [hnT[:, kc, :] for kc in range(8)]),
                             reads=[b_ring[sl], b_hnT], writes=[PB[gb]])
                        S.op("pe", lambda e, sl=sl, ub=ub: mm_acc(e, banks[ub][:, :], [ring[sl][:, 1, kc, :] for kc in range(8)],
                                                                 [hnT[:, kc, :] for kc in range(8)]),
                             reads=[b_ring[sl], b_hnT], writes=[PB[ub]])
                        S.op("act", lambda e, gb=gb, jf=jf: e.activation(out=sg[jf % 2], in_=banks[gb][:, :], func=AF.Silu),
                             reads=[PB[gb]], writes=[b_sg[jf % 2]])
                        S.op("dve", lambda e, ub=ub, jf=jf: e.tensor_tensor(out=aT[:, jf, :], in0=banks[ub][:, :], in1=sg[jf % 2], op=ALU.mult),
                             reads=[PB[ub], b_sg[jf % 2]], writes=[b_aT])
                    if c + 1 < 4:
                        front(c + 1)
                    for j in range(4):
                        for half in range(2):
                            bk = half
                            S.op("pe", lambda e, bk=bk, j=j, half=half: mm_acc(
                                e, banks[bk][:, :], [aT[:, jf, 128 * j:128 * j + 128] for jf in range(NJF)],
                                [wd[:, jf, half * 512:(half + 1) * 512] for jf in range(NJF)]),
                                reads=[b_aT, b_wd], writes=[PB[bk]])
                            S.op("dve", lambda e, bk=bk, j=j, half=half: e.tensor_tensor(
                                out=xh_[:, j, half * 512:(half + 1) * 512], in0=banks[bk][:, :], in1=xh_[:, j, half * 512:(half + 1) * 512],
                                op=ALU.add), reads=[PB[bk], bxh_], writes=[bxh_])
                    dst = out_d[s, 512 * c:512 * c + 512, :].rearrange("(j p) f -> p j f", p=128)
                    S.dma("sp", lambda e, f, dst=dst: f(e.dma_start(out=dst, in_=xh_)), "xh_st%d" % (c % 2), reads=[bxh_], store=True)
                    if c + 1 < 4:
                        front_tr(c + 1)
        except _Stop:
            pass

        S.emit()
    return nc, dbg_d


def _host_layout(inp):
    f = lambda a: np.ascontiguousarray(np.asarray(a, dtype=np.float32))
    vecs = np.zeros((128, NV), np.float32)
    col = lambda v, n: f(v).reshape(n, 128).T
    vecs[:, V_G1:V_G1 + 8] = col(inp["ln1_g"][0], 8)
    vecs[:, V_GQA:V_GQA + 3] = col(inp["q_a_norm_g"][0], 3)
    vecs[:, V_GKVA:V_GKVA + 2] = col(inp["kv_a_norm_g"][0], 2)
    vecs[0:96, V_QG] = f(inp["q_norm_g"][0])
    vecs[0:96, V_KG] = f(inp["k_norm_g"][0])
    cw = f(inp["conv_w"][0])
    for cc in range(4):
        for j in range(4):
            vecs[:, V_CW + cc * 4 + j] = cw[j, cc * 128:(cc + 1) * 128]
    vecs[:, V_CB:V_CB + 4] = col(inp["conv_b"][0], 4)
    for d in range(2):
        vecs[:, V_BA + d * 4:V_BA + d * 4 + 4] = col(inp["lru_ba"][0, d], 4)
        vecs[:, V_BI + d * 4:V_BI + d * 4 + 4] = col(inp["lru_bi"][0, d], 4)
        vecs[:, V_LAM + d * 4:V_LAM + d * 4 + 4] = col(inp["lru_lambda"][0, d], 4)
    vecs[:, V_GA:V_GA + 4] = col(inp["attn_out_g"][0], 4)
    vecs[:, V_GR:V_GR + 4] = col(inp["rnn_out_g"][0], 4)
    vecs[:, V_G2:V_G2 + 8] = col(inp["ln2_g"][0], 8)
    vecs[:, V_EPS] = EPS
    vecs[:, V_ONE] = 1.0
    cst = np.zeros((128, 288), np.float32)
    cst[:, 0:128] = np.eye(128, dtype=np.float32)
    cst[:, 128:256] = 1.0
    pm = np.zeros((32, 32), np.float32)
    for m in range(16):
        pm[m + 16, m] = -1.0
        pm[m, m + 16] = 1.0
    cst[64:96, 256:288] = pm
    half = 16
    freqs = (1.0 / (np.float32(10000.0) ** (np.arange(half, dtype=np.float32) / np.float32(half)))).astype(np.float32)
    ang = (np.arange(T, dtype=np.float32)[:, None] * freqs[None, :]).astype(np.float32)
    cos = np.cos(ang).astype(np.float32).T
    sin = np.sin(ang).astype(np.float32).T
    rope = np.zeros((32, 2, T), np.float32)
    rope[0:16, 0] = cos; rope[16:32, 0] = cos
    rope[0:16, 1] = sin; rope[16:32, 1] = sin
    w_ukv = f(inp["w_ukv"][0]).reshape(256, NH, 128)
    w_kn = np.ascontiguousarray(w_ukv[:, :, 0:64].reshape(256, 512))
    w_v = np.ascontiguousarray(w_ukv[:, :, 64:128].reshape(256, 512))
    lru_w = np.ascontiguousarray(np.stack([f(inp["lru_wa"][0]), f(inp["lru_wi"][0])], axis=0))
    wg = f(inp["w_gate"][0]).reshape(8, 128, NJF, 128)
    wu = f(inp["w_up"][0]).reshape(8, 128, NJF, 128)
    wgu = np.ascontiguousarray(np.stack([wg, wu], axis=0).transpose(3, 2, 0, 1, 4).reshape(NJF, 128, 2048))
    shared = {
        "meta": f(inp["meta_tokens"]), "vecs": vecs, "cst": cst, "rope": rope,
        "w_in": f(inp["w_in"][0]), "w_uq": f(inp["w_uq"][0]), "w_kn": w_kn, "w_v": w_v, "lru_w": lru_w,
        "w_out": f(inp["w_out"][0]), "wgu": wgu, "w_down": f(inp["w_down"][0]),
    }
    return shared


_CACHE = {}


def kernel(**inputs):
    x = np.asarray(inputs["x"], dtype=np.float32)
    shared = _host_layout(inputs)
    if "nc" not in _CACHE:
        _CACHE["nc"] = build()
    nc, dbg = _CACHE["nc"]
    in_maps = []
    for i in range(NCORES):
        m = dict(shared)
        m["x"] = np.ascontiguousarray(x[NSEQ * i:NSEQ * (i + 1)])
        in_maps.append(m)
    res = run_bass_kernel_spmd(nc, in_maps, core_ids=list(range(NCORES)))
    if KDEBUG:
        _CACHE["dbg"] = {k: np.asarray(res.results[0]["dbg_" + k]) for k in dbg}
    out = np.concatenate([np.asarray(res.results[i]["out"]) for i in range(NCORES)], axis=0)
    return out.astype(np.float32)
```
